# Optimizing a Trainium2 kernel written in Bass

```python
import jax, jax.numpy as jnp
from jax import lax
import numpy as np

D_MODEL = 1024
BATCH = 16
SEQ = 2048
DEPTH = 1

MEM_LEN = 256
HEAD_DIM = 64
FOX_HEADS = D_MODEL // 128
FOX_W = FOX_HEADS * HEAD_DIM
RWKV_HEADS = D_MODEL // 128
RWKV_W = RWKV_HEADS * HEAD_DIM
MEM_HEADS = 4
MEM_W = D_MODEL // 2
MEM_HEAD_DIM = MEM_W // MEM_HEADS
DECAY_LORA = 64
AAA_LORA = 64
GATE_LORA = 128
N_BRANCH = 3
D_FF = -(-8 * D_MODEL // (3 * 256)) * 256
Q_BLOCK = 128
NORM_EPS = 1e-6
GN_EPS = 64e-5

FOX_COLS = 3 * FOX_W + FOX_HEADS
RWKV_WIDTHS = (RWKV_W, RWKV_W, RWKV_W, DECAY_LORA, AAA_LORA, GATE_LORA)
RWKV_COLS = sum(RWKV_WIDTHS)
GATE_COLS = N_BRANCH * D_MODEL
IN_COLS = FOX_COLS + RWKV_COLS + MEM_W + GATE_COLS

kernel_name = "fox_rwkv7_memxattn_gated_hybrid"


def _split(x, widths):
    idx = [int(i) for i in np.cumsum(widths)[:-1]]
    return jnp.split(x, idx, axis=-1)


def rmsnorm(x, g):
    xf = x.astype(jnp.float32)
    y = xf * lax.rsqrt(jnp.mean(xf * xf, axis=-1, keepdims=True) + NORM_EPS)
    return (y * g.astype(jnp.float32)).astype(x.dtype)


def forgetting_attention(q, k, v, f_logit):
    B, S, H, Dh = q.shape
    c = jnp.cumsum(jax.nn.log_sigmoid(f_logit.astype(jnp.float32)), axis=1)
    cT = jnp.transpose(c, (0, 2, 1))
    scale = Dh ** -0.5
    tri = jnp.tril(jnp.ones((Q_BLOCK, Q_BLOCK), dtype=bool))
    outs = []
    for i in range(S // Q_BLOCK):
        lo, hi = i * Q_BLOCK, (i + 1) * Q_BLOCK
        logits = jnp.einsum('bqhd,bkhd->bhqk', q[:, lo:hi], k[:, :hi]).astype(jnp.float32) * scale
        bias = cT[:, :, lo:hi, None] - cT[:, :, None, :hi]
        mask = jnp.concatenate([jnp.ones((Q_BLOCK, lo), dtype=bool), tri], axis=1)
        logits = jnp.where(mask, logits + bias, -jnp.inf)
        p = jax.nn.softmax(logits, axis=-1).astype(v.dtype)
        outs.append(jnp.einsum('bhqk,bkhd->bqhd', p, v[:, :hi]))
    return jnp.concatenate(outs, axis=1).reshape(B, S, H * Dh)


def rwkv7_time_mix(p, mu, w0, w_up, a0, a_up, g_up, k_k, k_a, r_k, gn_g, gn_b):
    B, S, _ = p.shape
    H, N = RWKV_HEADS, HEAD_DIM
    p = p.astype(jnp.float32)
    p_prev = jnp.pad(p, ((0, 0), (1, 0), (0, 0)))[:, :-1]
    p = p + (p_prev - p) * mu
    r, k, v, wd, ad, gd = _split(p, RWKV_WIDTHS)
    w_log = -jnp.exp(jax.nn.log_sigmoid(w0 + jnp.tanh(wd) @ w_up) - 0.5)
    a = jax.nn.sigmoid(a0 + ad @ a_up)
    g = jax.nn.sigmoid(gd) @ g_up
    kk = (k * k_k).reshape(B, S, H, N)
    kk = kk * lax.rsqrt(jnp.maximum(jnp.sum(kk * kk, axis=-1, keepdims=True), 1e-24))
    k = k * (1.0 + (a - 1.0) * k_a)
    rh = r.reshape(B, S, H, N)
    kh = k.reshape(B, S, H, N)
    vh = v.reshape(B, S, H, N)
    ah = a.reshape(B, S, H, N)
    wh = jnp.exp(w_log).reshape(B, S, H, N)
    a_vec = -kk
    b_vec = kk * ah

    def step(state, inp):
        r_t, w_t, k_t, v_t, a_t, b_t = inp
        sa = jnp.einsum('bhij,bhj->bhi', state, a_t)
        state = state * w_t[:, :, None, :] + sa[..., None] * b_t[:, :, None, :] + v_t[..., None] * k_t[:, :, None, :]
        y = jnp.einsum('bhij,bhj->bhi', state, r_t)
        return state, y

    xs = tuple(jnp.moveaxis(t, 1, 0) for t in (rh, wh, kh, vh, a_vec, b_vec))
    s0 = jnp.zeros((B, H, N, N), jnp.float32)
    _, ys = lax.scan(step, s0, xs)
    y = jnp.moveaxis(ys, 0, 1)
    mean = jnp.mean(y, axis=-1, keepdims=True)
    var = jnp.mean(jnp.square(y - mean), axis=-1, keepdims=True)
    y = ((y - mean) * lax.rsqrt(var + GN_EPS)).reshape(B, S, H * N) * gn_g + gn_b
    bonus = jnp.sum(rh * kh * r_k, axis=-1, keepdims=True) * vh
    return (y + bonus.reshape(B, S, H * N)) * g


def memory_cross_attention(q, mem_kv):
    B, S, _ = q.shape
    km, vm = _split(mem_kv, (MEM_W, MEM_W))
    qh = q.reshape(B, S, MEM_HEADS, MEM_HEAD_DIM)
    kh = km.reshape(B, -1, MEM_HEADS, MEM_HEAD_DIM)
    vh = vm.reshape(B, -1, MEM_HEADS, MEM_HEAD_DIM)
    logits = jnp.einsum('bqhd,bkhd->bhqk', qh, kh).astype(jnp.float32) * MEM_HEAD_DIM ** -0.5
    p = jax.nn.softmax(logits, axis=-1).astype(vh.dtype)
    return jnp.einsum('bhqk,bkhd->bqhd', p, vh).reshape(B, S, MEM_W)


def setup_inputs(seed: int = 0) -> dict:
    key = jax.random.key(seed)
    ks = jax.random.split(key, 32)
    f32 = jnp.float32
    L, D = DEPTH, D_MODEL

    def nrm(k, shape, fan_in):
        return jax.random.normal(k, shape, f32) * fan_in ** -0.5

    def gain(k, shape):
        return 1.0 + 0.1 * jax.random.normal(k, shape, f32)

    return {
        "x": jax.random.normal(ks[0], (BATCH, SEQ, D), f32),
        "mem": jax.random.normal(ks[1], (BATCH, MEM_LEN, D), f32),
        "pre1_g": gain(ks[2], (L, D)),
        "post1_g": gain(ks[3], (L, D)),
        "pre2_g": gain(ks[4], (L, D)),
        "post2_g": gain(ks[5], (L, D)),
        "mem_norm_g": gain(ks[6], (L, D)),
        "w_in": nrm(ks[7], (L, D, IN_COLS), D),
        "fox_f_bias": jax.random.uniform(ks[8], (L, FOX_HEADS), f32, 1.0, 4.0),
        "rwkv_mu": jax.random.uniform(ks[9], (L, RWKV_COLS), f32),
        "rwkv_w0": jax.random.normal(ks[10], (L, RWKV_W), f32),
        "rwkv_w_up": nrm(ks[11], (L, DECAY_LORA, RWKV_W), DECAY_LORA),
        "rwkv_a0": 0.5 * jax.random.normal(ks[12], (L, RWKV_W), f32),
        "rwkv_a_up": nrm(ks[13], (L, AAA_LORA, RWKV_W), AAA_LORA),
        "rwkv_g_up": nrm(ks[14], (L, GATE_LORA, RWKV_W), GATE_LORA),
        "rwkv_k_k": 0.85 + 0.05 * jax.random.normal(ks[15], (L, RWKV_W), f32),
        "rwkv_k_a": 1.0 + 0.05 * jax.random.normal(ks[16], (L, RWKV_W), f32),
        "rwkv_r_k": 0.1 * jax.random.normal(ks[17], (L, RWKV_HEADS, HEAD_DIM), f32),
        "rwkv_gn_g": gain(ks[18], (L, RWKV_W)),
        "rwkv_gn_b": 0.02 * jax.random.normal(ks[19], (L, RWKV_W), f32),
        "w_mem_kv": nrm(ks[20], (L, D, 2 * MEM_W), D),
        "w_fox_out": nrm(ks[21], (L, FOX_W, D), FOX_W),
        "w_rwkv_out": nrm(ks[22], (L, RWKV_W, D), RWKV_W),
        "w_mem_out": nrm(ks[23], (L, MEM_W, D), MEM_W),
        "w_o": nrm(ks[24], (L, D, D), D),
        "w_ffn_gate": nrm(ks[25], (L, D, D_FF), D),
        "w_ffn_up": nrm(ks[26], (L, D, D_FF), D),
        "w_ffn_down": nrm(ks[27], (L, D_FF, D), D_FF),
    }


def reference(x, mem, pre1_g, post1_g, pre2_g, post2_g, mem_norm_g, w_in, fox_f_bias,
              rwkv_mu, rwkv_w0, rwkv_w_up, rwkv_a0, rwkv_a_up, rwkv_g_up, rwkv_k_k, rwkv_k_a,
              rwkv_r_k, rwkv_gn_g, rwkv_gn_b, w_mem_kv, w_fox_out, w_rwkv_out, w_mem_out, w_o,
              w_ffn_gate, w_ffn_up, w_ffn_down):
    B, S, D = x.shape
    h = x
    for l in range(DEPTH):
        u = rmsnorm(h, pre1_g[l])
        proj = u @ w_in[l]
        p_fox, p_rwkv, p_memq, p_gate = _split(proj, (FOX_COLS, RWKV_COLS, MEM_W, GATE_COLS))

        fq, fk, fv, ff = _split(p_fox, (FOX_W, FOX_W, FOX_W, FOX_HEADS))
        fox_out = forgetting_attention(
            fq.reshape(B, S, FOX_HEADS, HEAD_DIM), fk.reshape(B, S, FOX_HEADS, HEAD_DIM),
            fv.reshape(B, S, FOX_HEADS, HEAD_DIM), ff + fox_f_bias[l])

        rwkv_out = rwkv7_time_mix(p_rwkv, rwkv_mu[l], rwkv_w0[l], rwkv_w_up[l], rwkv_a0[l],
                                  rwkv_a_up[l], rwkv_g_up[l], rwkv_k_k[l], rwkv_k_a[l],
                                  rwkv_r_k[l], rwkv_gn_g[l], rwkv_gn_b[l])

        mem_kv = rmsnorm(mem, mem_norm_g[l]) @ w_mem_kv[l]
        mem_out = memory_cross_attention(p_memq, mem_kv)

        g_fox, g_rwkv, g_mem = _split(jax.nn.sigmoid(p_gate.astype(jnp.float32)), (D, D, D))
        merged = (g_fox * (fox_out @ w_fox_out[l])
                  + g_rwkv * (rwkv_out @ w_rwkv_out[l])
                  + g_mem * (mem_out @ w_mem_out[l]))
        y = merged @ w_o[l]
        h = h + rmsnorm(y, post1_g[l])

        u2 = rmsnorm(h, pre2_g[l])
        ffn = (jax.nn.silu(u2 @ w_ffn_gate[l]) * (u2 @ w_ffn_up[l])) @ w_ffn_down[l]
        h = h + rmsnorm(ffn, post2_g[l])
    return h.astype(x.dtype)
```

```python
import contextlib
import numpy as np
import concourse.bass as bass
import concourse.mybir as mybir
from concourse.bass_utils import run_bass_kernel_spmd

F32 = mybir.dt.float32
BF16 = mybir.dt.bfloat16
ALU = mybir.AluOpType
AF = mybir.ActivationFunctionType
AX = mybir.AxisListType

PE, ACT, DVE, POOL, SP = "tensor", "scalar", "vector", "gpsimd", "sync"
ENGS = (PE, ACT, DVE, POOL, SP)
EPOCH = 24000

D = 1024
MEML = 256
DFF = 2816
NFC = DFF // 128
FOX_COLS = 1544
RW_COLS = 1792
RW0 = FOX_COLS
MQ0 = RW0 + RW_COLS
GT0 = MQ0 + 512
IN_COLS = GT0 + 3072
C0 = float(np.exp(-0.5))
NORM_EPS = 1e-6
GN_EPS = 64e-5


class Unit:
    __slots__ = ("name", "last_w", "readers", "psum", "sem", "cnt", "nobar")

    def __init__(self, name, psum=False, nobar=False):
        self.name = name
        self.last_w = None
        self.readers = []
        self.psum = psum
        self.sem = None
        self.cnt = 0
        self.nobar = nobar


class Op:
    __slots__ = ("eng", "fn", "deps", "mark", "midx", "dma", "sem", "val", "waits")

    def __init__(self, eng, fn, dma):
        self.eng = eng
        self.fn = fn
        self.deps = []
        self.mark = False
        self.midx = -1
        self.dma = dma
        self.sem = None
        self.val = 0
        self.waits = []


class _Rec:
    __slots__ = ("call",)

    def __init__(self):
        self.call = None

    def __getattr__(self, name):
        def f(*a, **k):
            assert self.call is None
            self.call = (name, a, k)
            return self
        return f


class Prog:
    def __init__(self, nc, stack):
        self.nc = nc
        self.stack = stack
        self.ops = {e: [] for e in ENGS}
        self.units = []
        self.dma_units = []
        self.out_dma_ops = []
        self.nops = 0
        self.cur_bar = None
        self.defer = None
        self.atomic_depth = 0

    def unit(self, name, psum=False, nobar=False):
        u = Unit(name, psum, nobar)
        u.last_w = self.cur_bar
        self.units.append(u)
        return u

    def sb(self, name, shape, dtype):
        return self.stack.enter_context(self.nc.sbuf_tensor(name, list(shape), dtype))

    def ps(self, name, shape, dtype):
        return self.stack.enter_context(self.nc.psum_tensor(name, list(shape), dtype))

    def add(self, eng, fn, reads=(), writes=(), dma=None, is_out=False):
        rec = _Rec()
        fn(rec)
        assert rec.call is not None
        if self.defer is not None:
            entry = (eng, rec.call, tuple(reads), tuple(writes), dma, is_out)
            if self.atomic_depth and self.defer and self.defer[-1][0]:
                self.defer[-1][1].append(entry)
            else:
                self.defer.append([bool(self.atomic_depth), [entry]])
            return None
        return self._reg(eng, rec.call, reads, writes, dma, is_out)

    def atomic_begin(self):
        self.atomic_depth += 1
        if self.defer is not None:
            self.defer.append([True, []])

    def atomic_end(self):
        self.atomic_depth -= 1
        if self.defer is not None and self.defer and self.defer[-1][0]:
            self.defer[-1][0] = False

    def run_deferred(self, queue, n=1):
        for _ in range(n):
            if not queue:
                return
            _, entries = queue.pop(0)
            for (eng, call, reads, writes, dma, is_out) in entries:
                self._reg(eng, call, reads, writes, dma, is_out)

    def _reg(self, eng, call, reads=(), writes=(), dma=None, is_out=False):
        op = Op(eng, call, dma)
        self.nops += 1
        deps = op.deps
        for u in reads:
            if u.psum:
                if u.last_w is not None:
                    deps.append((u.last_w, "RAW"))
                u.last_w = op
                continue
            if u.last_w is not None:
                deps.append((u.last_w, "RAW"))
            u.readers.append(op)
        for u in writes:
            if u.last_w is not None and u.last_w is not op:
                deps.append((u.last_w, "RAW" if u.psum else "WAW"))
            for r in u.readers:
                if r is not op:
                    deps.append((r, "WAR"))
            u.last_w = op
            u.readers = []
        if dma is not None:
            if dma.sem is None:
                dma.sem = True
                self.dma_units.append(dma)
            dma.cnt += 16
            op.sem = dma
            op.val = dma.cnt
            if is_out:
                self.out_dma_ops.append(op)
        self.ops[eng].append(op)
        return op

    def barrier(self, scratch_ap):
        us = [u for u in self.units if not u.nobar]
        self.cur_bar = self.add(DVE, lambda e: e.memset(scratch_ap, 0.0), writes=us)

    def emit(self):
        nc = self.nc
        for e in ENGS:
            for op in self.ops[e]:
                need = []
                seen = set()
                for (p, kind) in op.deps:
                    if id(p) in seen:
                        continue
                    if p.dma is not None or op.dma is not None:
                        pass
                    elif p.eng == e:
                        if e == PE or kind != "RAW":
                            continue
                    seen.add(id(p))
                    need.append(p)
                    if p.dma is None:
                        p.mark = True
                op.waits = need
                op.deps = None
        nep = {}
        for e in ENGS:
            k = 0
            for op in self.ops[e]:
                if op.mark:
                    op.midx = k
                    k += 1
            nep[e] = (k + EPOCH - 1) // EPOCH
        esem = {e: [self.stack.enter_context(nc.semaphore(f"es_{e}_{i}")) for i in range(nep[e])] for e in ENGS}
        for u in self.dma_units:
            u.sem = self.stack.enter_context(nc.semaphore(f"ds_{u.name}"))
        block = self.stack.enter_context(nc.Block())
        prog = self

        def body(e):
            def run(eng):
                waited = {}
                for op in prog.ops[e]:
                    for p in op.waits:
                        if p.dma is not None:
                            key = ("d", id(p.sem))
                            sem = p.sem.sem
                            val = p.val
                        else:
                            ep = p.midx // EPOCH
                            key = (p.eng, ep)
                            sem = esem[p.eng][ep]
                            val = p.midx % EPOCH + 1
                        if waited.get(key, 0) >= val:
                            continue
                        waited[key] = val
                        eng.wait_ge(sem, val)
                    nm, a_, k_ = op.fn
                    ins = getattr(eng, nm)(*a_, **k_)
                    if op.dma is not None:
                        ins.then_inc(op.sem.sem, 16)
                    elif op.mark:
                        ins.then_inc(esem[e][op.midx // EPOCH], 1)
                if e == SP:
                    done = set()
                    for op in prog.out_dma_ops:
                        if id(op.sem) in done:
                            continue
                        done.add(id(op.sem))
                        eng.wait_ge(op.sem.sem, op.sem.cnt)
            return run

        block.tensor(body(PE))
        block.scalar(body(ACT))
        block.vector(body(DVE))
        block.gpsimd(body(POOL))
        block.sync(body(SP))


CST_IDENT, CST_SU, CST_SL, CST_UI, CST_CAUS, CST_ONES, CST_SEL, CST_BONES, CST_STK = 0, 128, 256, 384, 512, 640, 768, 896, 1024
NCST = 1088
PV_PRE1, PV_PRE2, PV_MEMG, PV_MU, PV_W0, PV_A0, PV_KK, PV_KA, PV_RK, PV_FB = 0, 8, 16, 24, 38, 42, 46, 50, 54, 58
NPV = 64


def make_consts():
    c = np.zeros((128, NCST), np.float32)
    i = np.arange(128)
    blk = (i[:, None] // 64) == (i[None, :] // 64)
    s = i[:, None] % 64
    t = i[None, :] % 64
    c[:, CST_IDENT:CST_IDENT + 128] = np.eye(128)
    c[:, CST_SU:CST_SU + 128] = blk & (s < t)
    c[:, CST_SL:CST_SL + 128] = blk & (s > t)
    c[:, CST_UI:CST_UI + 128] = blk & (s <= t)
    c[:, CST_CAUS:CST_CAUS + 128] = i[:, None] <= i[None, :]
    c[:, CST_ONES:CST_ONES + 128] = 1.0
    c[127, CST_SEL:CST_SEL + 128] = 1.0
    c[:, CST_BONES:CST_BONES + 128] = blk
    c[:, CST_STK:CST_STK + 64] = (i[:, None] % 64) == np.arange(64)[None, :]
    return c


def build(NB=2, S=2048, dbg=False):
    nc = bass.Bass("TRN2", target_bir_lowering=False)
    NT = S // 128
    NQ = S // 512
    TQ = 512

    def din(name, shape):
        return nc.dram_tensor(name, list(shape), F32, kind="ExternalInput").ap()

    x_d = din("x", [NB, S, D])
    mem_d = din("mem", [NB, MEML, D])
    w_in_d = din("w_in", [D, IN_COLS])
    w_kv_d = din("w_mem_kv", [D, 1024])
    w_fo_d = din("w_fox_out", [512, D])
    w_ro_d = din("w_rwkv_out", [512, D])
    w_mo_d = din("w_mem_out", [512, D])
    w_o_d = din("w_o", [D, D])
    w_fg_d = din("w_ffn_gate", [D, DFF])
    w_fu_d = din("w_ffn_up", [D, DFF])
    w_fd_d = din("w_ffn_down", [DFF, D])
    wup_d = din("rwkv_w_up", [64, 512])
    aup_d = din("rwkv_a_up", [64, 512])
    gup_d = din("rwkv_g_up", [128, 512])
    pv_d = din("pv", [128, NPV])
    cst_d = din("cst", [128, NCST])
    gnp_d = din("gnp", [128, 4 * 2 * 64])
    rowp_d = din("rowp", [1, 2 * D])
    out_d = nc.dram_tensor("out", [NB, S, D], F32, kind="ExternalOutput").ap()
    dbg_d = {}
    if dbg:
        for nm, shp in (("d_uT", [128, 8 * S]), ("d_brT", [128, 12 * S]), ("d_h", [S, D])):
            dbg_d[nm] = nc.dram_tensor(nm, shp, F32, kind="ExternalOutput").ap()

    with contextlib.ExitStack() as st:
        P = Prog(nc, st)
        add = P.add

        cst = P.sb("cst_sb", [128, NCST], F32)
        cstb = P.sb("cstb", [128, NCST], BF16)
        pv = P.sb("pv_sb", [128, NPV], F32)
        pvx = P.sb("pvx", [128, 16], F32)
        u_cst = P.unit("cst")
        u_pv = P.unit("pv")
        UT_B, BR_B, W_B, SCR_B = 32768, 49152, 49152, 57344
        AR_B = UT_B + BR_B + W_B + SCR_B
        ar = P.sb("arena", [128, AR_B // 4], F32)
        OFF_UT, OFF_BR, OFF_W, OFF_SCR = 0, UT_B, UT_B + BR_B, UT_B + BR_B + W_B

        def view(off, nbytes, dtype, pat=None, **kw):
            assert off % 4 == 0 and nbytes % 4 == 0
            a = ar[:, off // 4:(off + nbytes) // 4]
            if dtype == BF16:
                a = a.bitcast(BF16)
            if pat:
                a = a.rearrange(pat, **kw)
            return a

        class Carver:
            def __init__(self, off, size):
                self.off = off
                self.end = off + size

            def take(self, nbytes, dtype, pat=None, **kw):
                nbytes = (nbytes + 3) // 4 * 4
                assert self.off + nbytes <= self.end, ("carver overflow", self.off + nbytes - self.end)
                v = view(self.off, nbytes, dtype, pat, **kw)
                self.off += nbytes
                return v

        SMAX = 2048
        uT = view(OFF_UT, 8 * SMAX * 2, BF16, "p (k t) -> p k t", k=8)[:, :, 0:S]
        brT = view(OFF_BR, 12 * SMAX * 2, BF16, "p (k t) -> p k t", k=12)[:, :, 0:S]
        BR_FOX, BR_MEM, BR_RWK = 0, 4, 8
        u_uT = P.unit("uT")
        u_br = [P.unit(f"brT{i}") for i in range(3)]

        banks = [P.ps(f"pb{i}", [128, 512], F32) for i in range(8)]
        ub = [P.unit(f"pb{i}", psum=True) for i in range(8)]
        bar_scr = P.sb("barscr", [128, 2], F32)

        class Rot:
            def __init__(self, ids):
                self.ids = list(ids)
                self.i = 0

            def nxt(self):
                b = self.ids[self.i % len(self.ids)]
                self.i += 1
                return b

        identb = cstb[:, CST_IDENT:CST_IDENT + 128]
        identf = cst[:, CST_IDENT:CST_IDENT + 128]

        add(SP, lambda e: e.dma_start(out=cst[:], in_=cst_d[:, :]), writes=[u_cst], dma=u_cst)
        add(SP, lambda e: e.dma_start(out=pv[:], in_=pv_d[:, :]), writes=[u_pv], dma=u_pv)
        add(DVE, lambda e: e.tensor_copy(out=cstb[:], in_=cst[:]), reads=[u_cst], writes=[u_cst])
        add(DVE, lambda e: e.tensor_scalar(out=pvx[:, 0:4], in0=pv[:, PV_KA:PV_KA + 4], scalar1=-1.0, scalar2=1.0,
                                           op0=ALU.mult, op1=ALU.add), reads=[u_pv], writes=[u_pv])
        add(DVE, lambda e: e.tensor_scalar(out=pvx[:, 4:5], in0=pv[:, PV_FB:PV_FB + 1], scalar1=-1.0, scalar2=None,
                                           op0=ALU.mult), reads=[u_pv], writes=[u_pv])
        add(DVE, lambda e: e.tensor_scalar(out=pvx[:, 8:12], in0=pv[:, PV_W0:PV_W0 + 4], scalar1=0.5, scalar2=None,
                                           op0=ALU.mult), reads=[u_pv], writes=[u_pv])
        add(DVE, lambda e: e.tensor_scalar(out=pvx[:, 12:16], in0=pv[:, PV_A0:PV_A0 + 4], scalar1=0.5, scalar2=None,
                                           op0=ALU.mult), reads=[u_pv], writes=[u_pv])

        def wload(dst, src, unit):
            add(POOL, lambda e: e.dma_start(out=dst, in_=src), writes=[unit], dma=unit)

        def mk_scr(name, ncols):
            return nc.dram_tensor(name, [128, ncols], BF16).ap(), P.unit("S" + name, nobar=True)

        S_rw, uS_rw = mk_scr("s_rw", 8 * RW_COLS)
        S_fx, uS_fx = mk_scr("s_fx", 8 * FOX_COLS)
        S_mq, uS_mq = mk_scr("s_mq", 8 * 512)
        S_kv, uS_kv = mk_scr("s_kv", 8 * 1024)
        S_bo, uS_bo = mk_scr("s_bo", 12 * 1024)
        S_gt, uS_gt = mk_scr("s_gt", 8 * 8 * 3 * 128)
        S_o, uS_o = mk_scr("s_o", 8 * 1024)
        S_dn, uS_dn = mk_scr("s_dn", NFC * 1024)
        S_ff, uS_ff = mk_scr("s_ff", NFC * 8 * 2 * 128)
        stg = [P.sb(f"stg{i}", [128, 4096], BF16) for i in range(2)]
        u_stg = [P.unit(f"stg{i}", nobar=True) for i in range(2)]
        stg_n = [0]

        def stage(scr, u_scr, col0, ncols, parts):
            sidx = stg_n[0] % 2
            stg_n[0] += 1
            for (dv, src) in parts:
                add(POOL, lambda e: e.dma_start(out=dv(stg[sidx]), in_=src), writes=[u_stg[sidx]], dma=u_stg[sidx])
            add(SP, lambda e: e.dma_start(out=scr[:, col0:col0 + ncols], in_=stg[sidx][:, 0:ncols]), reads=[u_stg[sidx]], writes=[u_scr], dma=u_scr)

        def kp(ap_):
            return ap_.rearrange("(k p) n -> p k n", p=128)

        def stage_first():
            for k0 in range(0, 8, 2):
                stage(S_rw, uS_rw, k0 * RW_COLS, 2 * RW_COLS,
                      [(lambda t: t[:, 0:2 * RW_COLS].rearrange("p (k n) -> p k n", k=2), kp(w_in_d[k0 * 128:(k0 + 2) * 128, RW0:RW0 + RW_COLS]))])

        def stage_rest():
            for k0 in range(0, 8, 2):
                stage(S_fx, uS_fx, k0 * FOX_COLS, 2 * FOX_COLS,
                      [(lambda t: t[:, 0:2 * FOX_COLS].rearrange("p (k n) -> p k n", k=2), kp(w_in_d[k0 * 128:(k0 + 2) * 128, 0:FOX_COLS]))])
            stage(S_mq, uS_mq, 0, 4096, [(lambda t: t[:, 0:4096].rearrange("p (k n) -> p k n", k=8), kp(w_in_d[:, MQ0:MQ0 + 512]))])
            for k0 in range(0, 8, 4):
                stage(S_kv, uS_kv, k0 * 1024, 4096, [(lambda t: t[:, 0:4096].rearrange("p (k n) -> p k n", k=4), kp(w_kv_d[k0 * 128:(k0 + 4) * 128, :]))])
            for r, wd in enumerate((w_fo_d, w_ro_d, w_mo_d)):
                stage(S_bo, uS_bo, r * 4096, 4096, [(lambda t: t[:, 0:4096].rearrange("p (k n) -> p k n", k=4), kp(wd[:, :]))])
            for oc in range(8):
                parts = []
                for r in range(3):
                    gc0 = GT0 + r * 1024 + oc * 128
                    parts.append((lambda t, r=r: t[:, 0:3072].rearrange("p (k r n) -> p k r n", k=8, r=3)[:, :, r, :], kp(w_in_d[:, gc0:gc0 + 128])))
                stage(S_gt, uS_gt, oc * 3072, 3072, parts)
            for k0 in range(0, 8, 4):
                stage(S_o, uS_o, k0 * 1024, 4096, [(lambda t: t[:, 0:4096].rearrange("p (k n) -> p k n", k=4), kp(w_o_d[k0 * 128:(k0 + 4) * 128, :]))])
            for f0 in range(0, NFC, 4):
                nf = min(4, NFC - f0)
                stage(S_dn, uS_dn, f0 * 1024, nf * 1024,
                      [(lambda t, nf=nf: t[:, 0:nf * 1024].rearrange("p (k n) -> p k n", k=nf), kp(w_fd_d[f0 * 128:(f0 + nf) * 128, :]))])
            for f0 in range(0, NFC, 2):
                parts = []
                for ff in range(2):
                    for r, wd in enumerate((w_fg_d, w_fu_d)):
                        parts.append((lambda t, ff=ff, r=r: t[:, 0:4096].rearrange("p (f k r n) -> p f k r n", f=2, k=8, r=2)[:, ff, :, r, :],
                                      kp(wd[:, (f0 + ff) * 128:(f0 + ff + 1) * 128])))
                stage(S_ff, uS_ff, f0 * 2048, 4096, parts)

        def sload(dst_flat, scr, col0, ncols, u_dst, u_scr):
            add(SP, lambda e: e.dma_start(out=dst_flat, in_=scr[:, col0:col0 + ncols]), reads=[u_scr], writes=[u_dst], dma=u_dst)

        stage_first()

        def rms_T(cv, src_rows, ntiles, gcol, dstT, u_dst, bank_ids, tag):
            xin = [cv.take(4096, F32) for _ in range(2)]
            xn = [cv.take(2048, BF16) for _ in range(2)]
            junk = cv.take(2048, BF16)
            stt = [cv.take(16, F32) for _ in range(2)]
            u_x = [P.unit(f"{tag}x{i}") for i in range(2)]
            u_xn = [P.unit(f"{tag}xn{i}") for i in range(2)]
            u_s = [P.unit(f"{tag}s{i}") for i in range(2)]
            u_j = P.unit(f"{tag}j")
            for tt in range(ntiles):
                s = tt % 2
                bk = bank_ids[tt % len(bank_ids)]
                add(SP, lambda e, s=s, tt=tt: e.dma_start(out=xin[s], in_=src_rows(tt)), writes=[u_x[s]], dma=u_x[s])
                add(ACT, lambda e, s=s: e.activation(out=junk, in_=xin[s], func=AF.Square, accum_out=stt[s][:, 0:1]),
                    reads=[u_x[s]], writes=[u_j, u_s[s]])
                add(ACT, lambda e, s=s: e.activation(out=stt[s][:, 1:2], in_=stt[s][:, 0:1], func=AF.Sqrt,
                                                     scale=1.0 / D, bias=NORM_EPS), reads=[u_s[s]], writes=[u_s[s]])
                add(DVE, lambda e, s=s: e.reciprocal(out=stt[s][:, 2:3], in_=stt[s][:, 1:2]), reads=[u_s[s]], writes=[u_s[s]])
                add(DVE, lambda e, s=s: e.tensor_scalar(out=xn[s], in0=xin[s], scalar1=stt[s][:, 2:3], scalar2=None,
                                                        op0=ALU.mult), reads=[u_x[s], u_s[s]], writes=[u_xn[s]])
                pbf = banks[bk].bitcast(BF16)
                for c in range(8):
                    add(PE, lambda e, c=c, s=s, pbf=pbf: e.transpose(pbf[:, c * 128:(c + 1) * 128], xn[s][:, c * 128:(c + 1) * 128], identb),
                        reads=[u_xn[s], u_cst], writes=[ub[bk]])
                add(DVE, lambda e, tt=tt, pbf=pbf: e.tensor_tensor(
                    out=dstT[:, :, tt * 128:(tt + 1) * 128],
                    in0=pbf[:, 0:1024].rearrange("p (k t) -> p k t", k=8),
                    in1=pv[:, gcol:gcol + 8].unsqueeze(2).broadcast_to([128, 8, 128]), op=ALU.mult),
                    reads=[ub[bk], u_pv], writes=[u_dst])

        def proj_fm(w_tile, u_w, col0, rhs_fn, u_rhs, ntok, bk, nk=8, m=128):
            for k in range(nk):
                add(PE, lambda e, k=k: e.matmul(banks[bk][0:m, 0:ntok], lhsT=w_tile[:, k, col0:col0 + m], rhs=rhs_fn(k),
                                                start=(k == 0), stop=(k == nk - 1)),
                    reads=[u_w] + list(u_rhs), writes=[ub[bk]])

        for b in range(NB):
            P.barrier(bar_scr[:, 0:1])
            cv = Carver(OFF_SCR, SCR_B)
            rms_T(cv, lambda tt: x_d[b, tt * 128:(tt + 1) * 128, :], NT, PV_PRE1, uT, u_uT, [0, 1], f"A{b}")
            if dbg and b == 0:
                P.barrier(bar_scr[:, 0:1])
                cvd = Carver(OFF_SCR, SCR_B)
                dtmp = cvd.take(S * 4, F32)
                u_d = P.unit("dbgA")
                for kk_ in range(8):
                    add(DVE, lambda e: e.tensor_copy(out=dtmp, in_=uT[:, kk_, :]), reads=[u_uT], writes=[u_d])
                    add(SP, lambda e: e.dma_start(out=dbg_d["d_uT"][:, kk_ * S:(kk_ + 1) * S], in_=dtmp), reads=[u_d], dma=u_d, is_out=True)

            P.barrier(bar_scr[:, 0:1])
            cw = Carver(OFF_W, W_B)
            w_rw = cw.take(8 * RW_COLS * 2, BF16, "p (k n) -> p k n", k=8)
            lu = cw.take(3 * 512 * 2, BF16, "p (k n) -> p k n", k=3)
            u_wrw = P.unit(f"wrw{b}")
            u_lu = P.unit(f"lu{b}")
            rkv2 = [cw.take(TQ * 4, F32) for _ in range(3)]
            u_rkv2 = [P.unit(f"rkv2{b}{i}") for i in range(3)]
            bonp = [None, cw.take(TQ * 4, F32)]
            gbp = [None, cw.take(TQ * 2, BF16)]
            gcxp = [None, None]
            u_bonp = [None, P.unit(f"bon1{b}")]
            u_gbp = [None, P.unit(f"gb1{b}")]
            u_gcp = [None, P.unit(f"gc1{b}")]
            H2d = cw.take(8 * 128 * 2, BF16, "p (c n) -> p c n", c=8)
            YMd = cw.take(8 * 128 * 2, BF16, "p (c n) -> p c n", c=8)
            uH2d = [P.unit(f"h2d{b}{g}") for g in range(2)]
            uYMd = [P.unit(f"ymd{b}{g}") for g in range(2)]
            YAd = cw.take(TQ * 4, F32)
            u_YAd = P.unit(f"yad{b}")
            stage2_q = []
            bgq = []
            pend_gn = [None]

            def tick(n=1):
                P.run_deferred(bgq, n)

            def pump(n=1):
                for _ in range(n):
                    if stage2_q:
                        stage2_q.pop(0)()
            sload(w_rw.rearrange("p k n -> p (k n)"), S_rw, 0, 8 * RW_COLS, u_wrw, uS_rw)
            wload(lu[0:64, 0, :], wup_d[:, :], u_lu)
            wload(lu[64:128, 1, :], aup_d[:, :], u_lu)
            wload(lu[:, 2, :], gup_d[:, :], u_lu)

            cv = Carver(OFF_SCR, SCR_B)
            cv2 = Carver(OFF_BR, 8 * SMAX * 2)
            l12 = cv2.take(S * 2, BF16)
            l13 = cv2.take(S * 2, BF16)
            u_l = P.unit(f"l{b}")
            gnt = cv.take(4 * 2 * 64 * 4, F32, "p (a c i) -> p a c i", a=4, c=2)
            u_gn = P.unit(f"gn{b}")
            add(SP, lambda e: e.dma_start(out=gnt.rearrange("p a c i -> p (a c i)"), in_=gnp_d[:, :]), writes=[u_gn], dma=u_gn)
            if b == 0:
                stage_rest()
            rmask = cv.take(TQ * 4, F32)
            u_rm = P.unit(f"rm{b}")
            add(DVE, lambda e: e.memset(rmask, 1.0), writes=[u_rm])
            add(DVE, lambda e: e.memset(rmask.rearrange("p (c t) -> p c t", t=64)[:, :, 0:1], 0.0), writes=[u_rm])
            praw = [cv.take((TQ + 2) * 4, F32) for _ in range(3)]
            u_praw = [P.unit(f"praw{b}{i}") for i in range(3)]
            NSC = 12
            sc = [cv.take(TQ * 4, F32) for _ in range(NSC)]
            u_sc = [P.unit(f"sc{b}{i}") for i in range(NSC)]
            scb = [cv.take(TQ * 2, BF16) for _ in range(3)]
            u_scb = [P.unit(f"scb{b}{i}") for i in range(3)]
            NBD = 7
            bd = [cv2.take(8 * 128 * 2, BF16, "p (c n) -> p c n", c=8) for _ in range(NBD)]
            u_bd = [P.unit(f"bd{b}{i}") for i in range(NBD)]
            for i in range(NBD):
                add(DVE, lambda e, i=i: e.memset(bd[i].rearrange("p c n -> p (c n)"), 0.0), writes=[u_bd[i]])
            ybd = cv2.take(8 * 128 * 2, BF16, "p (c n) -> p c n", c=8)
            u_ybd = P.unit(f"ybd{b}")
            add(DVE, lambda e: e.memset(ybd.rearrange("p c n -> p (c n)"), 0.0), writes=[u_ybd])
            gcx = cv.take(8 * 4, F32)
            u_gc = P.unit(f"gc{b}")
            TMd = cv.take(TQ * 4, F32)
            u_TMd = P.unit(f"tmd{b}")
            gcxp[1] = cv.take(8 * 4, F32)
            bonp[0], gbp[0], gcxp[0] = sc[11], scb[0], gcx
            u_bonp[0], u_gbp[0], u_gcp[0] = u_sc[11], u_scb[0], u_gc
            def carr(cvx):
                return cvx.take(8 * 128 * 2, BF16, "p (c n) -> p c n", c=8)
            Wt, Lt, Xt, LAKt, PRBt, PRKt, ATMt, BHTt = [carr(cv) for _ in range(8)]
            KHTt, YVt = carr(cv2), carr(cv2)
            VSt = cv2.take(8 * 64 * 2, BF16, "p (c i) -> p c i", c=8)
            uW, uL, uX, uLAK, uPRB, uPRK, uATM, uBHT, uKHT, uYV = [[P.unit(f"ca{b}{n_}{g}") for g in range(2)] for n_ in range(10)]
            u_VS = P.unit(f"VS{b}")
            Mf = cv.take(64 * 4, F32)
            Mb = [cv.take(64 * 2, BF16) for _ in range(2)]
            u_M = P.unit(f"M{b}")
            u_Mb = [P.unit(f"Mb{b}{i}") for i in range(2)]
            gst = cv.take(8 * 8 * 4, F32, "p (a c) -> p a c", a=8)
            u_gst = P.unit(f"gst{b}")

            rp = Rot([0, 1])
            rs = Rot([3, 4, 5, 6, 7])
            rsY = Rot([2])
            rsM = Rot([4, 5, 6, 7])

            def proj_lerp(cc, tq, pslot, dst, u_dst):
                bk = rp.nxt()
                proj_fm(w_rw, u_wrw, cc * 128, lambda k: uT[:, k, tq * TQ:(tq + 1) * TQ], [u_uT], TQ, bk)
                pr = praw[pslot]
                up = u_praw[pslot]
                if tq == 0:
                    add(DVE, lambda e: e.memset(pr[:, 0:1], 0.0), writes=[up])
                else:
                    add(DVE, lambda e: e.tensor_copy(out=pr[:, 0:1], in_=pr[:, TQ:TQ + 1]), reads=[up], writes=[up])
                add(ACT, lambda e: e.copy(out=pr[:, 1:TQ + 1], in_=banks[bk][:, 0:TQ]), reads=[ub[bk]], writes=[up])
                add(DVE, lambda e: e.tensor_tensor(out=dst, in0=pr[:, 0:TQ], in1=pr[:, 1:TQ + 1], op=ALU.subtract),
                    reads=[up], writes=[u_dst])
                add(DVE, lambda e: e.scalar_tensor_tensor(out=dst, in0=dst, scalar=pv[:, PV_MU + cc:PV_MU + cc + 1],
                                                          in1=pr[:, 1:TQ + 1], op0=ALU.mult, op1=ALU.add),
                    reads=[up, u_dst, u_pv], writes=[u_dst])

            for tq in range(NQ):
                tsl = slice(tq * TQ, (tq + 1) * TQ)
                proj_lerp(12, tq, 0, sc[0], u_sc[0])
                add(ACT, lambda e, tsl=tsl: e.activation(out=l12[0:64, tsl], in_=sc[0][0:64, :], func=AF.Tanh),
                    reads=[u_sc[0]], writes=[u_l])
                add(ACT, lambda e, tsl=tsl: e.copy(out=l12[64:128, tsl], in_=sc[0][64:128, :]), reads=[u_sc[0]], writes=[u_l])
                proj_lerp(13, tq, 1, sc[1], u_sc[1])
                add(ACT, lambda e, tsl=tsl: e.activation(out=l13[:, tsl], in_=sc[1], func=AF.Sigmoid),
                    reads=[u_sc[1]], writes=[u_l])

            msu = cst[:, CST_SU:CST_SU + 128]
            msl = cst[:, CST_SL:CST_SL + 128]
            mui = cst[:, CST_UI:CST_UI + 128]
            bones = cstb[:, CST_BONES:CST_BONES + 128]
            stk = cstb[:, CST_STK:CST_STK + 64]

            for hp in range(4):
                for tq in range(NQ):
                    tsl = slice(tq * TQ, (tq + 1) * TQ)
                    tidx = hp * NQ + tq
                    _, _, _, SG, A_, KK, T1, BV, CS, CP, DE, BON = sc
                    _, _, _, uSG, uA, uKK, uT1, uBV, uCS, uCP, uDE, uBON = u_sc

                    def rkv_bufs(ti):
                        if ti % 2 == 0:
                            return (sc[0], sc[1], sc[2]), (u_sc[0], u_sc[1], u_sc[2])
                        return tuple(rkv2), tuple(u_rkv2)

                    (R_, K_, V_), (uR, uK, uV) = rkv_bufs(tidx)

                    def emit_proj(ti, which):
                        hp_, tq_ = ti // NQ, ti % NQ
                        bufs, us = rkv_bufs(ti)
                        proj_lerp(which * 4 + hp_, tq_, which, bufs[which], us[which])

                    if tidx == 0:
                        for w_ in range(3):
                            emit_proj(0, w_)
                    hs = slice(hp * 128, (hp + 1) * 128)
                    _, SQb, RKb = scb
                    _, uSQ, uRK = u_scb
                    par = tidx % 2
                    BON, uBON, Gb, uG, gcx, u_gc = bonp[par], u_bonp[par], gbp[par], u_gbp[par], gcxp[par], u_gcp[par]
                    pump()
                    def prep(ti):
                        hp_, tq_ = ti // NQ, ti % NQ
                        tsl_ = slice(tq_ * TQ, (tq_ + 1) * TQ)
                        hs_ = slice(hp_ * 128, (hp_ + 1) * 128)
                        pr_ = ti % 2
                        (R_, K_, V_), (uR, uK, uV) = rkv_bufs(ti)
                        BON, uBON, Gb, uG, gcx, u_gc = bonp[pr_], u_bonp[pr_], gbp[pr_], u_gbp[pr_], gcxp[pr_], u_gcp[pr_]
                        add(DVE, lambda e: e.tensor_scalar(out=KK, in0=K_, scalar1=pv[:, PV_KK + hp_:PV_KK + hp_ + 1], scalar2=None, op0=ALU.mult),
                            reads=[uK, u_pv], writes=[uKK])
                        add(DVE, lambda e: e.tensor_tensor(out=SQb, in0=KK, in1=KK, op=ALU.mult), reads=[uKK], writes=[uSQ])
                        P.atomic_begin()
                        bs = rp.nxt()
                        add(PE, lambda e, bs=bs: e.matmul(banks[bs][:, 0:TQ], lhsT=bones, rhs=SQb, start=True, stop=True),
                            reads=[u_cst, uSQ], writes=[ub[bs]])
                        add(ACT, lambda e, bs=bs: e.activation(out=T1, in_=banks[bs][:, 0:TQ], func=AF.Sqrt), reads=[ub[bs]], writes=[uT1])
                        P.atomic_end()
                        for (li_, rows_, rhs_, is_g) in ((0, slice(0, 64), l12, False), (1, slice(64, 128), l12, False), (2, slice(0, 128), l13, True)):
                            P.atomic_begin()
                            bq = rp.nxt()
                            add(PE, lambda e: e.matmul(banks[bq][:, 0:TQ], lhsT=lu[rows_, li_, hs_], rhs=rhs_[rows_, tsl_], start=True, stop=True),
                                reads=[u_lu, u_l], writes=[ub[bq]])
                            if li_ == 0:
                                add(ACT, lambda e: e.activation(out=SG, in_=banks[bq][:, 0:TQ], func=AF.Tanh, scale=0.5, bias=pvx[:, 8 + hp_:9 + hp_]),
                                    reads=[ub[bq], u_pv], writes=[uSG])
                            elif li_ == 1:
                                add(ACT, lambda e: e.activation(out=A_, in_=banks[bq][:, 0:TQ], func=AF.Tanh, scale=0.5, bias=pvx[:, 12 + hp_:13 + hp_]),
                                    reads=[ub[bq], u_pv], writes=[uA])
                            else:
                                add(ACT, lambda e: e.copy(out=Gb, in_=banks[bq][:, 0:TQ]), reads=[ub[bq]], writes=[uG])
                            P.atomic_end()
                        add(DVE, lambda e: e.tensor_scalar(out=SG, in0=SG, scalar1=0.5, scalar2=0.5, op0=ALU.mult, op1=ALU.add), reads=[uSG], writes=[uSG])
                        add(DVE, lambda e: e.tensor_scalar(out=A_, in0=A_, scalar1=0.5, scalar2=0.5, op0=ALU.mult, op1=ALU.add), reads=[uA], writes=[uA])
                        add(DVE, lambda e: e.tensor_scalar(out=T1, in0=T1, scalar1=1e-12, scalar2=None, op0=ALU.max), reads=[uT1], writes=[uT1])
                        add(DVE, lambda e: e.reciprocal(out=T1, in_=T1), reads=[uT1], writes=[uT1])
                        add(DVE, lambda e: e.tensor_tensor(out=KK, in0=KK, in1=T1, op=ALU.mult), reads=[uKK, uT1], writes=[uKK])
                        add(DVE, lambda e: e.tensor_scalar(out=T1, in0=A_, scalar1=pv[:, PV_KA + hp_:PV_KA + hp_ + 1], scalar2=pvx[:, hp_:hp_ + 1],
                                                           op0=ALU.mult, op1=ALU.add), reads=[uA, u_pv, uT1], writes=[uT1])
                        add(DVE, lambda e: e.tensor_tensor(out=K_, in0=K_, in1=T1, op=ALU.mult), reads=[uK, uT1], writes=[uK])
                        add(DVE, lambda e: e.tensor_tensor(out=BV, in0=KK, in1=A_, op=ALU.mult), reads=[uKK, uA], writes=[uBV])
                        add(DVE, lambda e: e.scalar_tensor_tensor(out=RKb, in0=R_, scalar=pv[:, PV_RK + hp_:PV_RK + hp_ + 1], in1=K_,
                                                                  op0=ALU.mult, op1=ALU.mult), reads=[uR, uK, u_pv], writes=[uRK])
                        P.atomic_begin()
                        bb = rp.nxt()
                        add(PE, lambda e, bb=bb: e.matmul(banks[bb][:, 0:TQ], lhsT=bones, rhs=RKb, start=True, stop=True),
                            reads=[u_cst, uRK], writes=[ub[bb]])
                        add(DVE, lambda e, bb=bb: e.tensor_tensor(out=BON, in0=banks[bb][:, 0:TQ], in1=V_, op=ALU.mult),
                            reads=[ub[bb], uV], writes=[uBON])
                        P.atomic_end()
                        add(DVE, lambda e: e.tensor_tensor_scan(out=CS, data0=rmask, data1=SG, initial=0.0, op0=ALU.mult, op1=ALU.add),
                            reads=[u_rm, uSG], writes=[uCS])
                        add(DVE, lambda e: e.tensor_tensor(out=CP, in0=CS, in1=SG, op=ALU.subtract), reads=[uCS, uSG], writes=[uCP])
                        CS3 = CS.rearrange("p (c t) -> p c t", t=64)
                        add(DVE, lambda e: e.tensor_tensor(out=DE.rearrange("p (c t) -> p c t", t=64),
                                                           in0=CS3[:, :, 63:64].broadcast_to([128, 8, 64]), in1=CS3, op=ALU.subtract),
                            reads=[uCS], writes=[uDE])
                        add(ACT, lambda e: e.activation(out=gcx.unsqueeze(2), in_=CS3[:, :, 63:64], func=AF.Exp, scale=-C0),
                            reads=[uCS], writes=[u_gc])
                        add(ACT, lambda e: e.activation(out=SG, in_=CS, func=AF.Exp, scale=-C0), reads=[uCS, uCP], writes=[uSG])
                        add(ACT, lambda e: e.activation(out=T1, in_=CS, func=AF.Exp, scale=C0), reads=[uCS, uK], writes=[uT1])
                        add(ACT, lambda e: e.activation(out=CP, in_=CP, func=AF.Exp, scale=-C0), reads=[uCP], writes=[uCP])
                        add(ACT, lambda e: e.activation(out=DE, in_=DE, func=AF.Exp, scale=-C0), reads=[uDE], writes=[uDE])

                    if tidx == 0:
                        prep(0)
                    EP, EN, EPP, EE = SG, T1, CP, DE
                    uEP, uEN, uEPP, uEE = uSG, uT1, uCP, uDE
                    bdR, bdK, bdB, bdA, bdBH, bdKH, bdV = bd
                    uBR, uBK, uBB, uBA, uBBH, uBKH, uBVv = u_bd
                    for hh in range(2):
                        pump(2)
                        ps_ = slice(hh * 64, hh * 64 + 64)

                        def bdo(t):
                            return t[ps_, :, hh * 64:hh * 64 + 64]

                        def src(t):
                            return t[ps_, :].rearrange("p (c t) -> p c t", t=64)

                        add(DVE, lambda e, bdo=bdo, src=src: e.tensor_tensor(out=bdo(bdR), in0=src(R_), in1=src(EP), op=ALU.mult),
                            reads=[uR, uEP], writes=[uBR])
                        add(DVE, lambda e, bdo=bdo, src=src: e.tensor_tensor(out=bdo(bdK), in0=src(K_), in1=src(EN), op=ALU.mult),
                            reads=[uK, uEN], writes=[uBK])
                        add(DVE, lambda e, bdo=bdo, src=src: e.tensor_tensor(out=bdo(bdB), in0=src(BV), in1=src(EN), op=ALU.mult),
                            reads=[uBV, uEN], writes=[uBB])
                        add(DVE, lambda e, bdo=bdo, src=src: e.scalar_tensor_tensor(out=bdo(bdA), in0=src(KK), scalar=-1.0, in1=src(EPP),
                                                                                    op0=ALU.mult, op1=ALU.mult),
                            reads=[uKK, uEPP], writes=[uBA])
                        add(DVE, lambda e, bdo=bdo, src=src: e.tensor_tensor(out=bdo(bdBH), in0=src(BV), in1=src(EE), op=ALU.mult),
                            reads=[uBV, uEE], writes=[uBBH])
                        add(DVE, lambda e, bdo=bdo, src=src: e.tensor_tensor(out=bdo(bdKH), in0=src(K_), in1=src(EE), op=ALU.mult),
                            reads=[uK, uEE], writes=[uBKH])
                        add(ACT, lambda e, bdo=bdo, src=src: e.copy(out=bdo(bdV), in_=src(V_)), reads=[uV], writes=[uBVv])

                    def b4(bk):
                        return banks[bk][:, 0:512].rearrange("p (c n) -> p c n", c=4)

                    def g4(t, g):
                        return t[:, g * 4:(g + 1) * 4, :]

                    def mm4(bk, g, lt, ult, rt, urt, plus=None):
                        for cc in range(4):
                            c = g * 4 + cc
                            if plus is not None:
                                add(PE, lambda e: e.matmul(banks[bk][:, cc * 128:(cc + 1) * 128], lhsT=identb, rhs=plus[0][:, c, :], start=True, stop=False),
                                    reads=[u_cst, plus[1]], writes=[ub[bk]])
                            add(PE, lambda e: e.matmul(banks[bk][:, cc * 128:(cc + 1) * 128], lhsT=lt[:, c, :], rhs=rt[:, c, :],
                                                       start=(plus is None), stop=True),
                                reads=[ult, urt], writes=[ub[bk]])

                    evn = [0]

                    def evcopy(dst, bk, udst):
                        evn[0] += 1
                        if evn[0] % 2 == 0:
                            add(ACT, lambda e: e.copy(out=dst, in_=b4(bk)), reads=[ub[bk]], writes=[udst])
                        else:
                            add(DVE, lambda e: e.tensor_copy(out=dst, in_=b4(bk)), reads=[ub[bk]], writes=[udst])
                        tick()

                    def score(dst, udst, lt, ult, rt, urt, mask):
                        for g in range(2):
                            bk = rs.nxt()
                            mm4(bk, g, lt, ult, rt, urt)
                            add(DVE, lambda e: e.tensor_tensor(out=g4(dst, g), in0=b4(bk), in1=mask.unsqueeze(1).broadcast_to([128, 4, 128]), op=ALU.mult),
                                reads=[ub[bk], u_cst], writes=[udst[g]])

                    score(Wt, uW, bdB, uBB, bdA, uBA, msu)
                    pump()
                    score(Lt, uL, bdA, uBA, bdB, uBB, msl)
                    pump()
                    for g in range(2):
                        add(DVE, lambda e: e.tensor_tensor(out=g4(Xt, g), in0=g4(Wt, g), in1=identb.unsqueeze(1).broadcast_to([128, 4, 128]), op=ALU.add),
                            reads=[uW[g], u_cst], writes=[uX[g]])
                    score(LAKt, uLAK, bdA, uBA, bdK, uBK, msl)
                    pump()
                    score(PRBt, uPRB, bdB, uBB, bdR, uBR, mui)
                    pump()
                    score(PRKt, uPRK, bdK, uBK, bdR, uBR, mui)
                    pump(99)
                    if pend_gn[0] is not None:
                        P.defer = bgq
                        pend_gn[0]()
                        P.defer = None
                        pend_gn[0] = None
                    if tidx + 1 < 4 * NQ:
                        for w_ in range(3):
                            emit_proj(tidx + 1, w_)
                        P.defer = bgq
                        prep(tidx + 1)
                        P.defer = None
                    for lev in range(5):
                        bl = [rs.nxt(), rs.nxt()]
                        bw2 = [rs.nxt(), rs.nxt()] if lev < 4 else None
                        for g in range(2):
                            mm4(bl[g], g, Wt, uW[g], Lt, uL[g])
                            if lev < 4:
                                mm4(bw2[g], g, Lt, uL[g], Wt, uW[g])
                        for g in range(2):
                            add(ACT, lambda e: e.copy(out=g4(Lt, g), in_=b4(bl[g])), reads=[ub[bl[g]]], writes=[uL[g]])
                            tick()
                            if lev < 4:
                                if (lev + g) % 2 == 0:
                                    add(ACT, lambda e: e.copy(out=g4(Wt, g), in_=b4(bw2[g])), reads=[ub[bw2[g]]], writes=[uW[g]])
                                else:
                                    add(DVE, lambda e: e.tensor_copy(out=g4(Wt, g), in_=b4(bw2[g])), reads=[ub[bw2[g]]], writes=[uW[g]])
                        bx = [rs.nxt(), rs.nxt()]
                        for g in range(2):
                            mm4(bx[g], g, Lt, uL[g], Xt, uX[g], plus=(Xt, uX[g]))
                        for g in range(2):
                            evcopy(g4(Xt, g), bx[g], uX[g])
                    for (srcbd, usrc, dstt, udst) in ((bdA, uBA, ATMt, uATM), (bdBH, uBBH, BHTt, uBHT), (bdKH, uBKH, KHTt, uKHT)):
                        bk = rs.nxt()
                        pbf = banks[bk].bitcast(BF16)
                        for c in range(8):
                            add(PE, lambda e: e.transpose(pbf[:, c * 128:(c + 1) * 128], srcbd[:, c, :], identb), reads=[usrc, u_cst], writes=[ub[bk]])
                        add(ACT, lambda e: e.copy(out=dstt.rearrange("p c n -> p (c n)"), in_=pbf[:, 0:1024]), reads=[ub[bk]], writes=udst)
                        tick()
                    bk = rs.nxt()
                    for c in range(8):
                        add(PE, lambda e: e.matmul(banks[bk][:, c * 64:(c + 1) * 64], lhsT=bdV[:, c, :], rhs=stk, start=True, stop=True),
                            reads=[uBVv, u_cst], writes=[ub[bk]])
                    add(ACT, lambda e: e.copy(out=VSt.rearrange("p c i -> p (c i)"), in_=banks[bk][:, 0:512]), reads=[ub[bk]], writes=[u_VS])
                    bta = [rs.nxt(), rs.nxt()]
                    btk = [rs.nxt(), rs.nxt()]
                    for g in range(2):
                        mm4(bta[g], g, Xt, uX[g], ATMt, uATM[g])
                        mm4(btk[g], g, Xt, uX[g], LAKt, uLAK[g])
                    for g in range(2):
                        add(ACT, lambda e: e.copy(out=g4(Wt, g), in_=b4(bta[g])), reads=[ub[bta[g]]], writes=[uW[g]])
                        tick()
                        add(ACT, lambda e: e.copy(out=g4(Lt, g), in_=b4(btk[g])), reads=[ub[btk[g]]], writes=[uL[g]])
                        tick()
                    TAt, uTA, TKt, uTK = Wt, uW, Lt, uL
                    for g in range(2):
                        b1, b2, b3, b4_ = rs.nxt(), rs.nxt(), rs.nxt(), rs.nxt()
                        mm4(b1, g, TAt, uTA[g], BHTt, uBHT[g])
                        mm4(b2, g, TKt, uTK[g], BHTt, uBHT[g], plus=(KHTt, uKHT[g]))
                        mm4(b3, g, TAt, uTA[g], PRBt, uPRB[g], plus=(bdR, uBR))
                        mm4(b4_, g, TKt, uTK[g], PRBt, uPRB[g], plus=(PRKt, uPRK[g]))
                        evcopy(g4(ATMt, g), b1, uATM[g])
                        evcopy(g4(H2d, g), b2, uH2d[g])
                        evcopy(g4(YMd, g), b3, uYMd[g])
                        evcopy(g4(YVt, g), b4_, uYV[g])
                    G1t, uG1, H2t, uH2, YMt, uYM = ATMt, uATM, H2d, uH2d, YMd, uYMd
                    tick(10 ** 6)
                    def make_stage2(hp, tq, tsl, YA, uYA, TM, uTM, BON, uBON, Gb, uG, gcx, u_gc, G1t, uG1, H2t, uH2, YMt, uYM):
                        st = {}
                        steps = []

                        def chain_step(c):
                            def f():
                                if c == 0:
                                    st["bkY"] = rsY.nxt()
                                    if tq == 0:
                                        add(DVE, lambda e: e.memset(Mf, 0.0), writes=[u_M])
                                        add(DVE, lambda e: e.memset(Mb[0], 0.0), writes=[u_Mb[0]])
                                bkY = st["bkY"]
                                g = c // 4
                                mo = c % 2
                                bkM = rsM.nxt()
                                add(PE, lambda e: e.matmul(banks[bkM][:, 0:64], lhsT=G1t[:, c, :], rhs=Mb[mo], start=True, stop=False),
                                    reads=[uG1[g], u_Mb[mo]], writes=[ub[bkM]])
                                add(PE, lambda e: e.matmul(banks[bkM][:, 0:64], lhsT=H2t[:, c, :], rhs=VSt[:, c, :], start=False, stop=True),
                                    reads=[uH2[g], u_VS], writes=[ub[bkM]])
                                add(PE, lambda e: e.matmul(banks[bkY][:, c * 64:(c + 1) * 64], lhsT=YMt[:, c, :], rhs=Mb[mo], start=True, stop=False),
                                    reads=[uYM[g], u_Mb[mo]], writes=[ub[bkY]])
                                add(PE, lambda e: e.matmul(banks[bkY][:, c * 64:(c + 1) * 64], lhsT=YVt[:, c, :], rhs=VSt[:, c, :], start=False, stop=True),
                                    reads=[uYV[g], u_VS], writes=[ub[bkY]])
                                add(DVE, lambda e: e.scalar_tensor_tensor(out=Mb[1 - mo], in0=Mf, scalar=gcx[:, c:c + 1], in1=banks[bkM][:, 0:64],
                                                                          op0=ALU.mult, op1=ALU.add),
                                    reads=[u_M, u_gc, ub[bkM]], writes=[u_Mb[1 - mo]])
                                add(DVE, lambda e: e.scalar_tensor_tensor(out=Mf, in0=Mf, scalar=gcx[:, c:c + 1], in1=banks[bkM][:, 0:64],
                                                                          op0=ALU.mult, op1=ALU.add),
                                    reads=[u_M, u_gc, ub[bkM]], writes=[u_M])
                                if c == 7:
                                    add(ACT, lambda e: e.copy(out=YA, in_=banks[bkY][:, 0:512]), reads=[ub[bkY]], writes=[uYA])
                            return f

                        for c_ in range(8):
                            steps.append(chain_step(c_))

                        def gn_final():
                            YA3 = YA.rearrange("p (c i) -> p c i", i=64)
                            add(DVE, lambda e: e.tensor_reduce(out=gst[:, 0, :], in_=YA3, axis=AX.X, op=ALU.add), reads=[uYA], writes=[u_gst])
                            add(ACT, lambda e: e.activation(out=TM, in_=YA, func=AF.Square), reads=[uYA], writes=[uTM])
                            add(DVE, lambda e: e.tensor_reduce(out=gst[:, 1, :], in_=TM.rearrange("p (c i) -> p c i", i=64), axis=AX.X, op=ALU.add),
                                reads=[uTM, u_gst], writes=[u_gst])
                            add(DVE, lambda e: e.tensor_scalar(out=gst[:, 2, :], in0=gst[:, 0, :], scalar1=1.0 / 64, scalar2=None, op0=ALU.mult),
                                reads=[u_gst], writes=[u_gst])
                            add(DVE, lambda e: e.tensor_tensor(out=gst[:, 3, :], in0=gst[:, 2, :], in1=gst[:, 2, :], op=ALU.mult),
                                reads=[u_gst], writes=[u_gst])
                            add(DVE, lambda e: e.scalar_tensor_tensor(out=gst[:, 4, :], in0=gst[:, 1, :], scalar=1.0 / 64, in1=gst[:, 3, :],
                                                                      op0=ALU.mult, op1=ALU.subtract), reads=[u_gst], writes=[u_gst])
                            add(ACT, lambda e: e.activation(out=gst[:, 5, :], in_=gst[:, 4, :], func=AF.Sqrt, bias=GN_EPS), reads=[u_gst], writes=[u_gst])
                            add(DVE, lambda e: e.reciprocal(out=gst[:, 6, :], in_=gst[:, 5, :]), reads=[u_gst], writes=[u_gst])
                            add(DVE, lambda e: e.tensor_tensor(out=YA3, in0=YA3, in1=gst[:, 2, :].unsqueeze(2).broadcast_to([128, 8, 64]), op=ALU.subtract),
                                reads=[uYA, u_gst], writes=[uYA])
                            add(DVE, lambda e: e.tensor_tensor(out=YA3, in0=YA3, in1=gst[:, 6, :].unsqueeze(2).broadcast_to([128, 8, 64]), op=ALU.mult),
                                reads=[uYA, u_gst], writes=[uYA])
                            add(DVE, lambda e: e.tensor_tensor(out=YA3, in0=YA3, in1=gnt[:, hp, 0:1, :].broadcast_to([128, 8, 64]), op=ALU.mult),
                                reads=[uYA, u_gn], writes=[uYA])
                            for hh in range(2):
                                ps_ = slice(hh * 64, hh * 64 + 64)
                                add(DVE, lambda e, ps_=ps_, hh=hh: e.tensor_tensor(out=ybd[ps_, :, hh * 64:hh * 64 + 64], in0=YA3[ps_],
                                                                                   in1=gnt[ps_, hp, 1:2, :].broadcast_to([64, 8, 64]), op=ALU.add),
                                    reads=[uYA, u_gn], writes=[u_ybd])
                            P.atomic_begin()
                            bk = rp.nxt()
                            for c in range(8):
                                add(PE, lambda e, c=c, bk=bk: e.matmul(banks[bk][:, c * 64:(c + 1) * 64], lhsT=ybd[:, c, :], rhs=stk, start=True, stop=True),
                                    reads=[u_ybd, u_cst], writes=[ub[bk]])
                            add(DVE, lambda e, bk=bk: e.tensor_tensor(out=TM, in0=banks[bk][:, 0:TQ], in1=BON, op=ALU.add),
                                reads=[ub[bk], uBON], writes=[uTM])
                            P.atomic_end()
                            add(DVE, lambda e: e.tensor_tensor(out=brT[:, BR_RWK + hp, tsl], in0=TM, in1=Gb, op=ALU.mult),
                                reads=[uTM, uG], writes=[u_br[1]])

                        return steps, gn_final

                    steps_, gnf_ = make_stage2(hp, tq, tsl, YAd, u_YAd, TMd, u_TMd, BON, uBON, Gb, uG, gcx, u_gc, G1t, uG1, H2t, uH2, YMt, uYM)
                    stage2_q.extend(steps_)
                    assert pend_gn[0] is None
                    pend_gn[0] = gnf_

            pump(99)
            if pend_gn[0] is not None:
                pend_gn[0]()
                pend_gn[0] = None
            P.barrier(bar_scr[:, 0:1])
            cw = Carver(OFF_W, W_B)
            w_fx = cw.take(8 * FOX_COLS * 2, BF16, "p (k n) -> p k n", k=8)
            u_wfx = P.unit(f"wfx{b}")
            sload(w_fx.rearrange("p k n -> p (k n)"), S_fx, 0, 8 * FOX_COLS, u_wfx, uS_fx)
            cv = Carver(OFF_SCR, SCR_B)
            qa = [cv.take(S * 2, BF16) for _ in range(2)]
            ka = [cv.take(S * 2, BF16) for _ in range(2)]
            vaug = cv.take(NT * 2 * 65 * 2, BF16, "p (t h d) -> p t h d", t=NT, h=2)
            ftm = cv.take(NT * 128 * 2, BF16, "p (t n) -> p t n", t=NT)
            NPT = 4
            pT = [cv.take(4 * 128 * 2, BF16) for _ in range(NPT - 1)]
            u_qa = [P.unit(f"qa{b}{i}") for i in range(2)]
            u_ka = [P.unit(f"ka{b}{i}") for i in range(2)]
            u_v, u_ftm = P.unit(f"v{b}"), P.unit(f"ftm{b}")
            u_pT = [P.unit(f"pT{b}{i}") for i in range(NPT)]
            EL = cv.take(S * 4, F32)
            CPs = cv.take(S * 4, F32)
            QP = cv.take(3 * S * 2, BF16, "p (r s) -> p r s", r=3)
            cmem = Carver(OFF_BR + 4 * SMAX * 2, 4 * SMAX * 2)
            KP = cmem.take(3 * S * 2, BF16, "p (r s) -> p r s", r=3)
            pT.append(cmem.take(4 * 128 * 2, BF16))
            rden = cv.take(16, F32)
            u_EL, u_CP, u_KP, u_QP, u_rden = (P.unit(f"EL{b}"), P.unit(f"CP{b}"), P.unit(f"KP{b}"), P.unit(f"QP{b}"), P.unit(f"rden{b}"))
            rp = Rot([0, 1])
            for tq in range(NQ):
                tsl = slice(tq * TQ, (tq + 1) * TQ)
                bk = rp.nxt()
                proj_fm(w_fx, u_wfx, 1536, lambda k: uT[:, k, tsl], [u_uT], TQ, bk, m=8)
                add(ACT, lambda e, bk=bk, tsl=tsl: e.activation(out=EL[0:8, tsl], in_=banks[bk][0:8, 0:TQ], func=AF.Exp, scale=-1.0,
                                                                bias=pvx[0:8, 4:5]), reads=[ub[bk], u_pv], writes=[u_EL])
            add(ACT, lambda e: e.activation(out=EL[0:8, :], in_=EL[0:8, :], func=AF.Ln, bias=1.0), reads=[u_EL], writes=[u_EL])
            add(DVE, lambda e: e.tensor_tensor_scan(out=CPs[0:8, :], data0=EL[0:8, :], data1=EL[0:8, :], initial=0.0, op0=ALU.add, op1=ALU.max),
                reads=[u_EL], writes=[u_CP])
            def pieces(dst, u_dst):
                for r in range(3):
                    add(DVE, lambda e: e.tensor_copy(out=dst[0:8, r, :], in_=EL[0:8, :]), reads=[u_EL], writes=[u_dst])
                    if r < 2:
                        add(DVE, lambda e: e.tensor_tensor(out=EL[0:8, :], in0=EL[0:8, :], in1=dst[0:8, r, :], op=ALU.subtract),
                            reads=[u_EL, u_dst], writes=[u_EL])

            add(DVE, lambda e: e.tensor_scalar(out=EL[0:8, :], in0=CPs[0:8, :], scalar1=8.0, scalar2=None, op0=ALU.mult), reads=[u_CP], writes=[u_EL])
            pieces(KP, u_KP)
            CP3 = CPs[0:8, :].rearrange("p (t n) -> p t n", n=128)
            add(DVE, lambda e: e.tensor_scalar(out=EL[0:8, :].rearrange("p (t n) -> p t n", n=128), in0=CP3[:, :, 127:128].broadcast_to([8, NT, 128]),
                                               scalar1=-8.0, scalar2=None, op0=ALU.mult), reads=[u_CP, u_EL], writes=[u_EL])
            pieces(QP, u_QP)
            for h2 in range(2):
                base = 64 if h2 == 0 else 0
                add(DVE, lambda e: e.memset(ka[h2][base:base + 64, :], 0.0), writes=[u_ka[h2]])
                add(DVE, lambda e: e.memset(ka[h2][base:base + 32, :], 1.0), writes=[u_ka[h2]])
                add(DVE, lambda e: e.memset(qa[h2][base:base + 64, :], 0.0), writes=[u_qa[h2]])
                add(DVE, lambda e: e.memset(qa[h2][base:base + 3, :], 1.0), writes=[u_qa[h2]])
            add(DVE, lambda e: e.memset(vaug[:, :, :, 64:65], 1.0), writes=[u_v])
            caus = cstb[:, CST_CAUS:CST_CAUS + 128]
            rsS = Rot([2, 3, 4, 5])
            rsO = Rot([6, 7])
            for hp in range(4):
                for tq in range(NQ):
                    tsl = slice(tq * TQ, (tq + 1) * TQ)
                    bk = rp.nxt()
                    proj_fm(w_fx, u_wfx, hp * 128, lambda k: uT[:, k, tsl], [u_uT], TQ, bk)
                    add(ACT, lambda e: e.copy(out=qa[0][0:64, tsl], in_=banks[bk][0:64, 0:TQ]), reads=[ub[bk]], writes=[u_qa[0]])
                    add(ACT, lambda e: e.copy(out=qa[1][64:128, tsl], in_=banks[bk][64:128, 0:TQ]), reads=[ub[bk]], writes=[u_qa[1]])
                    bk = rp.nxt()
                    proj_fm(w_fx, u_wfx, 512 + hp * 128, lambda k: uT[:, k, tsl], [u_uT], TQ, bk)
                    add(DVE, lambda e: e.tensor_copy(out=ka[0][0:64, tsl], in_=banks[bk][0:64, 0:TQ]), reads=[ub[bk]], writes=[u_ka[0]])
                    add(DVE, lambda e: e.tensor_copy(out=ka[1][64:128, tsl], in_=banks[bk][64:128, 0:TQ]), reads=[ub[bk]], writes=[u_ka[1]])
                for h2 in range(2):
                    h = 2 * hp + h2
                    base = 64 if h2 == 0 else 0
                    for r in range(3):
                        add(SP, lambda e: e.dma_start(out=ka[h2][base + r:base + r + 1, :], in_=KP[h:h + 1, r, :]),
                            reads=[u_KP], writes=[u_ka[h2]], dma=u_ka[h2])
                        add(SP, lambda e: e.dma_start(out=qa[h2][base + 3 + r:base + 4 + r, :], in_=QP[h:h + 1, r, :]),
                            reads=[u_QP], writes=[u_qa[h2]], dma=u_qa[h2])
                for tt in range(NT):
                    bk = rp.nxt()
                    for k in range(8):
                        add(PE, lambda e, k=k, bk=bk, tt=tt: e.matmul(banks[bk][:, 0:128], lhsT=uT[:, k, tt * 128:(tt + 1) * 128],
                                                                      rhs=w_fx[:, k, 1024 + hp * 128:1024 + (hp + 1) * 128],
                                                                      start=(k == 0), stop=(k == 7)),
                            reads=[u_uT, u_wfx], writes=[ub[bk]])
                    add(ACT, lambda e, bk=bk, tt=tt: e.copy(out=vaug[:, tt, :, 0:64], in_=banks[bk][:, 0:128].rearrange("p (h d) -> p h d", h=2)),
                        reads=[ub[bk]], writes=[u_v])
                npt = 0
                pend = []

                def emit_pv(G):
                    (qb, h2, grp, sl, bo, last_of_qb) = G
                    for jj, kb in enumerate(grp):
                        add(PE, lambda e: e.matmul(banks[bo][:, h2 * 65:(h2 + 1) * 65], lhsT=pT[sl][:, jj * 128:(jj + 1) * 128],
                                                   rhs=vaug[:, kb, h2, :], start=(kb == 0), stop=(kb == qb)),
                            reads=[u_pT[sl], u_v], writes=[ub[bo]])
                    if last_of_qb:
                        qs = slice(qb * 128, (qb + 1) * 128)
                        o3 = banks[bo][:, 0:130].rearrange("p (h d) -> p h d", h=2)
                        add(DVE, lambda e: e.reciprocal(out=rden[:, 0:2].unsqueeze(2), in_=o3[:, :, 64:65]), reads=[ub[bo]], writes=[u_rden])
                        add(DVE, lambda e: e.tensor_tensor(out=ftm[:, qb, :].rearrange("p (h d) -> p h d", h=2), in0=o3[:, :, 0:64],
                                                           in1=rden[:, 0:2].unsqueeze(2).broadcast_to([128, 2, 64]), op=ALU.mult),
                            reads=[ub[bo], u_rden], writes=[u_ftm])
                        bk = rp.nxt()
                        pbf = banks[bk].bitcast(BF16)
                        add(PE, lambda e: e.transpose(pbf[:, 0:128], ftm[:, qb, :], identb), reads=[u_ftm, u_cst], writes=[ub[bk]])
                        add(DVE, lambda e: e.tensor_copy(out=brT[:, BR_FOX + hp, qs], in_=pbf[:, 0:128]), reads=[ub[bk]], writes=[u_br[0]])

                for qb in range(NT):
                    bo = rsO.nxt()
                    qs = slice(qb * 128, (qb + 1) * 128)
                    for h2 in range(2):
                        h = 2 * hp + h2
                        rows = slice(h2 * 64, h2 * 64 + 64)
                        starts = list(range(0, qb + 1, 4))
                        for kb0 in starts:
                            grp = list(range(kb0, min(kb0 + 4, qb + 1)))
                            bsk = rsS.nxt()
                            sl = npt % NPT
                            npt += 1
                            for jj, kb in enumerate(grp):
                                add(PE, lambda e: e.matmul(banks[bsk][:, jj * 128:(jj + 1) * 128], lhsT=ka[h2][:, kb * 128:(kb + 1) * 128],
                                                           rhs=qa[h2][:, qs], start=True, stop=True),
                                    reads=[u_ka[h2], u_qa[h2]], writes=[ub[bsk]])
                            ng = len(grp)
                            add(ACT, lambda e: e.activation(out=pT[sl][:, 0:ng * 128], in_=banks[bsk][:, 0:ng * 128], func=AF.Exp, scale=0.125),
                                reads=[ub[bsk]], writes=[u_pT[sl]])
                            for jj, kb in enumerate(grp):
                                if kb == qb:
                                    add(DVE, lambda e: e.tensor_tensor(out=pT[sl][:, jj * 128:(jj + 1) * 128], in0=pT[sl][:, jj * 128:(jj + 1) * 128],
                                                                       in1=caus, op=ALU.mult),
                                        reads=[u_pT[sl], u_cst], writes=[u_pT[sl]])
                            pend.append((qb, h2, grp, sl, bo, (h2 == 1 and kb0 == starts[-1])))
                            if len(pend) > 3:
                                emit_pv(pend.pop(0))
                while pend:
                    emit_pv(pend.pop(0))


            P.barrier(bar_scr[:, 0:1])
            cw = Carver(OFF_W, W_B)
            w_mq = cw.take(8 * 512 * 2, BF16, "p (k n) -> p k n", k=8)
            w_kv = cw.take(8 * 1024 * 2, BF16, "p (k n) -> p k n", k=8)
            u_wmq, u_wkv = P.unit(f"wmq{b}"), P.unit(f"wkv{b}")
            sload(w_mq.rearrange("p k n -> p (k n)"), S_mq, 0, 4096, u_wmq, uS_mq)
            sload(w_kv.rearrange("p k n -> p (k n)"), S_kv, 0, 8192, u_wkv, uS_kv)
            cv = Carver(OFF_SCR, SCR_B)
            mnT = cv.take(8 * MEML * 2, BF16, "p (k t) -> p k t", k=8)
            u_mn = P.unit(f"mn{b}")
            rms_T(cv, lambda tt: mem_d[b, tt * 128:(tt + 1) * 128, :], 2, PV_MEMG, mnT, u_mn, [0, 1], f"D{b}")
            kmT = cv.take(4 * MEML * 2, BF16, "p (h t) -> p h t", h=4)
            vma = cv.take(2 * 4 * 129 * 2, BF16, "p (t h d) -> p t h d", t=2, h=4)
            qmT = cv.take(S * 2, BF16)
            mtm = cv.take(NT * 128 * 2, BF16, "p (t n) -> p t n", t=NT)
            pM = [cv.take(TQ * 2, BF16) for _ in range(2)]
            rdm = cv.take(16, F32)
            u_km, u_vm, u_qm, u_mtm, u_rdm = P.unit(f"km{b}"), P.unit(f"vm{b}"), P.unit(f"qm{b}"), P.unit(f"mtm{b}"), P.unit(f"rdm{b}")
            u_pM = [P.unit(f"pM{b}{i}") for i in range(2)]
            rp = Rot([0, 1])
            for h in range(4):
                bk = rp.nxt()
                proj_fm(w_kv, u_wkv, h * 128, lambda k: mnT[:, k, :], [u_mn], MEML, bk)
                add(ACT, lambda e, bk=bk, h=h: e.copy(out=kmT[:, h, :], in_=banks[bk][:, 0:MEML]), reads=[ub[bk]], writes=[u_km])
            add(DVE, lambda e: e.memset(vma[:, :, :, 128:129], 1.0), writes=[u_vm])
            for kt in range(2):
                bk = rp.nxt()
                for k in range(8):
                    add(PE, lambda e, k=k, bk=bk, kt=kt: e.matmul(banks[bk][:, 0:512], lhsT=mnT[:, k, kt * 128:(kt + 1) * 128],
                                                                  rhs=w_kv[:, k, 512:1024], start=(k == 0), stop=(k == 7)),
                        reads=[u_mn, u_wkv], writes=[ub[bk]])
                add(ACT, lambda e, bk=bk, kt=kt: e.copy(out=vma[:, kt, :, 0:128], in_=banks[bk][:, 0:512].rearrange("p (h d) -> p h d", h=4)),
                    reads=[ub[bk]], writes=[u_vm])
            rsS = Rot([2, 3])
            rsO = Rot([4, 5, 6, 7])
            for h in range(4):
                for tq in range(NQ):
                    tsl = slice(tq * TQ, (tq + 1) * TQ)
                    bk = rp.nxt()
                    proj_fm(w_mq, u_wmq, h * 128, lambda k: uT[:, k, tsl], [u_uT], TQ, bk)
                    add(ACT, lambda e, bk=bk, tsl=tsl: e.copy(out=qmT[:, tsl], in_=banks[bk][:, 0:TQ]), reads=[ub[bk]], writes=[u_qm])
                for tq in range(NQ):
                    tsl = slice(tq * TQ, (tq + 1) * TQ)
                    for kt in range(2):
                        bsk = rsS.nxt()
                        add(PE, lambda e, bsk=bsk, kt=kt: e.matmul(banks[bsk][:, 0:TQ], lhsT=kmT[:, h, kt * 128:(kt + 1) * 128], rhs=qmT[:, tsl],
                                                                   start=True, stop=True), reads=[u_km, u_qm], writes=[ub[bsk]])
                        add(ACT, lambda e, bsk=bsk, kt=kt: e.activation(out=pM[kt], in_=banks[bsk][:, 0:TQ], func=AF.Exp, scale=128 ** -0.5),
                            reads=[ub[bsk]], writes=[u_pM[kt]])
                    for half in range(2):
                        bo = rsO.nxt()
                        for s2 in range(2):
                            sub = half * 2 + s2
                            for kt in range(2):
                                add(PE, lambda e, bo=bo, s2=s2, sub=sub, kt=kt: e.matmul(banks[bo][:, s2 * 129:(s2 + 1) * 129],
                                                                                         lhsT=pM[kt][:, sub * 128:(sub + 1) * 128],
                                                                                         rhs=vma[:, kt, h, :], start=(kt == 0), stop=(kt == 1)),
                                    reads=[u_pM[kt], u_vm], writes=[ub[bo]])
                        o3 = banks[bo][:, 0:258].rearrange("p (s d) -> p s d", s=2)
                        t0 = tq * 4 + half * 2
                        add(DVE, lambda e, o3=o3: e.reciprocal(out=rdm[:, 0:2].unsqueeze(2), in_=o3[:, :, 128:129]), reads=[ub[bo]], writes=[u_rdm])
                        add(DVE, lambda e, o3=o3, t0=t0: e.tensor_tensor(out=mtm[:, t0:t0 + 2, :], in0=o3[:, :, 0:128],
                                                                         in1=rdm[:, 0:2].unsqueeze(2).broadcast_to([128, 2, 128]), op=ALU.mult),
                            reads=[ub[bo], u_rdm], writes=[u_mtm])
                        for s2 in range(2):
                            tt = t0 + s2
                            bk = rp.nxt()
                            pbf = banks[bk].bitcast(BF16)
                            add(PE, lambda e, pbf=pbf, tt=tt: e.transpose(pbf[:, 0:128], mtm[:, tt, :], identb), reads=[u_mtm, u_cst], writes=[ub[bk]])
                            add(ACT, lambda e, pbf=pbf, tt=tt: e.copy(out=brT[:, BR_MEM + h, tt * 128:(tt + 1) * 128], in_=pbf[:, 0:128]),
                                reads=[ub[bk]], writes=[u_br[2]])

            if dbg and b == 0:
                P.barrier(bar_scr[:, 0:1])
                cvd = Carver(OFF_SCR, SCR_B)
                dt2 = cvd.take(S * 4, F32)
                u_d2 = P.unit("dbgB")
                for kk_ in range(12):
                    add(DVE, lambda e, kk_=kk_: e.tensor_copy(out=dt2, in_=brT[:, kk_, :]), reads=u_br, writes=[u_d2])
                    add(SP, lambda e, kk_=kk_: e.dma_start(out=dbg_d["d_brT"][:, kk_ * S:(kk_ + 1) * S], in_=dt2), reads=[u_d2], dma=u_d2, is_out=True)

            P.barrier(bar_scr[:, 0:1])
            cw = Carver(OFF_W, W_B)
            w_bo = cw.take(3 * 4 * 1024 * 2, BF16, "p (r c n) -> p r c n", r=3, c=4)
            u_wbo = P.unit(f"wbo{b}")
            sload(w_bo.rearrange("p r c n -> p (r c n)"), S_bo, 0, 12 * 1024, u_wbo, uS_bo)
            wg = [cw.take(8 * 3 * 128 * 2, BF16, "p (k r n) -> p k r n", k=8, r=3) for _ in range(2)]
            u_wg = [P.unit(f"wg{b}{i}", nobar=False) for i in range(2)]
            cv = Carver(OFF_SCR, SCR_B)
            mgT = cv.take(8 * SMAX * 2, BF16, "p (k t) -> p k t", k=8)[:, :, 0:S]
            u_mg = P.unit(f"mg{b}")
            sgt = [cv.take(TQ * 4, F32) for _ in range(3)]
            u_sg = [P.unit(f"sg{b}{i}") for i in range(3)]
            ra = Rot(range(8))
            for oc in range(8):
                s = oc % 2
                sload(wg[s].rearrange("p k r n -> p (k r n)"), S_gt, oc * 3072, 3072, u_wg[s], uS_gt)
                for tq in range(NQ):
                    tsl = slice(tq * TQ, (tq + 1) * TQ)
                    bg_, bb_ = [], []
                    for r in range(3):
                        bk = ra.nxt()
                        bg_.append(bk)
                        for k in range(8):
                            add(PE, lambda e, k=k, bk=bk, r=r: e.matmul(banks[bk][:, 0:TQ], lhsT=wg[s][:, k, r, :], rhs=uT[:, k, tsl],
                                                                        start=(k == 0), stop=(k == 7)), reads=[u_wg[s], u_uT], writes=[ub[bk]])
                        add(ACT, lambda e, bk=bk, r=r: e.activation(out=sgt[r], in_=banks[bk][:, 0:TQ], func=AF.Sigmoid), reads=[ub[bk]], writes=[u_sg[r]])
                    for r in range(3):
                        bk = ra.nxt()
                        bb_.append(bk)
                        for c in range(4):
                            add(PE, lambda e, c=c, bk=bk, r=r: e.matmul(banks[bk][:, 0:TQ], lhsT=w_bo[:, r, c, oc * 128:(oc + 1) * 128],
                                                                        rhs=brT[:, (BR_FOX, BR_RWK, BR_MEM)[r] + c, tsl], start=(c == 0), stop=(c == 3)),
                                reads=[u_wbo, u_br[r]], writes=[ub[bk]])
                        add(DVE, lambda e, bk=bk, r=r: e.tensor_tensor(out=sgt[r], in0=sgt[r], in1=banks[bk][:, 0:TQ], op=ALU.mult),
                            reads=[u_sg[r], ub[bk]], writes=[u_sg[r]])
                    add(DVE, lambda e: e.tensor_tensor(out=sgt[0], in0=sgt[0], in1=sgt[1], op=ALU.add), reads=[u_sg[0], u_sg[1]], writes=[u_sg[0]])
                    add(DVE, lambda e, tsl=tsl: e.tensor_tensor(out=mgT[:, oc, tsl], in0=sgt[0], in1=sgt[2], op=ALU.add),
                        reads=[u_sg[0], u_sg[2]], writes=[u_mg])

            P.barrier(bar_scr[:, 0:1])
            cw = Carver(OFF_W, W_B)
            w_dn = cw.take(NFC * 1024 * 2, BF16, "p (f n) -> p f n", f=NFC)
            u_wdn = P.unit(f"wdn{b}")
            cb = Carver(OFF_BR, BR_B)
            w_o = cb.take(8 * 1024 * 2, BF16, "p (k n) -> p k n", k=8)
            u_wo = P.unit(f"wo{b}")
            sload(w_o.rearrange("p k n -> p (k n)"), S_o, 0, 8192, u_wo, uS_o)
            sload(w_dn.rearrange("p f n -> p (f n)"), S_dn, 0, NFC * 1024, u_wdn, uS_dn)
            actT = cb.take(NFC * TQ * 2, BF16, "p (f t) -> p f t", f=NFC)
            u_act = P.unit(f"act{b}")
            cu = Carver(OFF_UT, UT_B)
            hres = cu.take(4 * 1024 * 4, F32, "p (s n) -> p s n", s=4)
            u2T = cu.take(8 * TQ * 2, BF16, "p (k t) -> p k t", k=8)
            xt = [cu.take(4096, F32) for _ in range(2)]
            u_hres = [P.unit(f"hres{b}{i}") for i in range(4)]
            u_u2 = P.unit(f"u2{b}")
            u_xt = [P.unit(f"xt{b}{i}") for i in range(2)]
            cv = Carver(OFF_SCR + 8 * SMAX * 2, SCR_B - 8 * SMAX * 2)
            wgu = [cv.take(8 * 2 * 128 * 2, BF16, "p (k r n) -> p k r n", k=8, r=2) for _ in range(3)]
            wgu.append(cw.take(8 * 2 * 128 * 2, BF16, "p (k r n) -> p k r n", k=8, r=2))
            NWG = len(wgu)
            u_wgu = [P.unit(f"wgu{b}{i}") for i in range(NWG)]
            pg = cv.take(2 * 1024 * 4, F32, "p (r n) -> p r n", r=2)
            u_pg = P.unit(f"pg{b}")
            add(SP, lambda e: e.dma_start(out=pg.rearrange("p r n -> p (r n)"), in_=rowp_d.partition_broadcast(128)), writes=[u_pg], dma=u_pg)
            yt = cb.take(4096, F32)
            u_yt = P.unit(f"yt{b}")
            hnb = cb.take(2048, BF16)
            u_hnb = P.unit(f"hnb{b}")
            jk = cb.take(2048, BF16)
            u_jk = P.unit(f"jk{b}")
            sst = cb.take(64 * 4, F32)
            u_ss = P.unit(f"ss{b}")
            slt = [cv.take(TQ * 4, F32) for _ in range(2)]
            u_sl = [P.unit(f"slt{b}{i}") for i in range(2)]
            ra = Rot(range(8))
            nxl = [0]
            nwl = 0
            for tq in range(NQ):
                def e2_mm(sub):
                        tok0 = tq * TQ + sub * 128
                        by = [ra.nxt(), ra.nxt()]
                        for nh in range(2):
                            for k in range(8):
                                add(PE, lambda e, k=k, nh=nh, by=by, tok0=tok0: e.matmul(banks[by[nh]][:, 0:512], lhsT=mgT[:, k, tok0:tok0 + 128],
                                                                                       rhs=w_o[:, k, nh * 512:(nh + 1) * 512], start=(k == 0), stop=(k == 7)),
                                    reads=[u_mg, u_wo], writes=[ub[by[nh]]])
                            add(ACT, lambda e, nh=nh, by=by: e.activation(out=jk[:, 0:512], in_=banks[by[nh]][:, 0:512], func=AF.Square,
                                                                          accum_out=sst[:, sub * 16 + nh:sub * 16 + nh + 1]), reads=[ub[by[nh]]], writes=[u_jk, u_ss])
                        return by, tok0

                def e2_post(sub, by, tok0):
                        add(DVE, lambda e: e.tensor_tensor(out=sst[:, sub * 16 + 2:sub * 16 + 3], in0=sst[:, sub * 16 + 0:sub * 16 + 1], in1=sst[:, sub * 16 + 1:sub * 16 + 2], op=ALU.add), reads=[u_ss], writes=[u_ss])
                        add(ACT, lambda e: e.activation(out=sst[:, sub * 16 + 3:sub * 16 + 4], in_=sst[:, sub * 16 + 2:sub * 16 + 3], func=AF.Sqrt, scale=1.0 / D, bias=NORM_EPS), reads=[u_ss], writes=[u_ss])
                        add(DVE, lambda e: e.reciprocal(out=sst[:, sub * 16 + 4:sub * 16 + 5], in_=sst[:, sub * 16 + 3:sub * 16 + 4]), reads=[u_ss], writes=[u_ss])
                        xs = nxl[0] % 2
                        nxl[0] += 1
                        add(SP, lambda e, xs=xs, tok0=tok0: e.dma_start(out=xt[xs], in_=x_d[b, tok0:tok0 + 128, :]), writes=[u_xt[xs]], dma=u_xt[xs])
                        for nh in range(2):
                            add(DVE, lambda e, nh=nh, by=by: e.scalar_tensor_tensor(out=yt[:, nh * 512:(nh + 1) * 512], in0=banks[by[nh]][:, 0:512],
                                                                                   scalar=sst[:, sub * 16 + 4:sub * 16 + 5], in1=pg[:, 0, nh * 512:(nh + 1) * 512],
                                                                                   op0=ALU.mult, op1=ALU.mult),
                                reads=[ub[by[nh]], u_ss, u_pg], writes=[u_yt])
                        add(DVE, lambda e, xs=xs, sub=sub: e.tensor_tensor(out=hres[:, sub, :], in0=yt, in1=xt[xs], op=ALU.add),
                            reads=[u_yt, u_xt[xs]], writes=[u_hres[sub]])
                        add(ACT, lambda e, sub=sub: e.activation(out=jk, in_=hres[:, sub, :], func=AF.Square, accum_out=sst[:, sub * 16 + 5:sub * 16 + 6]),
                            reads=[u_hres[sub]], writes=[u_jk, u_ss])
                        add(ACT, lambda e: e.activation(out=sst[:, sub * 16 + 6:sub * 16 + 7], in_=sst[:, sub * 16 + 5:sub * 16 + 6], func=AF.Sqrt, scale=1.0 / D, bias=NORM_EPS), reads=[u_ss], writes=[u_ss])
                        add(DVE, lambda e: e.reciprocal(out=sst[:, sub * 16 + 7:sub * 16 + 8], in_=sst[:, sub * 16 + 6:sub * 16 + 7]), reads=[u_ss], writes=[u_ss])
                        add(DVE, lambda e, sub=sub: e.tensor_scalar(out=hnb, in0=hres[:, sub, :], scalar1=sst[:, sub * 16 + 7:sub * 16 + 8], scalar2=None, op0=ALU.mult),
                            reads=[u_hres[sub], u_ss], writes=[u_hnb])
                        bk = ra.nxt()
                        pbf = banks[bk].bitcast(BF16)
                        for c in range(8):
                            add(PE, lambda e, c=c, pbf=pbf: e.transpose(pbf[:, c * 128:(c + 1) * 128], hnb[:, c * 128:(c + 1) * 128], identb),
                                reads=[u_hnb, u_cst], writes=[ub[bk]])
                        add(DVE, lambda e, pbf=pbf, sub=sub: e.tensor_tensor(out=u2T[:, :, sub * 128:(sub + 1) * 128],
                                                                             in0=pbf[:, 0:1024].rearrange("p (k t) -> p k t", k=8),
                                                                             in1=pv[:, PV_PRE2:PV_PRE2 + 8].unsqueeze(2).broadcast_to([128, 8, 128]), op=ALU.mult),
                            reads=[ub[bk], u_pv], writes=[u_u2])
                        if dbg and b == 0:
                            add(SP, lambda e, sub=sub, tok0=tok0: e.dma_start(out=dbg_d["d_h"][tok0:tok0 + 128, :], in_=hres[:, sub, :]),
                                reads=[u_hres[sub]], dma=u_hres[sub], is_out=True)

                pend2 = []
                for sub in range(4):
                    pend2.append((sub,) + e2_mm(sub))
                    if len(pend2) > 3:
                        e2_post(*pend2.pop(0))
                while pend2:
                    e2_post(*pend2.pop(0))
                for fc in range(NFC):
                    s = nwl % NWG
                    nwl += 1
                    sload(wgu[s].rearrange("p k r n -> p (k r n)"), S_ff, fc * 2048, 2048, u_wgu[s], uS_ff)
                    bgk, buk = ra.nxt(), ra.nxt()
                    for r, bk in ((0, bgk), (1, buk)):
                        for k in range(8):
                            add(PE, lambda e, k=k, r=r, bk=bk, s=s: e.matmul(banks[bk][:, 0:TQ], lhsT=wgu[s][:, k, r, :], rhs=u2T[:, k, :],
                                                                            start=(k == 0), stop=(k == 7)), reads=[u_wgu[s], u_u2], writes=[ub[bk]])
                    add(ACT, lambda e, bgk=bgk, s=s: e.activation(out=slt[s % 2], in_=banks[bgk][:, 0:TQ], func=AF.Silu), reads=[ub[bgk]], writes=[u_sl[s % 2]])
                    add(DVE, lambda e, buk=buk, s=s, fc=fc: e.tensor_tensor(out=actT[:, fc, :], in0=slt[s % 2], in1=banks[buk][:, 0:TQ], op=ALU.mult),
                        reads=[u_sl[s % 2], ub[buk]], writes=[u_act])
                for sub in range(4):
                    tok0 = tq * TQ + sub * 128
                    by = [ra.nxt(), ra.nxt()]
                    for nh in range(2):
                        for f in range(NFC):
                            add(PE, lambda e, f=f, nh=nh, by=by, sub=sub: e.matmul(banks[by[nh]][:, 0:512], lhsT=actT[:, f, sub * 128:(sub + 1) * 128],
                                                                                 rhs=w_dn[:, f, nh * 512:(nh + 1) * 512], start=(f == 0), stop=(f == NFC - 1)),
                                reads=[u_act, u_wdn], writes=[ub[by[nh]]])
                        add(ACT, lambda e, nh=nh, by=by: e.activation(out=jk[:, 0:512], in_=banks[by[nh]][:, 0:512], func=AF.Square,
                                                                      accum_out=sst[:, 8 + nh:9 + nh]), reads=[ub[by[nh]]], writes=[u_jk, u_ss])
                    add(DVE, lambda e: e.tensor_tensor(out=sst[:, 10:11], in0=sst[:, 8:9], in1=sst[:, 9:10], op=ALU.add), reads=[u_ss], writes=[u_ss])
                    add(ACT, lambda e: e.activation(out=sst[:, 11:12], in_=sst[:, 10:11], func=AF.Sqrt, scale=1.0 / D, bias=NORM_EPS), reads=[u_ss], writes=[u_ss])
                    add(DVE, lambda e: e.reciprocal(out=sst[:, 12:13], in_=sst[:, 11:12]), reads=[u_ss], writes=[u_ss])
                    for nh in range(2):
                        add(DVE, lambda e, nh=nh, by=by: e.scalar_tensor_tensor(out=yt[:, nh * 512:(nh + 1) * 512], in0=banks[by[nh]][:, 0:512],
                                                                               scalar=sst[:, 12:13], in1=pg[:, 1, nh * 512:(nh + 1) * 512],
                                                                               op0=ALU.mult, op1=ALU.mult),
                            reads=[ub[by[nh]], u_ss, u_pg], writes=[u_yt])
                    add(DVE, lambda e, sub=sub: e.tensor_tensor(out=hres[:, sub, :], in0=yt, in1=hres[:, sub, :], op=ALU.add),
                        reads=[u_yt, u_hres[sub]], writes=[u_hres[sub]])
                    add(SP, lambda e, sub=sub, tok0=tok0: e.dma_start(out=out_d[b, tok0:tok0 + 128, :], in_=hres[:, sub, :]),
                        reads=[u_hres[sub]], dma=u_hres[sub], is_out=True)
        P.emit()
        nops = P.nops
    return nc, nops


def _cols(v, n):
    return np.ascontiguousarray(np.asarray(v, np.float32).reshape(n, 128).T)


def host_params(inp):
    pvv = np.zeros((128, NPV), np.float32)
    pvv[:, PV_PRE1:PV_PRE1 + 8] = _cols(inp["pre1_g"][0], 8)
    pvv[:, PV_PRE2:PV_PRE2 + 8] = _cols(inp["pre2_g"][0], 8)
    pvv[:, PV_MEMG:PV_MEMG + 8] = _cols(inp["mem_norm_g"][0], 8)
    pvv[:, PV_MU:PV_MU + 14] = _cols(inp["rwkv_mu"][0], 14)
    pvv[:, PV_W0:PV_W0 + 4] = _cols(inp["rwkv_w0"][0], 4)
    pvv[:, PV_A0:PV_A0 + 4] = _cols(inp["rwkv_a0"][0], 4)
    pvv[:, PV_KK:PV_KK + 4] = _cols(inp["rwkv_k_k"][0], 4)
    pvv[:, PV_KA:PV_KA + 4] = _cols(inp["rwkv_k_a"][0], 4)
    pvv[:, PV_RK:PV_RK + 4] = _cols(np.asarray(inp["rwkv_r_k"][0]).reshape(-1), 4)
    pvv[0:8, PV_FB] = np.asarray(inp["fox_f_bias"][0], np.float32)
    gg = np.asarray(inp["rwkv_gn_g"][0], np.float32).reshape(4, 2, 64)
    gb = np.asarray(inp["rwkv_gn_b"][0], np.float32).reshape(4, 2, 64)
    gnp = np.zeros((128, 4, 2, 64), np.float32)
    for hp in range(4):
        for hh in range(2):
            gnp[hh * 64:(hh + 1) * 64, hp, 0, :] = gg[hp, hh][None, :]
            gnp[hh * 64:(hh + 1) * 64, hp, 1, :] = gb[hp, hh][None, :]
    rowp = np.concatenate([np.asarray(inp["post1_g"][0], np.float32), np.asarray(inp["post2_g"][0], np.float32)])[None, :]
    f = lambda k: np.ascontiguousarray(np.asarray(inp[k][0], np.float32))
    shared = {
        "w_in": f("w_in"), "w_mem_kv": f("w_mem_kv"), "w_fox_out": f("w_fox_out"), "w_rwkv_out": f("w_rwkv_out"),
        "w_mem_out": f("w_mem_out"), "w_o": f("w_o"), "w_ffn_gate": f("w_ffn_gate"), "w_ffn_up": f("w_ffn_up"),
        "w_ffn_down": f("w_ffn_down"), "rwkv_w_up": f("rwkv_w_up"), "rwkv_a_up": f("rwkv_a_up"), "rwkv_g_up": f("rwkv_g_up"),
        "pv": pvv, "cst": make_consts(), "gnp": gnp.reshape(128, -1), "rowp": np.ascontiguousarray(rowp),
    }
    return shared


_CACHE = {}


def kernel(**inputs):
    x = np.asarray(inputs["x"], np.float32)
    mem = np.asarray(inputs["mem"], np.float32)
    B, S, _ = x.shape
    n = 8
    NB = B // n
    shared = host_params(inputs)
    key = (NB, S)
    if key not in _CACHE:
        _CACHE[key] = build(NB=NB, S=S)[0]
    nc = _CACHE[key]
    in_maps = []
    for c in range(n):
        m = dict(shared)
        m["x"] = np.ascontiguousarray(x[c * NB:(c + 1) * NB])
        m["mem"] = np.ascontiguousarray(mem[c * NB:(c + 1) * NB])
        in_maps.append(m)
    res = run_bass_kernel_spmd(nc, in_maps, core_ids=list(range(n)))
    out = np.concatenate([np.asarray(r["out"], np.float32) for r in res.results], axis=0)
    return out
```

```python
import contextlib
import numpy as np
import concourse.bass as bass
import concourse.mybir as mybir
from concourse.bass_utils import run_bass_kernel_spmd

F32 = mybir.dt.float32
BF16 = mybir.dt.bfloat16
ALU = mybir.AluOpType
AF = mybir.ActivationFunctionType
AX = mybir.AxisListType

PE, ACT, DVE, POOL, SP = "tensor", "scalar", "vector", "gpsimd", "sync"
ENGS = (PE, ACT, DVE, POOL, SP)
EPOCH = 24000

D = 1024
MEML = 256
DFF = 2816
NFC = DFF // 128
FOX_COLS = 1544
RW_COLS = 1792
RW0 = FOX_COLS
MQ0 = RW0 + RW_COLS
GT0 = MQ0 + 512
IN_COLS = GT0 + 3072
C0 = float(np.exp(-0.5))
NORM_EPS = 1e-6
GN_EPS = 64e-5


class Unit:
    __slots__ = ("name", "last_w", "readers", "psum", "sem", "cnt", "nobar")

    def __init__(self, name, psum=False, nobar=False):
        self.name = name
        self.last_w = None
        self.readers = []
        self.psum = psum
        self.sem = None
        self.cnt = 0
        self.nobar = nobar


class Op:
    __slots__ = ("eng", "fn", "deps", "mark", "midx", "dma", "sem", "val", "waits")

    def __init__(self, eng, fn, dma):
        self.eng = eng
        self.fn = fn
        self.deps = []
        self.mark = False
        self.midx = -1
        self.dma = dma
        self.sem = None
        self.val = 0
        self.waits = []


class _Rec:
    __slots__ = ("call",)

    def __init__(self):
        self.call = None

    def __getattr__(self, name):
        def f(*a, **k):
            assert self.call is None
            self.call = (name, a, k)
            return self
        return f


class Prog:
    def __init__(self, nc, stack):
        self.nc = nc
        self.stack = stack
        self.ops = {e: [] for e in ENGS}
        self.units = []
        self.dma_units = []
        self.out_dma_ops = []
        self.nops = 0
        self.cur_bar = None
        self.defer = None
        self.atomic_depth = 0

    def unit(self, name, psum=False, nobar=False):
        u = Unit(name, psum, nobar)
        u.last_w = self.cur_bar
        self.units.append(u)
        return u

    def sb(self, name, shape, dtype):
        return self.stack.enter_context(self.nc.sbuf_tensor(name, list(shape), dtype))

    def ps(self, name, shape, dtype):
        return self.stack.enter_context(self.nc.psum_tensor(name, list(shape), dtype))

    def add(self, eng, fn, reads=(), writes=(), dma=None, is_out=False):
        rec = _Rec()
        fn(rec)
        assert rec.call is not None
        if self.defer is not None:
            entry = (eng, rec.call, tuple(reads), tuple(writes), dma, is_out)
            if self.atomic_depth and self.defer and self.defer[-1][0]:
                self.defer[-1][1].append(entry)
            else:
                self.defer.append([bool(self.atomic_depth), [entry]])
            return None
        return self._reg(eng, rec.call, reads, writes, dma, is_out)

    def atomic_begin(self):
        self.atomic_depth += 1
        if self.defer is not None:
            self.defer.append([True, []])

    def atomic_end(self):
        self.atomic_depth -= 1
        if self.defer is not None and self.defer and self.defer[-1][0]:
            self.defer[-1][0] = False

    def run_deferred(self, queue, n=1):
        for _ in range(n):
            if not queue:
                return
            _, entries = queue.pop(0)
            for (eng, call, reads, writes, dma, is_out) in entries:
                self._reg(eng, call, reads, writes, dma, is_out)

    def _reg(self, eng, call, reads=(), writes=(), dma=None, is_out=False):
        op = Op(eng, call, dma)
        self.nops += 1
        deps = op.deps
        for u in reads:
            if u.psum:
                if u.last_w is not None:
                    deps.append((u.last_w, "RAW"))
                u.last_w = op
                continue
            if u.last_w is not None:
                deps.append((u.last_w, "RAW"))
            u.readers.append(op)
        for u in writes:
            if u.last_w is not None and u.last_w is not op:
                deps.append((u.last_w, "RAW" if u.psum else "WAW"))
            for r in u.readers:
                if r is not op:
                    deps.append((r, "WAR"))
            u.last_w = op
            u.readers = []
        if dma is not None:
            if dma.sem is None:
                dma.sem = True
                self.dma_units.append(dma)
            dma.cnt += 16
            op.sem = dma
            op.val = dma.cnt
            if is_out:
                self.out_dma_ops.append(op)
        self.ops[eng].append(op)
        return op

    def barrier(self, scratch_ap):
        us = [u for u in self.units if not u.nobar]
        self.cur_bar = self.add(DVE, lambda e: e.memset(scratch_ap, 0.0), writes=us)

    def emit(self):
        nc = self.nc
        for e in ENGS:
            for op in self.ops[e]:
                need = []
                seen = set()
                for (p, kind) in op.deps:
                    if id(p) in seen:
                        continue
                    if p.dma is not None or op.dma is not None:
                        pass
                    elif p.eng == e:
                        if e == PE or kind != "RAW":
                            continue
                    seen.add(id(p))
                    need.append(p)
                    if p.dma is None:
                        p.mark = True
                op.waits = need
                op.deps = None
        nep = {}
        for e in ENGS:
            k = 0
            for op in self.ops[e]:
                if op.mark:
                    op.midx = k
                    k += 1
            nep[e] = (k + EPOCH - 1) // EPOCH
        esem = {e: [self.stack.enter_context(nc.semaphore(f"es_{e}_{i}")) for i in range(nep[e])] for e in ENGS}
        for u in self.dma_units:
            u.sem = self.stack.enter_context(nc.semaphore(f"ds_{u.name}"))
        block = self.stack.enter_context(nc.Block())
        prog = self

        def body(e):
            def run(eng):
                waited = {}
                for op in prog.ops[e]:
                    for p in op.waits:
                        if p.dma is not None:
                            key = ("d", id(p.sem))
                            sem = p.sem.sem
                            val = p.val
                        else:
                            ep = p.midx // EPOCH
                            key = (p.eng, ep)
                            sem = esem[p.eng][ep]
                            val = p.midx % EPOCH + 1
                        if waited.get(key, 0) >= val:
                            continue
                        waited[key] = val
                        eng.wait_ge(sem, val)
                    nm, a_, k_ = op.fn
                    ins = getattr(eng, nm)(*a_, **k_)
                    if op.dma is not None:
                        ins.then_inc(op.sem.sem, 16)
                    elif op.mark:
                        ins.then_inc(esem[e][op.midx // EPOCH], 1)
                if e == SP:
                    done = set()
                    for op in prog.out_dma_ops:
                        if id(op.sem) in done:
                            continue
                        done.add(id(op.sem))
                        eng.wait_ge(op.sem.sem, op.sem.cnt)
            return run

        block.tensor(body(PE))
        block.scalar(body(ACT))
        block.vector(body(DVE))
        block.gpsimd(body(POOL))
        block.sync(body(SP))


CST_IDENT, CST_SU, CST_SL, CST_UI, CST_CAUS, CST_ONES, CST_SEL, CST_BONES, CST_STK = 0, 128, 256, 384, 512, 640, 768, 896, 1024
NCST = 1088
PV_PRE1, PV_PRE2, PV_MEMG, PV_MU, PV_W0, PV_A0, PV_KK, PV_KA, PV_RK, PV_FB = 0, 8, 16, 24, 38, 42, 46, 50, 54, 58
NPV = 64


def make_consts():
    c = np.zeros((128, NCST), np.float32)
    i = np.arange(128)
    blk = (i[:, None] // 64) == (i[None, :] // 64)
    s = i[:, None] % 64
    t = i[None, :] % 64
    c[:, CST_IDENT:CST_IDENT + 128] = np.eye(128)
    c[:, CST_SU:CST_SU + 128] = blk & (s < t)
    c[:, CST_SL:CST_SL + 128] = blk & (s > t)
    c[:, CST_UI:CST_UI + 128] = blk & (s <= t)
    c[:, CST_CAUS:CST_CAUS + 128] = i[:, None] <= i[None, :]
    c[:, CST_ONES:CST_ONES + 128] = 1.0
    c[127, CST_SEL:CST_SEL + 128] = 1.0
    c[:, CST_BONES:CST_BONES + 128] = blk
    c[:, CST_STK:CST_STK + 64] = (i[:, None] % 64) == np.arange(64)[None, :]
    return c


def build(NB=2, S=2048, dbg=False):
    nc = bass.Bass("TRN2", target_bir_lowering=False)
    NT = S // 128
    NQ = S // 512
    TQ = 512

    def din(name, shape):
        return nc.dram_tensor(name, list(shape), F32, kind="ExternalInput").ap()

    x_d = din("x", [NB, S, D])
    mem_d = din("mem", [NB, MEML, D])
    w_in_d = din("w_in", [D, IN_COLS])
    w_kv_d = din("w_mem_kv", [D, 1024])
    w_fo_d = din("w_fox_out", [512, D])
    w_ro_d = din("w_rwkv_out", [512, D])
    w_mo_d = din("w_mem_out", [512, D])
    w_o_d = din("w_o", [D, D])
    w_fg_d = din("w_ffn_gate", [D, DFF])
    w_fu_d = din("w_ffn_up", [D, DFF])
    w_fd_d = din("w_ffn_down", [DFF, D])
    wup_d = din("rwkv_w_up", [64, 512])
    aup_d = din("rwkv_a_up", [64, 512])
    gup_d = din("rwkv_g_up", [128, 512])
    pv_d = din("pv", [128, NPV])
    cst_d = din("cst", [128, NCST])
    gnp_d = din("gnp", [128, 4 * 2 * 64])
    rowp_d = din("rowp", [1, 2 * D])
    out_d = nc.dram_tensor("out", [NB, S, D], F32, kind="ExternalOutput").ap()
    dbg_d = {}
    if dbg:
        for nm, shp in (("d_uT", [128, 8 * S]), ("d_brT", [128, 12 * S]), ("d_h", [S, D])):
            dbg_d[nm] = nc.dram_tensor(nm, shp, F32, kind="ExternalOutput").ap()

    with contextlib.ExitStack() as st:
        P = Prog(nc, st)
        add = P.add

        cst = P.sb("cst_sb", [128, NCST], F32)
        cstb = P.sb("cstb", [128, NCST], BF16)
        pv = P.sb("pv_sb", [128, NPV], F32)
        pvx = P.sb("pvx", [128, 16], F32)
        u_cst = P.unit("cst")
        u_pv = P.unit("pv")
        UT_B, BR_B, W_B, SCR_B = 32768, 49152, 49152, 57344
        AR_B = UT_B + BR_B + W_B + SCR_B
        ar = P.sb("arena", [128, AR_B // 4], F32)
        OFF_UT, OFF_BR, OFF_W, OFF_SCR = 0, UT_B, UT_B + BR_B, UT_B + BR_B + W_B

        def view(off, nbytes, dtype, pat=None, **kw):
            assert off % 4 == 0 and nbytes % 4 == 0
            a = ar[:, off // 4:(off + nbytes) // 4]
            if dtype == BF16:
                a = a.bitcast(BF16)
            if pat:
                a = a.rearrange(pat, **kw)
            return a

        class Carver:
            def __init__(self, off, size):
                self.off = off
                self.end = off + size

            def take(self, nbytes, dtype, pat=None, **kw):
                nbytes = (nbytes + 3) // 4 * 4
                assert self.off + nbytes <= self.end, ("carver overflow", self.off + nbytes - self.end)
                v = view(self.off, nbytes, dtype, pat, **kw)
                self.off += nbytes
                return v

        SMAX = 2048
        uT = view(OFF_UT, 8 * SMAX * 2, BF16, "p (k t) -> p k t", k=8)[:, :, 0:S]
        brT = view(OFF_BR, 12 * SMAX * 2, BF16, "p (k t) -> p k t", k=12)[:, :, 0:S]
        BR_FOX, BR_MEM, BR_RWK = 0, 4, 8
        u_uT = P.unit("uT")
        u_br = [P.unit(f"brT{i}") for i in range(3)]

        banks = [P.ps(f"pb{i}", [128, 512], F32) for i in range(8)]
        ub = [P.unit(f"pb{i}", psum=True) for i in range(8)]
        bar_scr = P.sb("barscr", [128, 2], F32)

        class Rot:
            def __init__(self, ids):
                self.ids = list(ids)
                self.i = 0

            def nxt(self):
                b = self.ids[self.i % len(self.ids)]
                self.i += 1
                return b

        identb = cstb[:, CST_IDENT:CST_IDENT + 128]
        identf = cst[:, CST_IDENT:CST_IDENT + 128]

        add(SP, lambda e: e.dma_start(out=cst[:], in_=cst_d[:, :]), writes=[u_cst], dma=u_cst)
        add(SP, lambda e: e.dma_start(out=pv[:], in_=pv_d[:, :]), writes=[u_pv], dma=u_pv)
        add(DVE, lambda e: e.tensor_copy(out=cstb[:], in_=cst[:]), reads=[u_cst], writes=[u_cst])
        add(DVE, lambda e: e.tensor_scalar(out=pvx[:, 0:4], in0=pv[:, PV_KA:PV_KA + 4], scalar1=-1.0, scalar2=1.0,
                                           op0=ALU.mult, op1=ALU.add), reads=[u_pv], writes=[u_pv])
        add(DVE, lambda e: e.tensor_scalar(out=pvx[:, 4:5], in0=pv[:, PV_FB:PV_FB + 1], scalar1=-1.0, scalar2=None,
                                           op0=ALU.mult), reads=[u_pv], writes=[u_pv])
        add(DVE, lambda e: e.tensor_scalar(out=pvx[:, 8:12], in0=pv[:, PV_W0:PV_W0 + 4], scalar1=0.5, scalar2=None,
                                           op0=ALU.mult), reads=[u_pv], writes=[u_pv])
        add(DVE, lambda e: e.tensor_scalar(out=pvx[:, 12:16], in0=pv[:, PV_A0:PV_A0 + 4], scalar1=0.5, scalar2=None,
                                           op0=ALU.mult), reads=[u_pv], writes=[u_pv])

        def wload(dst, src, unit):
            add(POOL, lambda e: e.dma_start(out=dst, in_=src), writes=[unit], dma=unit)

        def mk_scr(name, ncols):
            return nc.dram_tensor(name, [128, ncols], BF16).ap(), P.unit("S" + name, nobar=True)

        S_rw, uS_rw = mk_scr("s_rw", 8 * RW_COLS)
        S_fx, uS_fx = mk_scr("s_fx", 8 * FOX_COLS)
        S_mq, uS_mq = mk_scr("s_mq", 8 * 512)
        S_kv, uS_kv = mk_scr("s_kv", 8 * 1024)
        S_bo, uS_bo = mk_scr("s_bo", 12 * 1024)
        S_gt, uS_gt = mk_scr("s_gt", 8 * 8 * 3 * 128)
        S_o, uS_o = mk_scr("s_o", 8 * 1024)
        S_dn, uS_dn = mk_scr("s_dn", NFC * 1024)
        S_ff, uS_ff = mk_scr("s_ff", NFC * 8 * 2 * 128)
        stg = [P.sb(f"stg{i}", [128, 4096], BF16) for i in range(2)]
        u_stg = [P.unit(f"stg{i}", nobar=True) for i in range(2)]
        stg_n = [0]

        def stage(scr, u_scr, col0, ncols, parts):
            sidx = stg_n[0] % 2
            stg_n[0] += 1
            for (dv, src) in parts:
                add(POOL, lambda e: e.dma_start(out=dv(stg[sidx]), in_=src), writes=[u_stg[sidx]], dma=u_stg[sidx])
            add(SP, lambda e: e.dma_start(out=scr[:, col0:col0 + ncols], in_=stg[sidx][:, 0:ncols]), reads=[u_stg[sidx]], writes=[u_scr], dma=u_scr)

        def kp(ap_):
            return ap_.rearrange("(k p) n -> p k n", p=128)

        def stage_first():
            for k0 in range(0, 8, 2):
                stage(S_rw, uS_rw, k0 * RW_COLS, 2 * RW_COLS,
                      [(lambda t: t[:, 0:2 * RW_COLS].rearrange("p (k n) -> p k n", k=2), kp(w_in_d[k0 * 128:(k0 + 2) * 128, RW0:RW0 + RW_COLS]))])

        def stage_rest():
            for k0 in range(0, 8, 2):
                stage(S_fx, uS_fx, k0 * FOX_COLS, 2 * FOX_COLS,
                      [(lambda t: t[:, 0:2 * FOX_COLS].rearrange("p (k n) -> p k n", k=2), kp(w_in_d[k0 * 128:(k0 + 2) * 128, 0:FOX_COLS]))])
            stage(S_mq, uS_mq, 0, 4096, [(lambda t: t[:, 0:4096].rearrange("p (k n) -> p k n", k=8), kp(w_in_d[:, MQ0:MQ0 + 512]))])
            for k0 in range(0, 8, 4):
                stage(S_kv, uS_kv, k0 * 1024, 4096, [(lambda t: t[:, 0:4096].rearrange("p (k n) -> p k n", k=4), kp(w_kv_d[k0 * 128:(k0 + 4) * 128, :]))])
            for r, wd in enumerate((w_fo_d, w_ro_d, w_mo_d)):
                stage(S_bo, uS_bo, r * 4096, 4096, [(lambda t: t[:, 0:4096].rearrange("p (k n) -> p k n", k=4), kp(wd[:, :]))])
            for oc in range(8):
                parts = []
                for r in range(3):
                    gc0 = GT0 + r * 1024 + oc * 128
                    parts.append((lambda t, r=r: t[:, 0:3072].rearrange("p (k r n) -> p k r n", k=8, r=3)[:, :, r, :], kp(w_in_d[:, gc0:gc0 + 128])))
                stage(S_gt, uS_gt, oc * 3072, 3072, parts)
            for k0 in range(0, 8, 4):
                stage(S_o, uS_o, k0 * 1024, 4096, [(lambda t: t[:, 0:4096].rearrange("p (k n) -> p k n", k=4), kp(w_o_d[k0 * 128:(k0 + 4) * 128, :]))])
            for f0 in range(0, NFC, 4):
                nf = min(4, NFC - f0)
                stage(S_dn, uS_dn, f0 * 1024, nf * 1024,
                      [(lambda t, nf=nf: t[:, 0:nf * 1024].rearrange("p (k n) -> p k n", k=nf), kp(w_fd_d[f0 * 128:(f0 + nf) * 128, :]))])
            for f0 in range(0, NFC, 2):
                parts = []
                for ff in range(2):
                    for r, wd in enumerate((w_fg_d, w_fu_d)):
                        parts.append((lambda t, ff=ff, r=r: t[:, 0:4096].rearrange("p (f k r n) -> p f k r n", f=2, k=8, r=2)[:, ff, :, r, :],
                                      kp(wd[:, (f0 + ff) * 128:(f0 + ff + 1) * 128])))
                stage(S_ff, uS_ff, f0 * 2048, 4096, parts)

        def sload(dst_flat, scr, col0, ncols, u_dst, u_scr):
            add(SP, lambda e: e.dma_start(out=dst_flat, in_=scr[:, col0:col0 + ncols]), reads=[u_scr], writes=[u_dst], dma=u_dst)

        stage_first()

        def rms_T(cv, src_rows, ntiles, gcol, dstT, u_dst, bank_ids, tag):
            xin = [cv.take(4096, F32) for _ in range(2)]
            xn = [cv.take(2048, BF16) for _ in range(2)]
            junk = cv.take(2048, BF16)
            stt = [cv.take(16, F32) for _ in range(2)]
            u_x = [P.unit(f"{tag}x{i}") for i in range(2)]
            u_xn = [P.unit(f"{tag}xn{i}") for i in range(2)]
            u_s = [P.unit(f"{tag}s{i}") for i in range(2)]
            u_j = P.unit(f"{tag}j")
            for tt in range(ntiles):
                s = tt % 2
                bk = bank_ids[tt % len(bank_ids)]
                add(SP, lambda e, s=s, tt=tt: e.dma_start(out=xin[s], in_=src_rows(tt)), writes=[u_x[s]], dma=u_x[s])
                add(ACT, lambda e, s=s: e.activation(out=junk, in_=xin[s], func=AF.Square, accum_out=stt[s][:, 0:1]),
                    reads=[u_x[s]], writes=[u_j, u_s[s]])
                add(ACT, lambda e, s=s: e.activation(out=stt[s][:, 1:2], in_=stt[s][:, 0:1], func=AF.Sqrt,
                                                     scale=1.0 / D, bias=NORM_EPS), reads=[u_s[s]], writes=[u_s[s]])
                add(DVE, lambda e, s=s: e.reciprocal(out=stt[s][:, 2:3], in_=stt[s][:, 1:2]), reads=[u_s[s]], writes=[u_s[s]])
                add(DVE, lambda e, s=s: e.tensor_scalar(out=xn[s], in0=xin[s], scalar1=stt[s][:, 2:3], scalar2=None,
                                                        op0=ALU.mult), reads=[u_x[s], u_s[s]], writes=[u_xn[s]])
                pbf = banks[bk].bitcast(BF16)
                for c in range(8):
                    add(PE, lambda e, c=c, s=s, pbf=pbf: e.transpose(pbf[:, c * 128:(c + 1) * 128], xn[s][:, c * 128:(c + 1) * 128], identb),
                        reads=[u_xn[s], u_cst], writes=[ub[bk]])
                add(DVE, lambda e, tt=tt, pbf=pbf: e.tensor_tensor(
                    out=dstT[:, :, tt * 128:(tt + 1) * 128],
                    in0=pbf[:, 0:1024].rearrange("p (k t) -> p k t", k=8),
                    in1=pv[:, gcol:gcol + 8].unsqueeze(2).broadcast_to([128, 8, 128]), op=ALU.mult),
                    reads=[ub[bk], u_pv], writes=[u_dst])

        def proj_fm(w_tile, u_w, col0, rhs_fn, u_rhs, ntok, bk, nk=8, m=128):
            for k in range(nk):
                add(PE, lambda e, k=k: e.matmul(banks[bk][0:m, 0:ntok], lhsT=w_tile[:, k, col0:col0 + m], rhs=rhs_fn(k),
                                                start=(k == 0), stop=(k == nk - 1)),
                    reads=[u_w] + list(u_rhs), writes=[ub[bk]])

        for b in range(NB):
            P.barrier(bar_scr[:, 0:1])
            cv = Carver(OFF_SCR, SCR_B)
            rms_T(cv, lambda tt: x_d[b, tt * 128:(tt + 1) * 128, :], NT, PV_PRE1, uT, u_uT, [0, 1], f"A{b}")
            if dbg and b == 0:
                P.barrier(bar_scr[:, 0:1])
                cvd = Carver(OFF_SCR, SCR_B)
                dtmp = cvd.take(S * 4, F32)
                u_d = P.unit("dbgA")
                for kk_ in range(8):
                    add(DVE, lambda e: e.tensor_copy(out=dtmp, in_=uT[:, kk_, :]), reads=[u_uT], writes=[u_d])
                    add(SP, lambda e: e.dma_start(out=dbg_d["d_uT"][:, kk_ * S:(kk_ + 1) * S], in_=dtmp), reads=[u_d], dma=u_d, is_out=True)

            P.barrier(bar_scr[:, 0:1])
            cw = Carver(OFF_W, W_B)
            w_rw = cw.take(8 * RW_COLS * 2, BF16, "p (k n) -> p k n", k=8)
            lu = cw.take(3 * 512 * 2, BF16, "p (k n) -> p k n", k=3)
            u_wrw = P.unit(f"wrw{b}")
            u_lu = P.unit(f"lu{b}")
            rkv2 = [cw.take(TQ * 4, F32) for _ in range(3)]
            u_rkv2 = [P.unit(f"rkv2{b}{i}") for i in range(3)]
            bonp = [None, cw.take(TQ * 4, F32)]
            gbp = [None, cw.take(TQ * 2, BF16)]
            gcxp = [None, None]
            u_bonp = [None, P.unit(f"bon1{b}")]
            u_gbp = [None, P.unit(f"gb1{b}")]
            u_gcp = [None, P.unit(f"gc1{b}")]
            H2d = cw.take(8 * 128 * 2, BF16, "p (c n) -> p c n", c=8)
            YMd = cw.take(8 * 128 * 2, BF16, "p (c n) -> p c n", c=8)
            uH2d = [P.unit(f"h2d{b}{g}") for g in range(2)]
            uYMd = [P.unit(f"ymd{b}{g}") for g in range(2)]
            YAd = cw.take(TQ * 4, F32)
            u_YAd = P.unit(f"yad{b}")
            stage2_q = []
            bgq = []
            pend_gn = [None]

            def tick(n=1):
                P.run_deferred(bgq, n)

            def pump(n=1):
                for _ in range(n):
                    if stage2_q:
                        stage2_q.pop(0)()
            sload(w_rw.rearrange("p k n -> p (k n)"), S_rw, 0, 8 * RW_COLS, u_wrw, uS_rw)
            wload(lu[0:64, 0, :], wup_d[:, :], u_lu)
            wload(lu[64:128, 1, :], aup_d[:, :], u_lu)
            wload(lu[:, 2, :], gup_d[:, :], u_lu)

            cv = Carver(OFF_SCR, SCR_B)
            cv2 = Carver(OFF_BR, 8 * SMAX * 2)
            l12 = cv2.take(S * 2, BF16)
            l13 = cv2.take(S * 2, BF16)
            u_l = P.unit(f"l{b}")
            gnt = cv.take(4 * 2 * 64 * 4, F32, "p (a c i) -> p a c i", a=4, c=2)
            u_gn = P.unit(f"gn{b}")
            add(SP, lambda e: e.dma_start(out=gnt.rearrange("p a c i -> p (a c i)"), in_=gnp_d[:, :]), writes=[u_gn], dma=u_gn)
            if b == 0:
                stage_rest()
            rmask = cv.take(TQ * 4, F32)
            u_rm = P.unit(f"rm{b}")
            add(DVE, lambda e: e.memset(rmask, 1.0), writes=[u_rm])
            add(DVE, lambda e: e.memset(rmask.rearrange("p (c t) -> p c t", t=64)[:, :, 0:1], 0.0), writes=[u_rm])
            praw = [cv.take((TQ + 2) * 4, F32) for _ in range(3)]
            u_praw = [P.unit(f"praw{b}{i}") for i in range(3)]
            NSC = 12
            sc = [cv.take(TQ * 4, F32) for _ in range(NSC)]
            u_sc = [P.unit(f"sc{b}{i}") for i in range(NSC)]
            scb = [cv.take(TQ * 2, BF16) for _ in range(3)]
            u_scb = [P.unit(f"scb{b}{i}") for i in range(3)]
            NBD = 7
            bd = [cv2.take(8 * 128 * 2, BF16, "p (c n) -> p c n", c=8) for _ in range(NBD)]
            u_bd = [P.unit(f"bd{b}{i}") for i in range(NBD)]
            for i in range(NBD):
                add(DVE, lambda e, i=i: e.memset(bd[i].rearrange("p c n -> p (c n)"), 0.0), writes=[u_bd[i]])
            ybd = cv2.take(8 * 128 * 2, BF16, "p (c n) -> p c n", c=8)
            u_ybd = P.unit(f"ybd{b}")
            add(DVE, lambda e: e.memset(ybd.rearrange("p c n -> p (c n)"), 0.0), writes=[u_ybd])
            gcx = cv.take(8 * 4, F32)
            u_gc = P.unit(f"gc{b}")
            TMd = cv.take(TQ * 4, F32)
            u_TMd = P.unit(f"tmd{b}")
            gcxp[1] = cv.take(8 * 4, F32)
            bonp[0], gbp[0], gcxp[0] = sc[11], scb[0], gcx
            u_bonp[0], u_gbp[0], u_gcp[0] = u_sc[11], u_scb[0], u_gc
            def carr(cvx):
                return cvx.take(8 * 128 * 2, BF16, "p (c n) -> p c n", c=8)
            Wt, Lt, Xt, LAKt, PRBt, PRKt, ATMt, BHTt = [carr(cv) for _ in range(8)]
            KHTt, YVt = carr(cv2), carr(cv2)
            VSt = cv2.take(8 * 64 * 2, BF16, "p (c i) -> p c i", c=8)
            uW, uL, uX, uLAK, uPRB, uPRK, uATM, uBHT, uKHT, uYV = [[P.unit(f"ca{b}{n_}{g}") for g in range(2)] for n_ in range(10)]
            u_VS = P.unit(f"VS{b}")
            Mf = cv.take(64 * 4, F32)
            Mb = [cv.take(64 * 2, BF16) for _ in range(2)]
            u_M = P.unit(f"M{b}")
            u_Mb = [P.unit(f"Mb{b}{i}") for i in range(2)]
            gst = cv.take(8 * 8 * 4, F32, "p (a c) -> p a c", a=8)
            u_gst = P.unit(f"gst{b}")

            rp = Rot([0, 1])
            rs = Rot([3, 4, 5, 6, 7])
            rsY = Rot([2])
            rsM = Rot([4, 5, 6, 7])

            def proj_lerp(cc, tq, pslot, dst, u_dst):
                bk = rp.nxt()
                proj_fm(w_rw, u_wrw, cc * 128, lambda k: uT[:, k, tq * TQ:(tq + 1) * TQ], [u_uT], TQ, bk)
                pr = praw[pslot]
                up = u_praw[pslot]
                if tq == 0:
                    add(DVE, lambda e: e.memset(pr[:, 0:1], 0.0), writes=[up])
                else:
                    add(DVE, lambda e: e.tensor_copy(out=pr[:, 0:1], in_=pr[:, TQ:TQ + 1]), reads=[up], writes=[up])
                add(ACT, lambda e: e.copy(out=pr[:, 1:TQ + 1], in_=banks[bk][:, 0:TQ]), reads=[ub[bk]], writes=[up])
                add(DVE, lambda e: e.tensor_tensor(out=dst, in0=pr[:, 0:TQ], in1=pr[:, 1:TQ + 1], op=ALU.subtract),
                    reads=[up], writes=[u_dst])
                add(DVE, lambda e: e.scalar_tensor_tensor(out=dst, in0=dst, scalar=pv[:, PV_MU + cc:PV_MU + cc + 1],
                                                          in1=pr[:, 1:TQ + 1], op0=ALU.mult, op1=ALU.add),
                    reads=[up, u_dst, u_pv], writes=[u_dst])

            for tq in range(NQ):
                tsl = slice(tq * TQ, (tq + 1) * TQ)
                proj_lerp(12, tq, 0, sc[0], u_sc[0])
                add(ACT, lambda e, tsl=tsl: e.activation(out=l12[0:64, tsl], in_=sc[0][0:64, :], func=AF.Tanh),
                    reads=[u_sc[0]], writes=[u_l])
                add(ACT, lambda e, tsl=tsl: e.copy(out=l12[64:128, tsl], in_=sc[0][64:128, :]), reads=[u_sc[0]], writes=[u_l])
                proj_lerp(13, tq, 1, sc[1], u_sc[1])
                add(ACT, lambda e, tsl=tsl: e.activation(out=l13[:, tsl], in_=sc[1], func=AF.Sigmoid),
                    reads=[u_sc[1]], writes=[u_l])

            msu = cst[:, CST_SU:CST_SU + 128]
            msl = cst[:, CST_SL:CST_SL + 128]
            mui = cst[:, CST_UI:CST_UI + 128]
            bones = cstb[:, CST_BONES:CST_BONES + 128]
            stk = cstb[:, CST_STK:CST_STK + 64]

            for hp in range(4):
                for tq in range(NQ):
                    tsl = slice(tq * TQ, (tq + 1) * TQ)
                    tidx = hp * NQ + tq
                    _, _, _, SG, A_, KK, T1, BV, CS, CP, DE, BON = sc
                    _, _, _, uSG, uA, uKK, uT1, uBV, uCS, uCP, uDE, uBON = u_sc

                    def rkv_bufs(ti):
                        if ti % 2 == 0:
                            return (sc[0], sc[1], sc[2]), (u_sc[0], u_sc[1], u_sc[2])
                        return tuple(rkv2), tuple(u_rkv2)

                    (R_, K_, V_), (uR, uK, uV) = rkv_bufs(tidx)

                    def emit_proj(ti, which):
                        hp_, tq_ = ti // NQ, ti % NQ
                        bufs, us = rkv_bufs(ti)
                        proj_lerp(which * 4 + hp_, tq_, which, bufs[which], us[which])

                    if tidx == 0:
                        for w_ in range(3):
                            emit_proj(0, w_)
                    hs = slice(hp * 128, (hp + 1) * 128)
                    _, SQb, RKb = scb
                    _, uSQ, uRK = u_scb
                    par = tidx % 2
                    BON, uBON, Gb, uG, gcx, u_gc = bonp[par], u_bonp[par], gbp[par], u_gbp[par], gcxp[par], u_gcp[par]
                    pump()
                    def prep(ti):
                        hp_, tq_ = ti // NQ, ti % NQ
                        tsl_ = slice(tq_ * TQ, (tq_ + 1) * TQ)
                        hs_ = slice(hp_ * 128, (hp_ + 1) * 128)
                        pr_ = ti % 2
                        (R_, K_, V_), (uR, uK, uV) = rkv_bufs(ti)
                        BON, uBON, Gb, uG, gcx, u_gc = bonp[pr_], u_bonp[pr_], gbp[pr_], u_gbp[pr_], gcxp[pr_], u_gcp[pr_]
                        add(DVE, lambda e: e.tensor_scalar(out=KK, in0=K_, scalar1=pv[:, PV_KK + hp_:PV_KK + hp_ + 1], scalar2=None, op0=ALU.mult),
                            reads=[uK, u_pv], writes=[uKK])
                        add(DVE, lambda e: e.tensor_tensor(out=SQb, in0=KK, in1=KK, op=ALU.mult), reads=[uKK], writes=[uSQ])
                        P.atomic_begin()
                        bs = rp.nxt()
                        add(PE, lambda e, bs=bs: e.matmul(banks[bs][:, 0:TQ], lhsT=bones, rhs=SQb, start=True, stop=True),
                            reads=[u_cst, uSQ], writes=[ub[bs]])
                        add(ACT, lambda e, bs=bs: e.activation(out=T1, in_=banks[bs][:, 0:TQ], func=AF.Sqrt), reads=[ub[bs]], writes=[uT1])
                        P.atomic_end()
                        for (li_, rows_, rhs_, is_g) in ((0, slice(0, 64), l12, False), (1, slice(64, 128), l12, False), (2, slice(0, 128), l13, True)):
                            P.atomic_begin()
                            bq = rp.nxt()
                            add(PE, lambda e: e.matmul(banks[bq][:, 0:TQ], lhsT=lu[rows_, li_, hs_], rhs=rhs_[rows_, tsl_], start=True, stop=True),
                                reads=[u_lu, u_l], writes=[ub[bq]])
                            if li_ == 0:
                                add(ACT, lambda e: e.activation(out=SG, in_=banks[bq][:, 0:TQ], func=AF.Tanh, scale=0.5, bias=pvx[:, 8 + hp_:9 + hp_]),
                                    reads=[ub[bq], u_pv], writes=[uSG])
                            elif li_ == 1:
                                add(ACT, lambda e: e.activation(out=A_, in_=banks[bq][:, 0:TQ], func=AF.Tanh, scale=0.5, bias=pvx[:, 12 + hp_:13 + hp_]),
                                    reads=[ub[bq], u_pv], writes=[uA])
                            else:
                                add(ACT, lambda e: e.copy(out=Gb, in_=banks[bq][:, 0:TQ]), reads=[ub[bq]], writes=[uG])
                            P.atomic_end()
                        add(DVE, lambda e: e.tensor_scalar(out=SG, in0=SG, scalar1=0.5, scalar2=0.5, op0=ALU.mult, op1=ALU.add), reads=[uSG], writes=[uSG])
                        add(DVE, lambda e: e.tensor_scalar(out=A_, in0=A_, scalar1=0.5, scalar2=0.5, op0=ALU.mult, op1=ALU.add), reads=[uA], writes=[uA])
                        add(DVE, lambda e: e.tensor_scalar(out=T1, in0=T1, scalar1=1e-12, scalar2=None, op0=ALU.max), reads=[uT1], writes=[uT1])
                        add(DVE, lambda e: e.reciprocal(out=T1, in_=T1), reads=[uT1], writes=[uT1])
                        add(DVE, lambda e: e.tensor_tensor(out=KK, in0=KK, in1=T1, op=ALU.mult), reads=[uKK, uT1], writes=[uKK])
                        add(DVE, lambda e: e.tensor_scalar(out=T1, in0=A_, scalar1=pv[:, PV_KA + hp_:PV_KA + hp_ + 1], scalar2=pvx[:, hp_:hp_ + 1],
                                                           op0=ALU.mult, op1=ALU.add), reads=[uA, u_pv, uT1], writes=[uT1])
                        add(DVE, lambda e: e.tensor_tensor(out=K_, in0=K_, in1=T1, op=ALU.mult), reads=[uK, uT1], writes=[uK])
                        add(DVE, lambda e: e.tensor_tensor(out=BV, in0=KK, in1=A_, op=ALU.mult), reads=[uKK, uA], writes=[uBV])
                        add(DVE, lambda e: e.scalar_tensor_tensor(out=RKb, in0=R_, scalar=pv[:, PV_RK + hp_:PV_RK + hp_ + 1], in1=K_,
                                                                  op0=ALU.mult, op1=ALU.mult), reads=[uR, uK, u_pv], writes=[uRK])
                        P.atomic_begin()
                        bb = rp.nxt()
                        add(PE, lambda e, bb=bb: e.matmul(banks[bb][:, 0:TQ], lhsT=bones, rhs=RKb, start=True, stop=True),
                            reads=[u_cst, uRK], writes=[ub[bb]])
                        add(DVE, lambda e, bb=bb: e.tensor_tensor(out=BON, in0=banks[bb][:, 0:TQ], in1=V_, op=ALU.mult),
                            reads=[ub[bb], uV], writes=[uBON])
                        P.atomic_end()
                        add(DVE, lambda e: e.tensor_tensor_scan(out=CS, data0=rmask, data1=SG, initial=0.0, op0=ALU.mult, op1=ALU.add),
                            reads=[u_rm, uSG], writes=[uCS])
                        add(DVE, lambda e: e.tensor_tensor(out=CP, in0=CS, in1=SG, op=ALU.subtract), reads=[uCS, uSG], writes=[uCP])
                        CS3 = CS.rearrange("p (c t) -> p c t", t=64)
                        add(DVE, lambda e: e.tensor_tensor(out=DE.rearrange("p (c t) -> p c t", t=64),
                                                           in0=CS3[:, :, 63:64].broadcast_to([128, 8, 64]), in1=CS3, op=ALU.subtract),
                            reads=[uCS], writes=[uDE])
                        add(ACT, lambda e: e.activation(out=gcx.unsqueeze(2), in_=CS3[:, :, 63:64], func=AF.Exp, scale=-C0),
                            reads=[uCS], writes=[u_gc])
                        add(ACT, lambda e: e.activation(out=SG, in_=CS, func=AF.Exp, scale=-C0), reads=[uCS, uCP], writes=[uSG])
                        add(ACT, lambda e: e.activation(out=T1, in_=CS, func=AF.Exp, scale=C0), reads=[uCS, uK], writes=[uT1])
                        add(ACT, lambda e: e.activation(out=CP, in_=CP, func=AF.Exp, scale=-C0), reads=[uCP], writes=[uCP])
                        add(ACT, lambda e: e.activation(out=DE, in_=DE, func=AF.Exp, scale=-C0), reads=[uDE], writes=[uDE])

                    if tidx == 0:
                        prep(0)
                    EP, EN, EPP, EE = SG, T1, CP, DE
                    uEP, uEN, uEPP, uEE = uSG, uT1, uCP, uDE
                    bdR, bdK, bdB, bdA, bdBH, bdKH, bdV = bd
                    uBR, uBK, uBB, uBA, uBBH, uBKH, uBVv = u_bd
                    for hh in range(2):
                        pump(2)
                        ps_ = slice(hh * 64, hh * 64 + 64)

                        def bdo(t):
                            return t[ps_, :, hh * 64:hh * 64 + 64]

                        def src(t):
                            return t[ps_, :].rearrange("p (c t) -> p c t", t=64)

                        add(DVE, lambda e, bdo=bdo, src=src: e.tensor_tensor(out=bdo(bdR), in0=src(R_), in1=src(EP), op=ALU.mult),
                            reads=[uR, uEP], writes=[uBR])
                        add(DVE, lambda e, bdo=bdo, src=src: e.tensor_tensor(out=bdo(bdK), in0=src(K_), in1=src(EN), op=ALU.mult),
                            reads=[uK, uEN], writes=[uBK])
                        add(DVE, lambda e, bdo=bdo, src=src: e.tensor_tensor(out=bdo(bdB), in0=src(BV), in1=src(EN), op=ALU.mult),
                            reads=[uBV, uEN], writes=[uBB])
                        add(DVE, lambda e, bdo=bdo, src=src: e.scalar_tensor_tensor(out=bdo(bdA), in0=src(KK), scalar=-1.0, in1=src(EPP),
                                                                                    op0=ALU.mult, op1=ALU.mult),
                            reads=[uKK, uEPP], writes=[uBA])
                        add(DVE, lambda e, bdo=bdo, src=src: e.tensor_tensor(out=bdo(bdBH), in0=src(BV), in1=src(EE), op=ALU.mult),
                            reads=[uBV, uEE], writes=[uBBH])
                        add(DVE, lambda e, bdo=bdo, src=src: e.tensor_tensor(out=bdo(bdKH), in0=src(K_), in1=src(EE), op=ALU.mult),
                            reads=[uK, uEE], writes=[uBKH])
                        add(ACT, lambda e, bdo=bdo, src=src: e.copy(out=bdo(bdV), in_=src(V_)), reads=[uV], writes=[uBVv])

                    def b4(bk):
                        return banks[bk][:, 0:512].rearrange("p (c n) -> p c n", c=4)

                    def g4(t, g):
                        return t[:, g * 4:(g + 1) * 4, :]

                    def mm4(bk, g, lt, ult, rt, urt, plus=None):
                        for cc in range(4):
                            c = g * 4 + cc
                            if plus is not None:
                                add(PE, lambda e: e.matmul(banks[bk][:, cc * 128:(cc + 1) * 128], lhsT=identb, rhs=plus[0][:, c, :], start=True, stop=False),
                                    reads=[u_cst, plus[1]], writes=[ub[bk]])
                            add(PE, lambda e: e.matmul(banks[bk][:, cc * 128:(cc + 1) * 128], lhsT=lt[:, c, :], rhs=rt[:, c, :],
                                                       start=(plus is None), stop=True),
                                reads=[ult, urt], writes=[ub[bk]])

                    evn = [0]

                    def evcopy(dst, bk, udst):
                        evn[0] += 1
                        if evn[0] % 2 == 0:
                            add(ACT, lambda e: e.copy(out=dst, in_=b4(bk)), reads=[ub[bk]], writes=[udst])
                        else:
                            add(DVE, lambda e: e.tensor_copy(out=dst, in_=b4(bk)), reads=[ub[bk]], writes=[udst])
                        tick()

                    def score(dst, udst, lt, ult, rt, urt, mask):
                        for g in range(2):
                            bk = rs.nxt()
                            mm4(bk, g, lt, ult, rt, urt)
                            add(DVE, lambda e: e.tensor_tensor(out=g4(dst, g), in0=b4(bk), in1=mask.unsqueeze(1).broadcast_to([128, 4, 128]), op=ALU.mult),
                                reads=[ub[bk], u_cst], writes=[udst[g]])

                    score(Wt, uW, bdB, uBB, bdA, uBA, msu)
                    pump()
                    score(Lt, uL, bdA, uBA, bdB, uBB, msl)
                    pump()
                    for g in range(2):
                        add(DVE, lambda e: e.tensor_tensor(out=g4(Xt, g), in0=g4(Wt, g), in1=identb.unsqueeze(1).broadcast_to([128, 4, 128]), op=ALU.add),
                            reads=[uW[g], u_cst], writes=[uX[g]])
                    score(LAKt, uLAK, bdA, uBA, bdK, uBK, msl)
                    pump()
                    score(PRBt, uPRB, bdB, uBB, bdR, uBR, mui)
                    pump()
                    score(PRKt, uPRK, bdK, uBK, bdR, uBR, mui)
                    pump(99)
                    if pend_gn[0] is not None:
                        P.defer = bgq
                        pend_gn[0]()
                        P.defer = None
                        pend_gn[0] = None
                    if tidx + 1 < 4 * NQ:
                        for w_ in range(3):
                            emit_proj(tidx + 1, w_)
                        P.defer = bgq
                        prep(tidx + 1)
                        P.defer = None
                    for lev in range(5):
                        bl = [rs.nxt(), rs.nxt()]
                        bw2 = [rs.nxt(), rs.nxt()] if lev < 4 else None
                        for g in range(2):
                            mm4(bl[g], g, Wt, uW[g], Lt, uL[g])
                            if lev < 4:
                                mm4(bw2[g], g, Lt, uL[g], Wt, uW[g])
                        for g in range(2):
                            add(ACT, lambda e: e.copy(out=g4(Lt, g), in_=b4(bl[g])), reads=[ub[bl[g]]], writes=[uL[g]])
                            tick()
                            if lev < 4:
                                if (lev + g) % 2 == 0:
                                    add(ACT, lambda e: e.copy(out=g4(Wt, g), in_=b4(bw2[g])), reads=[ub[bw2[g]]], writes=[uW[g]])
                                else:
                                    add(DVE, lambda e: e.tensor_copy(out=g4(Wt, g), in_=b4(bw2[g])), reads=[ub[bw2[g]]], writes=[uW[g]])
                        bx = [rs.nxt(), rs.nxt()]
                        for g in range(2):
                            mm4(bx[g], g, Lt, uL[g], Xt, uX[g], plus=(Xt, uX[g]))
                        for g in range(2):
                            evcopy(g4(Xt, g), bx[g], uX[g])
                    for (srcbd, usrc, dstt, udst) in ((bdA, uBA, ATMt, uATM), (bdBH, uBBH, BHTt, uBHT), (bdKH, uBKH, KHTt, uKHT)):
                        bk = rs.nxt()
                        pbf = banks[bk].bitcast(BF16)
                        for c in range(8):
                            add(PE, lambda e: e.transpose(pbf[:, c * 128:(c + 1) * 128], srcbd[:, c, :], identb), reads=[usrc, u_cst], writes=[ub[bk]])
                        add(ACT, lambda e: e.copy(out=dstt.rearrange("p c n -> p (c n)"), in_=pbf[:, 0:1024]), reads=[ub[bk]], writes=udst)
                        tick()
                    bk = rs.nxt()
                    for c in range(8):
                        add(PE, lambda e: e.matmul(banks[bk][:, c * 64:(c + 1) * 64], lhsT=bdV[:, c, :], rhs=stk, start=True, stop=True),
                            reads=[uBVv, u_cst], writes=[ub[bk]])
                    add(ACT, lambda e: e.copy(out=VSt.rearrange("p c i -> p (c i)"), in_=banks[bk][:, 0:512]), reads=[ub[bk]], writes=[u_VS])
                    bta = [rs.nxt(), rs.nxt()]
                    btk = [rs.nxt(), rs.nxt()]
                    for g in range(2):
                        mm4(bta[g], g, Xt, uX[g], ATMt, uATM[g])
                        mm4(btk[g], g, Xt, uX[g], LAKt, uLAK[g])
                    for g in range(2):
                        add(ACT, lambda e: e.copy(out=g4(Wt, g), in_=b4(bta[g])), reads=[ub[bta[g]]], writes=[uW[g]])
                        tick()
                        add(ACT, lambda e: e.copy(out=g4(Lt, g), in_=b4(btk[g])), reads=[ub[btk[g]]], writes=[uL[g]])
                        tick()
                    TAt, uTA, TKt, uTK = Wt, uW, Lt, uL
                    for g in range(2):
                        b1, b2, b3, b4_ = rs.nxt(), rs.nxt(), rs.nxt(), rs.nxt()
                        mm4(b1, g, TAt, uTA[g], BHTt, uBHT[g])
                        mm4(b2, g, TKt, uTK[g], BHTt, uBHT[g], plus=(KHTt, uKHT[g]))
                        mm4(b3, g, TAt, uTA[g], PRBt, uPRB[g], plus=(bdR, uBR))
                        mm4(b4_, g, TKt, uTK[g], PRBt, uPRB[g], plus=(PRKt, uPRK[g]))
                        evcopy(g4(ATMt, g), b1, uATM[g])
                        evcopy(g4(H2d, g), b2, uH2d[g])
                        evcopy(g4(YMd, g), b3, uYMd[g])
                        evcopy(g4(YVt, g), b4_, uYV[g])
                    G1t, uG1, H2t, uH2, YMt, uYM = ATMt, uATM, H2d, uH2d, YMd, uYMd
                    tick(10 ** 6)
                    def make_stage2(hp, tq, tsl, YA, uYA, TM, uTM, BON, uBON, Gb, uG, gcx, u_gc, G1t, uG1, H2t, uH2, YMt, uYM):
                        st = {}
                        steps = []

                        def chain_step(c):
                            def f():
                                if c == 0:
                                    st["bkY"] = rsY.nxt()
                                    if tq == 0:
                                        add(DVE, lambda e: e.memset(Mf, 0.0), writes=[u_M])
                                        add(DVE, lambda e: e.memset(Mb[0], 0.0), writes=[u_Mb[0]])
                                bkY = st["bkY"]
                                g = c // 4
                                mo = c % 2
                                bkM = rsM.nxt()
                                add(PE, lambda e: e.matmul(banks[bkM][:, 0:64], lhsT=G1t[:, c, :], rhs=Mb[mo], start=True, stop=False),
                                    reads=[uG1[g], u_Mb[mo]], writes=[ub[bkM]])
                                add(PE, lambda e: e.matmul(banks[bkM][:, 0:64], lhsT=H2t[:, c, :], rhs=VSt[:, c, :], start=False, stop=True),
                                    reads=[uH2[g], u_VS], writes=[ub[bkM]])
                                add(PE, lambda e: e.matmul(banks[bkY][:, c * 64:(c + 1) * 64], lhsT=YMt[:, c, :], rhs=Mb[mo], start=True, stop=False),
                                    reads=[uYM[g], u_Mb[mo]], writes=[ub[bkY]])
                                add(PE, lambda e: e.matmul(banks[bkY][:, c * 64:(c + 1) * 64], lhsT=YVt[:, c, :], rhs=VSt[:, c, :], start=False, stop=True),
                                    reads=[uYV[g], u_VS], writes=[ub[bkY]])
                                add(DVE, lambda e: e.scalar_tensor_tensor(out=Mb[1 - mo], in0=Mf, scalar=gcx[:, c:c + 1], in1=banks[bkM][:, 0:64],
                                                                          op0=ALU.mult, op1=ALU.add),
                                    reads=[u_M, u_gc, ub[bkM]], writes=[u_Mb[1 - mo]])
                                add(DVE, lambda e: e.scalar_tensor_tensor(out=Mf, in0=Mf, scalar=gcx[:, c:c + 1], in1=banks[bkM][:, 0:64],
                                                                          op0=ALU.mult, op1=ALU.add),
                                    reads=[u_M, u_gc, ub[bkM]], writes=[u_M])
                                if c == 7:
                                    add(ACT, lambda e: e.copy(out=YA, in_=banks[bkY][:, 0:512]), reads=[ub[bkY]], writes=[uYA])
                            return f

                        for c_ in range(8):
                            steps.append(chain_step(c_))

                        def gn_final():
                            YA3 = YA.rearrange("p (c i) -> p c i", i=64)
                            add(DVE, lambda e: e.tensor_reduce(out=gst[:, 0, :], in_=YA3, axis=AX.X, op=ALU.add), reads=[uYA], writes=[u_gst])
                            add(ACT, lambda e: e.activation(out=TM, in_=YA, func=AF.Square), reads=[uYA], writes=[uTM])
                            add(DVE, lambda e: e.tensor_reduce(out=gst[:, 1, :], in_=TM.rearrange("p (c i) -> p c i", i=64), axis=AX.X, op=ALU.add),
                                reads=[uTM, u_gst], writes=[u_gst])
                            add(DVE, lambda e: e.tensor_scalar(out=gst[:, 2, :], in0=gst[:, 0, :], scalar1=1.0 / 64, scalar2=None, op0=ALU.mult),
                                reads=[u_gst], writes=[u_gst])
                            add(DVE, lambda e: e.tensor_tensor(out=gst[:, 3, :], in0=gst[:, 2, :], in1=gst[:, 2, :], op=ALU.mult),
                                reads=[u_gst], writes=[u_gst])
                            add(DVE, lambda e: e.scalar_tensor_tensor(out=gst[:, 4, :], in0=gst[:, 1, :], scalar=1.0 / 64, in1=gst[:, 3, :],
                                                                      op0=ALU.mult, op1=ALU.subtract), reads=[u_gst], writes=[u_gst])
                            add(ACT, lambda e: e.activation(out=gst[:, 5, :], in_=gst[:, 4, :], func=AF.Sqrt, bias=GN_EPS), reads=[u_gst], writes=[u_gst])
                            add(DVE, lambda e: e.reciprocal(out=gst[:, 6, :], in_=gst[:, 5, :]), reads=[u_gst], writes=[u_gst])
                            add(DVE, lambda e: e.tensor_tensor(out=YA3, in0=YA3, in1=gst[:, 2, :].unsqueeze(2).broadcast_to([128, 8, 64]), op=ALU.subtract),
                                reads=[uYA, u_gst], writes=[uYA])
                            add(DVE, lambda e: e.tensor_tensor(out=YA3, in0=YA3, in1=gst[:, 6, :].unsqueeze(2).broadcast_to([128, 8, 64]), op=ALU.mult),
                                reads=[uYA, u_gst], writes=[uYA])
                            add(DVE, lambda e: e.tensor_tensor(out=YA3, in0=YA3, in1=gnt[:, hp, 0:1, :].broadcast_to([128, 8, 64]), op=ALU.mult),
                                reads=[uYA, u_gn], writes=[uYA])
                            for hh in range(2):
                                ps_ = slice(hh * 64, hh * 64 + 64)
                                add(DVE, lambda e, ps_=ps_, hh=hh: e.tensor_tensor(out=ybd[ps_, :, hh * 64:hh * 64 + 64], in0=YA3[ps_],
                                                                                   in1=gnt[ps_, hp, 1:2, :].broadcast_to([64, 8, 64]), op=ALU.add),
                                    reads=[uYA, u_gn], writes=[u_ybd])
                            P.atomic_begin()
                            bk = rp.nxt()
                            for c in range(8):
                                add(PE, lambda e, c=c, bk=bk: e.matmul(banks[bk][:, c * 64:(c + 1) * 64], lhsT=ybd[:, c, :], rhs=stk, start=True, stop=True),
                                    reads=[u_ybd, u_cst], writes=[ub[bk]])
                            add(DVE, lambda e, bk=bk: e.tensor_tensor(out=TM, in0=banks[bk][:, 0:TQ], in1=BON, op=ALU.add),
                                reads=[ub[bk], uBON], writes=[uTM])
                            P.atomic_end()
                            add(DVE, lambda e: e.tensor_tensor(out=brT[:, BR_RWK + hp, tsl], in0=TM, in1=Gb, op=ALU.mult),
                                reads=[uTM, uG], writes=[u_br[1]])

                        return steps, gn_final

                    steps_, gnf_ = make_stage2(hp, tq, tsl, YAd, u_YAd, TMd, u_TMd, BON, uBON, Gb, uG, gcx, u_gc, G1t, uG1, H2t, uH2, YMt, uYM)
                    stage2_q.extend(steps_)
                    assert pend_gn[0] is None
                    pend_gn[0] = gnf_

            pump(99)
            if pend_gn[0] is not None:
                pend_gn[0]()
                pend_gn[0] = None
            P.barrier(bar_scr[:, 0:1])
            cw = Carver(OFF_W, W_B)
            w_fx = cw.take(8 * FOX_COLS * 2, BF16, "p (k n) -> p k n", k=8)
            u_wfx = P.unit(f"wfx{b}")
            sload(w_fx.rearrange("p k n -> p (k n)"), S_fx, 0, 8 * FOX_COLS, u_wfx, uS_fx)
            cv = Carver(OFF_SCR, SCR_B)
            qa = [cv.take(S * 2, BF16) for _ in range(2)]
            ka = [cv.take(S * 2, BF16) for _ in range(2)]
            vaug = cv.take(NT * 2 * 65 * 2, BF16, "p (t h d) -> p t h d", t=NT, h=2)
            ftm = cv.take(NT * 128 * 2, BF16, "p (t n) -> p t n", t=NT)
            NPT = 4
            pT = [cv.take(4 * 128 * 2, BF16) for _ in range(NPT - 1)]
            u_qa = [P.unit(f"qa{b}{i}") for i in range(2)]
            u_ka = [P.unit(f"ka{b}{i}") for i in range(2)]
            u_v, u_ftm = P.unit(f"v{b}"), P.unit(f"ftm{b}")
            u_pT = [P.unit(f"pT{b}{i}") for i in range(NPT)]
            EL = cv.take(S * 4, F32)
            CPs = cv.take(S * 4, F32)
            QP = cv.take(3 * S * 2, BF16, "p (r s) -> p r s", r=3)
            cmem = Carver(OFF_BR + 4 * SMAX * 2, 4 * SMAX * 2)
            KP = cmem.take(3 * S * 2, BF16, "p (r s) -> p r s", r=3)
            pT.append(cmem.take(4 * 128 * 2, BF16))
            rden = cv.take(16, F32)
            u_EL, u_CP, u_KP, u_QP, u_rden = (P.unit(f"EL{b}"), P.unit(f"CP{b}"), P.unit(f"KP{b}"), P.unit(f"QP{b}"), P.unit(f"rden{b}"))
            rp = Rot([0, 1])
            for tq in range(NQ):
                tsl = slice(tq * TQ, (tq + 1) * TQ)
                bk = rp.nxt()
                proj_fm(w_fx, u_wfx, 1536, lambda k: uT[:, k, tsl], [u_uT], TQ, bk, m=8)
                add(ACT, lambda e, bk=bk, tsl=tsl: e.activation(out=EL[0:8, tsl], in_=banks[bk][0:8, 0:TQ], func=AF.Exp, scale=-1.0,
                                                                bias=pvx[0:8, 4:5]), reads=[ub[bk], u_pv], writes=[u_EL])
            add(ACT, lambda e: e.activation(out=EL[0:8, :], in_=EL[0:8, :], func=AF.Ln, bias=1.0), reads=[u_EL], writes=[u_EL])
            add(DVE, lambda e: e.tensor_tensor_scan(out=CPs[0:8, :], data0=EL[0:8, :], data1=EL[0:8, :], initial=0.0, op0=ALU.add, op1=ALU.max),
                reads=[u_EL], writes=[u_CP])
            def pieces(dst, u_dst):
                for r in range(3):
                    add(DVE, lambda e: e.tensor_copy(out=dst[0:8, r, :], in_=EL[0:8, :]), reads=[u_EL], writes=[u_dst])
                    if r < 2:
                        add(DVE, lambda e: e.tensor_tensor(out=EL[0:8, :], in0=EL[0:8, :], in1=dst[0:8, r, :], op=ALU.subtract),
                            reads=[u_EL, u_dst], writes=[u_EL])

            add(DVE, lambda e: e.tensor_scalar(out=EL[0:8, :], in0=CPs[0:8, :], scalar1=8.0, scalar2=None, op0=ALU.mult), reads=[u_CP], writes=[u_EL])
            pieces(KP, u_KP)
            CP3 = CPs[0:8, :].rearrange("p (t n) -> p t n", n=128)
            add(DVE, lambda e: e.tensor_scalar(out=EL[0:8, :].rearrange("p (t n) -> p t n", n=128), in0=CP3[:, :, 127:128].broadcast_to([8, NT, 128]),
                                               scalar1=-8.0, scalar2=None, op0=ALU.mult), reads=[u_CP, u_EL], writes=[u_EL])
            pieces(QP, u_QP)
            for h2 in range(2):
                base = 64 if h2 == 0 else 0
                add(DVE, lambda e: e.memset(ka[h2][base:base + 64, :], 0.0), writes=[u_ka[h2]])
                add(DVE, lambda e: e.memset(ka[h2][base:base + 32, :], 1.0), writes=[u_ka[h2]])
                add(DVE, lambda e: e.memset(qa[h2][base:base + 64, :], 0.0), writes=[u_qa[h2]])
                add(DVE, lambda e: e.memset(qa[h2][base:base + 3, :], 1.0), writes=[u_qa[h2]])
            add(DVE, lambda e: e.memset(vaug[:, :, :, 64:65], 1.0), writes=[u_v])
            caus = cstb[:, CST_CAUS:CST_CAUS + 128]
            rsS = Rot([2, 3, 4, 5])
            rsO = Rot([6, 7])
            for hp in range(4):
                for tq in range(NQ):
                    tsl = slice(tq * TQ, (tq + 1) * TQ)
                    bk = rp.nxt()
                    proj_fm(w_fx, u_wfx, hp * 128, lambda k: uT[:, k, tsl], [u_uT], TQ, bk)
                    add(ACT, lambda e: e.copy(out=qa[0][0:64, tsl], in_=banks[bk][0:64, 0:TQ]), reads=[ub[bk]], writes=[u_qa[0]])
                    add(ACT, lambda e: e.copy(out=qa[1][64:128, tsl], in_=banks[bk][64:128, 0:TQ]), reads=[ub[bk]], writes=[u_qa[1]])
                    bk = rp.nxt()
                    proj_fm(w_fx, u_wfx, 512 + hp * 128, lambda k: uT[:, k, tsl], [u_uT], TQ, bk)
                    add(DVE, lambda e: e.tensor_copy(out=ka[0][0:64, tsl], in_=banks[bk][0:64, 0:TQ]), reads=[ub[bk]], writes=[u_ka[0]])
                    add(DVE, lambda e: e.tensor_copy(out=ka[1][64:128, tsl], in_=banks[bk][64:128, 0:TQ]), reads=[ub[bk]], writes=[u_ka[1]])
                for h2 in range(2):
                    h = 2 * hp + h2
                    base = 64 if h2 == 0 else 0
                    for r in range(3):
                        add(SP, lambda e: e.dma_start(out=ka[h2][base + r:base + r + 1, :], in_=KP[h:h + 1, r, :]),
                            reads=[u_KP], writes=[u_ka[h2]], dma=u_ka[h2])
                        add(SP, lambda e: e.dma_start(out=qa[h2][base + 3 + r:base + 4 + r, :], in_=QP[h:h + 1, r, :]),
                            reads=[u_QP], writes=[u_qa[h2]], dma=u_qa[h2])
                for tt in range(NT):
                    bk = rp.nxt()
                    for k in range(8):
                        add(PE, lambda e, k=k, bk=bk, tt=tt: e.matmul(banks[bk][:, 0:128], lhsT=uT[:, k, tt * 128:(tt + 1) * 128],
                                                                      rhs=w_fx[:, k, 1024 + hp * 128:1024 + (hp + 1) * 128],
                                                                      start=(k == 0), stop=(k == 7)),
                            reads=[u_uT, u_wfx], writes=[ub[bk]])
                    add(ACT, lambda e, bk=bk, tt=tt: e.copy(out=vaug[:, tt, :, 0:64], in_=banks[bk][:, 0:128].rearrange("p (h d) -> p h d", h=2)),
                        reads=[ub[bk]], writes=[u_v])
                npt = 0
                pend = []

                def emit_pv(G):
                    (qb, h2, grp, sl, bo, last_of_qb) = G
                    for jj, kb in enumerate(grp):
                        add(PE, lambda e: e.matmul(banks[bo][:, h2 * 65:(h2 + 1) * 65], lhsT=pT[sl][:, jj * 128:(jj + 1) * 128],
                                                   rhs=vaug[:, kb, h2, :], start=(kb == 0), stop=(kb == qb)),
                            reads=[u_pT[sl], u_v], writes=[ub[bo]])
                    if last_of_qb:
                        qs = slice(qb * 128, (qb + 1) * 128)
                        o3 = banks[bo][:, 0:130].rearrange("p (h d) -> p h d", h=2)
                        add(DVE, lambda e: e.reciprocal(out=rden[:, 0:2].unsqueeze(2), in_=o3[:, :, 64:65]), reads=[ub[bo]], writes=[u_rden])
                        add(DVE, lambda e: e.tensor_tensor(out=ftm[:, qb, :].rearrange("p (h d) -> p h d", h=2), in0=o3[:, :, 0:64],
                                                           in1=rden[:, 0:2].unsqueeze(2).broadcast_to([128, 2, 64]), op=ALU.mult),
                            reads=[ub[bo], u_rden], writes=[u_ftm])
                        bk = rp.nxt()
                        pbf = banks[bk].bitcast(BF16)
                        add(PE, lambda e: e.transpose(pbf[:, 0:128], ftm[:, qb, :], identb), reads=[u_ftm, u_cst], writes=[ub[bk]])
                        add(DVE, lambda e: e.tensor_copy(out=brT[:, BR_FOX + hp, qs], in_=pbf[:, 0:128]), reads=[ub[bk]], writes=[u_br[0]])

                for qb in range(NT):
                    bo = rsO.nxt()
                    qs = slice(qb * 128, (qb + 1) * 128)
                    for h2 in range(2):
                        h = 2 * hp + h2
                        rows = slice(h2 * 64, h2 * 64 + 64)
                        starts = list(range(0, qb + 1, 4))
                        for kb0 in starts:
                            grp = list(range(kb0, min(kb0 + 4, qb + 1)))
                            bsk = rsS.nxt()
                            sl = npt % NPT
                            npt += 1
                            for jj, kb in enumerate(grp):
                                add(PE, lambda e: e.matmul(banks[bsk][:, jj * 128:(jj + 1) * 128], lhsT=ka[h2][:, kb * 128:(kb + 1) * 128],
                                                           rhs=qa[h2][:, qs], start=True, stop=True),
                                    reads=[u_ka[h2], u_qa[h2]], writes=[ub[bsk]])
                            ng = len(grp)
                            add(ACT, lambda e: e.activation(out=pT[sl][:, 0:ng * 128], in_=banks[bsk][:, 0:ng * 128], func=AF.Exp, scale=0.125),
                                reads=[ub[bsk]], writes=[u_pT[sl]])
                            for jj, kb in enumerate(grp):
                                if kb == qb:
                                    add(DVE, lambda e: e.tensor_tensor(out=pT[sl][:, jj * 128:(jj + 1) * 128], in0=pT[sl][:, jj * 128:(jj + 1) * 128],
                                                                       in1=caus, op=ALU.mult),
                                        reads=[u_pT[sl], u_cst], writes=[u_pT[sl]])
                            pend.append((qb, h2, grp, sl, bo, (h2 == 1 and kb0 == starts[-1])))
                            if len(pend) > 3:
                                emit_pv(pend.pop(0))
                while pend:
                    emit_pv(pend.pop(0))


            P.barrier(bar_scr[:, 0:1])
            cw = Carver(OFF_W, W_B)
            w_mq = cw.take(8 * 512 * 2, BF16, "p (k n) -> p k n", k=8)
            w_kv = cw.take(8 * 1024 * 2, BF16, "p (k n) -> p k n", k=8)
            u_wmq, u_wkv = P.unit(f"wmq{b}"), P.unit(f"wkv{b}")
            sload(w_mq.rearrange("p k n -> p (k n)"), S_mq, 0, 4096, u_wmq, uS_mq)
            sload(w_kv.rearrange("p k n -> p (k n)"), S_kv, 0, 8192, u_wkv, uS_kv)
            cv = Carver(OFF_SCR, SCR_B)
            mnT = cv.take(8 * MEML * 2, BF16, "p (k t) -> p k t", k=8)
            u_mn = P.unit(f"mn{b}")
            rms_T(cv, lambda tt: mem_d[b, tt * 128:(tt + 1) * 128, :], 2, PV_MEMG, mnT, u_mn, [0, 1], f"D{b}")
            kmT = cv.take(4 * MEML * 2, BF16, "p (h t) -> p h t", h=4)
            vma = cv.take(2 * 4 * 129 * 2, BF16, "p (t h d) -> p t h d", t=2, h=4)
            qmT = cv.take(S * 2, BF16)
            mtm = cv.take(NT * 128 * 2, BF16, "p (t n) -> p t n", t=NT)
            pM = [cv.take(TQ * 2, BF16) for _ in range(4)]
            qm2 = [qmT, cv.take(S * 2, BF16)]
            rdm = cv.take(16, F32)
            u_km, u_vm, u_qm, u_mtm, u_rdm = P.unit(f"km{b}"), P.unit(f"vm{b}"), P.unit(f"qm{b}"), P.unit(f"mtm{b}"), P.unit(f"rdm{b}")
            u_pM = [P.unit(f"pM{b}{i}") for i in range(4)]
            u_qm2 = [u_qm, P.unit(f"qmb{b}")]
            rp = Rot([0, 1])
            for h in range(4):
                bk = rp.nxt()
                proj_fm(w_kv, u_wkv, h * 128, lambda k: mnT[:, k, :], [u_mn], MEML, bk)
                add(ACT, lambda e, bk=bk, h=h: e.copy(out=kmT[:, h, :], in_=banks[bk][:, 0:MEML]), reads=[ub[bk]], writes=[u_km])
            add(DVE, lambda e: e.memset(vma[:, :, :, 128:129], 1.0), writes=[u_vm])
            for kt in range(2):
                bk = rp.nxt()
                for k in range(8):
                    add(PE, lambda e, k=k, bk=bk, kt=kt: e.matmul(banks[bk][:, 0:512], lhsT=mnT[:, k, kt * 128:(kt + 1) * 128],
                                                                  rhs=w_kv[:, k, 512:1024], start=(k == 0), stop=(k == 7)),
                        reads=[u_mn, u_wkv], writes=[ub[bk]])
                add(ACT, lambda e, bk=bk, kt=kt: e.copy(out=vma[:, kt, :, 0:128], in_=banks[bk][:, 0:512].rearrange("p (h d) -> p h d", h=4)),
                    reads=[ub[bk]], writes=[u_vm])
            rsS = Rot([2, 3, 4, 5])
            rsO = Rot([6, 7])
            items = [(h, tq) for h in range(4) for tq in range(NQ)]

            def emit_qproj(h):
                for tq in range(NQ):
                    tsl = slice(tq * TQ, (tq + 1) * TQ)
                    bk = rp.nxt()
                    proj_fm(w_mq, u_wmq, h * 128, lambda k: uT[:, k, tsl], [u_uT], TQ, bk)
                    add(ACT, lambda e: e.copy(out=qm2[h % 2][:, tsl], in_=banks[bk][:, 0:TQ]), reads=[ub[bk]], writes=[u_qm2[h % 2]])

            def emit_S(i):
                h, tq = items[i]
                tsl = slice(tq * TQ, (tq + 1) * TQ)
                for kt in range(2):
                    sl = (i % 2) * 2 + kt
                    bsk = rsS.nxt()
                    add(PE, lambda e: e.matmul(banks[bsk][:, 0:TQ], lhsT=kmT[:, h, kt * 128:(kt + 1) * 128], rhs=qm2[h % 2][:, tsl],
                                               start=True, stop=True), reads=[u_km, u_qm2[h % 2]], writes=[ub[bsk]])
                    add(ACT, lambda e: e.activation(out=pM[sl], in_=banks[bsk][:, 0:TQ], func=AF.Exp, scale=128 ** -0.5),
                        reads=[ub[bsk]], writes=[u_pM[sl]])

            def emit_PV(i):
                h, tq = items[i]
                for half in range(2):
                    bo = rsO.nxt()
                    for s2 in range(2):
                        sub = half * 2 + s2
                        for kt in range(2):
                            sl = (i % 2) * 2 + kt
                            add(PE, lambda e: e.matmul(banks[bo][:, s2 * 129:(s2 + 1) * 129], lhsT=pM[sl][:, sub * 128:(sub + 1) * 128],
                                                       rhs=vma[:, kt, h, :], start=(kt == 0), stop=(kt == 1)),
                                reads=[u_pM[sl], u_vm], writes=[ub[bo]])
                    o3 = banks[bo][:, 0:258].rearrange("p (s d) -> p s d", s=2)
                    t0 = tq * 4 + half * 2
                    add(DVE, lambda e: e.reciprocal(out=rdm[:, 0:2].unsqueeze(2), in_=o3[:, :, 128:129]), reads=[ub[bo]], writes=[u_rdm])
                    add(DVE, lambda e: e.tensor_tensor(out=mtm[:, t0:t0 + 2, :], in0=o3[:, :, 0:128],
                                                       in1=rdm[:, 0:2].unsqueeze(2).broadcast_to([128, 2, 128]), op=ALU.mult),
                        reads=[ub[bo], u_rdm], writes=[u_mtm])
                    for s2 in range(2):
                        tt = t0 + s2
                        bk = rp.nxt()
                        pbf = banks[bk].bitcast(BF16)
                        add(PE, lambda e: e.transpose(pbf[:, 0:128], mtm[:, tt, :], identb), reads=[u_mtm, u_cst], writes=[ub[bk]])
                        add(ACT, lambda e: e.copy(out=brT[:, BR_MEM + h, tt * 128:(tt + 1) * 128], in_=pbf[:, 0:128]),
                            reads=[ub[bk]], writes=[u_br[2]])

            emit_qproj(0)
            emit_S(0)
            for i in range(len(items)):
                h, tq = items[i]
                if tq == 0 and h + 1 < 4:
                    emit_qproj(h + 1)
                if i + 1 < len(items):
                    emit_S(i + 1)
                emit_PV(i)

            if dbg and b == 0:
                P.barrier(bar_scr[:, 0:1])
                cvd = Carver(OFF_SCR, SCR_B)
                dt2 = cvd.take(S * 4, F32)
                u_d2 = P.unit("dbgB")
                for kk_ in range(12):
                    add(DVE, lambda e, kk_=kk_: e.tensor_copy(out=dt2, in_=brT[:, kk_, :]), reads=u_br, writes=[u_d2])
                    add(SP, lambda e, kk_=kk_: e.dma_start(out=dbg_d["d_brT"][:, kk_ * S:(kk_ + 1) * S], in_=dt2), reads=[u_d2], dma=u_d2, is_out=True)

            P.barrier(bar_scr[:, 0:1])
            cw = Carver(OFF_W, W_B)
            w_bo = cw.take(3 * 4 * 1024 * 2, BF16, "p (r c n) -> p r c n", r=3, c=4)
            u_wbo = P.unit(f"wbo{b}")
            sload(w_bo.rearrange("p r c n -> p (r c n)"), S_bo, 0, 12 * 1024, u_wbo, uS_bo)
            wg = [cw.take(8 * 3 * 128 * 2, BF16, "p (k r n) -> p k r n", k=8, r=3) for _ in range(2)]
            u_wg = [P.unit(f"wg{b}{i}", nobar=False) for i in range(2)]
            cv = Carver(OFF_SCR, SCR_B)
            mgT = cv.take(8 * SMAX * 2, BF16, "p (k t) -> p k t", k=8)[:, :, 0:S]
            u_mg = P.unit(f"mg{b}")
            sgt = [cv.take(TQ * 4, F32) for _ in range(3)]
            u_sg = [P.unit(f"sg{b}{i}") for i in range(3)]
            ra = Rot(range(8))
            for oc in range(8):
                s = oc % 2
                sload(wg[s].rearrange("p k r n -> p (k r n)"), S_gt, oc * 3072, 3072, u_wg[s], uS_gt)
                for tq in range(NQ):
                    tsl = slice(tq * TQ, (tq + 1) * TQ)
                    bg_, bb_ = [], []
                    for r in range(3):
                        bk = ra.nxt()
                        bg_.append(bk)
                        for k in range(8):
                            add(PE, lambda e, k=k, bk=bk, r=r: e.matmul(banks[bk][:, 0:TQ], lhsT=wg[s][:, k, r, :], rhs=uT[:, k, tsl],
                                                                        start=(k == 0), stop=(k == 7)), reads=[u_wg[s], u_uT], writes=[ub[bk]])
                        add(ACT, lambda e, bk=bk, r=r: e.activation(out=sgt[r], in_=banks[bk][:, 0:TQ], func=AF.Sigmoid), reads=[ub[bk]], writes=[u_sg[r]])
                    for r in range(3):
                        bk = ra.nxt()
                        bb_.append(bk)
                        for c in range(4):
                            add(PE, lambda e, c=c, bk=bk, r=r: e.matmul(banks[bk][:, 0:TQ], lhsT=w_bo[:, r, c, oc * 128:(oc + 1) * 128],
                                                                        rhs=brT[:, (BR_FOX, BR_RWK, BR_MEM)[r] + c, tsl], start=(c == 0), stop=(c == 3)),
                                reads=[u_wbo, u_br[r]], writes=[ub[bk]])
                        add(DVE, lambda e, bk=bk, r=r: e.tensor_tensor(out=sgt[r], in0=sgt[r], in1=banks[bk][:, 0:TQ], op=ALU.mult),
                            reads=[u_sg[r], ub[bk]], writes=[u_sg[r]])
                    add(DVE, lambda e: e.tensor_tensor(out=sgt[0], in0=sgt[0], in1=sgt[1], op=ALU.add), reads=[u_sg[0], u_sg[1]], writes=[u_sg[0]])
                    add(DVE, lambda e, tsl=tsl: e.tensor_tensor(out=mgT[:, oc, tsl], in0=sgt[0], in1=sgt[2], op=ALU.add),
                        reads=[u_sg[0], u_sg[2]], writes=[u_mg])

            P.barrier(bar_scr[:, 0:1])
            cw = Carver(OFF_W, W_B)
            w_dn = cw.take(NFC * 1024 * 2, BF16, "p (f n) -> p f n", f=NFC)
            u_wdn = P.unit(f"wdn{b}")
            cb = Carver(OFF_BR, BR_B)
            w_o = cb.take(8 * 1024 * 2, BF16, "p (k n) -> p k n", k=8)
            u_wo = P.unit(f"wo{b}")
            sload(w_o.rearrange("p k n -> p (k n)"), S_o, 0, 8192, u_wo, uS_o)
            sload(w_dn.rearrange("p f n -> p (f n)"), S_dn, 0, NFC * 1024, u_wdn, uS_dn)
            actT = cb.take(NFC * TQ * 2, BF16, "p (f t) -> p f t", f=NFC)
            u_act = P.unit(f"act{b}")
            cu = Carver(OFF_UT, UT_B)
            hres = cu.take(4 * 1024 * 4, F32, "p (s n) -> p s n", s=4)
            u2T = cu.take(8 * TQ * 2, BF16, "p (k t) -> p k t", k=8)
            xt = [cu.take(4096, F32) for _ in range(2)]
            u_hres = [P.unit(f"hres{b}{i}") for i in range(4)]
            u_u2 = P.unit(f"u2{b}")
            u_xt = [P.unit(f"xt{b}{i}") for i in range(2)]
            cv = Carver(OFF_SCR + 8 * SMAX * 2, SCR_B - 8 * SMAX * 2)
            wgu = [cv.take(8 * 2 * 128 * 2, BF16, "p (k r n) -> p k r n", k=8, r=2) for _ in range(3)]
            wgu.append(cw.take(8 * 2 * 128 * 2, BF16, "p (k r n) -> p k r n", k=8, r=2))
            NWG = len(wgu)
            u_wgu = [P.unit(f"wgu{b}{i}") for i in range(NWG)]
            pg = cv.take(2 * 1024 * 4, F32, "p (r n) -> p r n", r=2)
            u_pg = P.unit(f"pg{b}")
            add(SP, lambda e: e.dma_start(out=pg.rearrange("p r n -> p (r n)"), in_=rowp_d.partition_broadcast(128)), writes=[u_pg], dma=u_pg)
            yt = cb.take(4096, F32)
            u_yt = P.unit(f"yt{b}")
            hnb = cb.take(2048, BF16)
            u_hnb = P.unit(f"hnb{b}")
            jk = cb.take(2048, BF16)
            u_jk = P.unit(f"jk{b}")
            sst = cb.take(64 * 4, F32)
            u_ss = P.unit(f"ss{b}")
            slt = [cv.take(TQ * 4, F32) for _ in range(2)]
            u_sl = [P.unit(f"slt{b}{i}") for i in range(2)]
            ra = Rot(range(8))
            nxl = [0]
            nwl = 0
            for tq in range(NQ):
                def e2_mm(sub):
                        tok0 = tq * TQ + sub * 128
                        by = [ra.nxt(), ra.nxt()]
                        for nh in range(2):
                            for k in range(8):
                                add(PE, lambda e, k=k, nh=nh, by=by, tok0=tok0: e.matmul(banks[by[nh]][:, 0:512], lhsT=mgT[:, k, tok0:tok0 + 128],
                                                                                       rhs=w_o[:, k, nh * 512:(nh + 1) * 512], start=(k == 0), stop=(k == 7)),
                                    reads=[u_mg, u_wo], writes=[ub[by[nh]]])
                            add(ACT, lambda e, nh=nh, by=by: e.activation(out=jk[:, 0:512], in_=banks[by[nh]][:, 0:512], func=AF.Square,
                                                                          accum_out=sst[:, sub * 16 + nh:sub * 16 + nh + 1]), reads=[ub[by[nh]]], writes=[u_jk, u_ss])
                        return by, tok0

                def e2_post(sub, by, tok0):
                        add(DVE, lambda e: e.tensor_tensor(out=sst[:, sub * 16 + 2:sub * 16 + 3], in0=sst[:, sub * 16 + 0:sub * 16 + 1], in1=sst[:, sub * 16 + 1:sub * 16 + 2], op=ALU.add), reads=[u_ss], writes=[u_ss])
                        add(ACT, lambda e: e.activation(out=sst[:, sub * 16 + 3:sub * 16 + 4], in_=sst[:, sub * 16 + 2:sub * 16 + 3], func=AF.Sqrt, scale=1.0 / D, bias=NORM_EPS), reads=[u_ss], writes=[u_ss])
                        add(DVE, lambda e: e.reciprocal(out=sst[:, sub * 16 + 4:sub * 16 + 5], in_=sst[:, sub * 16 + 3:sub * 16 + 4]), reads=[u_ss], writes=[u_ss])
                        xs = nxl[0] % 2
                        nxl[0] += 1
                        add(SP, lambda e, xs=xs, tok0=tok0: e.dma_start(out=xt[xs], in_=x_d[b, tok0:tok0 + 128, :]), writes=[u_xt[xs]], dma=u_xt[xs])
                        for nh in range(2):
                            add(DVE, lambda e, nh=nh, by=by: e.scalar_tensor_tensor(out=yt[:, nh * 512:(nh + 1) * 512], in0=banks[by[nh]][:, 0:512],
                                                                                   scalar=sst[:, sub * 16 + 4:sub * 16 + 5], in1=pg[:, 0, nh * 512:(nh + 1) * 512],
                                                                                   op0=ALU.mult, op1=ALU.mult),
                                reads=[ub[by[nh]], u_ss, u_pg], writes=[u_yt])
                        add(DVE, lambda e, xs=xs, sub=sub: e.tensor_tensor(out=hres[:, sub, :], in0=yt, in1=xt[xs], op=ALU.add),
                            reads=[u_yt, u_xt[xs]], writes=[u_hres[sub]])
                        add(ACT, lambda e, sub=sub: e.activation(out=jk, in_=hres[:, sub, :], func=AF.Square, accum_out=sst[:, sub * 16 + 5:sub * 16 + 6]),
                            reads=[u_hres[sub]], writes=[u_jk, u_ss])
                        add(ACT, lambda e: e.activation(out=sst[:, sub * 16 + 6:sub * 16 + 7], in_=sst[:, sub * 16 + 5:sub * 16 + 6], func=AF.Sqrt, scale=1.0 / D, bias=NORM_EPS), reads=[u_ss], writes=[u_ss])
                        add(DVE, lambda e: e.reciprocal(out=sst[:, sub * 16 + 7:sub * 16 + 8], in_=sst[:, sub * 16 + 6:sub * 16 + 7]), reads=[u_ss], writes=[u_ss])
                        add(DVE, lambda e, sub=sub: e.tensor_scalar(out=hnb, in0=hres[:, sub, :], scalar1=sst[:, sub * 16 + 7:sub * 16 + 8], scalar2=None, op0=ALU.mult),
                            reads=[u_hres[sub], u_ss], writes=[u_hnb])
                        bk = ra.nxt()
                        pbf = banks[bk].bitcast(BF16)
                        for c in range(8):
                            add(PE, lambda e, c=c, pbf=pbf: e.transpose(pbf[:, c * 128:(c + 1) * 128], hnb[:, c * 128:(c + 1) * 128], identb),
                                reads=[u_hnb, u_cst], writes=[ub[bk]])
                        add(DVE, lambda e, pbf=pbf, sub=sub: e.tensor_tensor(out=u2T[:, :, sub * 128:(sub + 1) * 128],
                                                                             in0=pbf[:, 0:1024].rearrange("p (k t) -> p k t", k=8),
                                                                             in1=pv[:, PV_PRE2:PV_PRE2 + 8].unsqueeze(2).broadcast_to([128, 8, 128]), op=ALU.mult),
                            reads=[ub[bk], u_pv], writes=[u_u2])
                        if dbg and b == 0:
                            add(SP, lambda e, sub=sub, tok0=tok0: e.dma_start(out=dbg_d["d_h"][tok0:tok0 + 128, :], in_=hres[:, sub, :]),
                                reads=[u_hres[sub]], dma=u_hres[sub], is_out=True)

                pend2 = []
                for sub in range(4):
                    pend2.append((sub,) + e2_mm(sub))
                    if len(pend2) > 3:
                        e2_post(*pend2.pop(0))
                while pend2:
                    e2_post(*pend2.pop(0))
                for fc in range(NFC):
                    s = nwl % NWG
                    nwl += 1
                    sload(wgu[s].rearrange("p k r n -> p (k r n)"), S_ff, fc * 2048, 2048, u_wgu[s], uS_ff)
                    bgk, buk = ra.nxt(), ra.nxt()
                    for r, bk in ((0, bgk), (1, buk)):
                        for k in range(8):
                            add(PE, lambda e, k=k, r=r, bk=bk, s=s: e.matmul(banks[bk][:, 0:TQ], lhsT=wgu[s][:, k, r, :], rhs=u2T[:, k, :],
                                                                            start=(k == 0), stop=(k == 7)), reads=[u_wgu[s], u_u2], writes=[ub[bk]])
                    add(ACT, lambda e, bgk=bgk, s=s: e.activation(out=slt[s % 2], in_=banks[bgk][:, 0:TQ], func=AF.Silu), reads=[ub[bgk]], writes=[u_sl[s % 2]])
                    add(DVE, lambda e, buk=buk, s=s, fc=fc: e.tensor_tensor(out=actT[:, fc, :], in0=slt[s % 2], in1=banks[buk][:, 0:TQ], op=ALU.mult),
                        reads=[u_sl[s % 2], ub[buk]], writes=[u_act])
                for sub in range(4):
                    tok0 = tq * TQ + sub * 128
                    by = [ra.nxt(), ra.nxt()]
                    for nh in range(2):
                        for f in range(NFC):
                            add(PE, lambda e, f=f, nh=nh, by=by, sub=sub: e.matmul(banks[by[nh]][:, 0:512], lhsT=actT[:, f, sub * 128:(sub + 1) * 128],
                                                                                 rhs=w_dn[:, f, nh * 512:(nh + 1) * 512], start=(f == 0), stop=(f == NFC - 1)),
                                reads=[u_act, u_wdn], writes=[ub[by[nh]]])
                        add(ACT, lambda e, nh=nh, by=by: e.activation(out=jk[:, 0:512], in_=banks[by[nh]][:, 0:512], func=AF.Square,
                                                                      accum_out=sst[:, 8 + nh:9 + nh]), reads=[ub[by[nh]]], writes=[u_jk, u_ss])
                    add(DVE, lambda e: e.tensor_tensor(out=sst[:, 10:11], in0=sst[:, 8:9], in1=sst[:, 9:10], op=ALU.add), reads=[u_ss], writes=[u_ss])
                    add(ACT, lambda e: e.activation(out=sst[:, 11:12], in_=sst[:, 10:11], func=AF.Sqrt, scale=1.0 / D, bias=NORM_EPS), reads=[u_ss], writes=[u_ss])
                    add(DVE, lambda e: e.reciprocal(out=sst[:, 12:13], in_=sst[:, 11:12]), reads=[u_ss], writes=[u_ss])
                    for nh in range(2):
                        add(DVE, lambda e, nh=nh, by=by: e.scalar_tensor_tensor(out=yt[:, nh * 512:(nh + 1) * 512], in0=banks[by[nh]][:, 0:512],
                                                                               scalar=sst[:, 12:13], in1=pg[:, 1, nh * 512:(nh + 1) * 512],
                                                                               op0=ALU.mult, op1=ALU.mult),
                            reads=[ub[by[nh]], u_ss, u_pg], writes=[u_yt])
                    add(DVE, lambda e, sub=sub: e.tensor_tensor(out=hres[:, sub, :], in0=yt, in1=hres[:, sub, :], op=ALU.add),
                        reads=[u_yt, u_hres[sub]], writes=[u_hres[sub]])
                    add(SP, lambda e, sub=sub, tok0=tok0: e.dma_start(out=out_d[b, tok0:tok0 + 128, :], in_=hres[:, sub, :]),
                        reads=[u_hres[sub]], dma=u_hres[sub], is_out=True)
        P.emit()
        nops = P.nops
    return nc, nops


def _cols(v, n):
    return np.ascontiguousarray(np.asarray(v, np.float32).reshape(n, 128).T)


def host_params(inp):
    pvv = np.zeros((128, NPV), np.float32)
    pvv[:, PV_PRE1:PV_PRE1 + 8] = _cols(inp["pre1_g"][0], 8)
    pvv[:, PV_PRE2:PV_PRE2 + 8] = _cols(inp["pre2_g"][0], 8)
    pvv[:, PV_MEMG:PV_MEMG + 8] = _cols(inp["mem_norm_g"][0], 8)
    pvv[:, PV_MU:PV_MU + 14] = _cols(inp["rwkv_mu"][0], 14)
    pvv[:, PV_W0:PV_W0 + 4] = _cols(inp["rwkv_w0"][0], 4)
    pvv[:, PV_A0:PV_A0 + 4] = _cols(inp["rwkv_a0"][0], 4)
    pvv[:, PV_KK:PV_KK + 4] = _cols(inp["rwkv_k_k"][0], 4)
    pvv[:, PV_KA:PV_KA + 4] = _cols(inp["rwkv_k_a"][0], 4)
    pvv[:, PV_RK:PV_RK + 4] = _cols(np.asarray(inp["rwkv_r_k"][0]).reshape(-1), 4)
    pvv[0:8, PV_FB] = np.asarray(inp["fox_f_bias"][0], np.float32)
    gg = np.asarray(inp["rwkv_gn_g"][0], np.float32).reshape(4, 2, 64)
    gb = np.asarray(inp["rwkv_gn_b"][0], np.float32).reshape(4, 2, 64)
    gnp = np.zeros((128, 4, 2, 64), np.float32)
    for hp in range(4):
        for hh in range(2):
            gnp[hh * 64:(hh + 1) * 64, hp, 0, :] = gg[hp, hh][None, :]
            gnp[hh * 64:(hh + 1) * 64, hp, 1, :] = gb[hp, hh][None, :]
    rowp = np.concatenate([np.asarray(inp["post1_g"][0], np.float32), np.asarray(inp["post2_g"][0], np.float32)])[None, :]
    f = lambda k: np.ascontiguousarray(np.asarray(inp[k][0], np.float32))
    shared = {
        "w_in": f("w_in"), "w_mem_kv": f("w_mem_kv"), "w_fox_out": f("w_fox_out"), "w_rwkv_out": f("w_rwkv_out"),
        "w_mem_out": f("w_mem_out"), "w_o": f("w_o"), "w_ffn_gate": f("w_ffn_gate"), "w_ffn_up": f("w_ffn_up"),
        "w_ffn_down": f("w_ffn_down"), "rwkv_w_up": f("rwkv_w_up"), "rwkv_a_up": f("rwkv_a_up"), "rwkv_g_up": f("rwkv_g_up"),
        "pv": pvv, "cst": make_consts(), "gnp": gnp.reshape(128, -1), "rowp": np.ascontiguousarray(rowp),
    }
    return shared


_CACHE = {}


def kernel(**inputs):
    x = np.asarray(inputs["x"], np.float32)
    mem = np.asarray(inputs["mem"], np.float32)
    B, S, _ = x.shape
    n = 8
    NB = B // n
    shared = host_params(inputs)
    key = (NB, S)
    if key not in _CACHE:
        _CACHE[key] = build(NB=NB, S=S)[0]
    nc = _CACHE[key]
    in_maps = []
    for c in range(n):
        m = dict(shared)
        m["x"] = np.ascontiguousarray(x[c * NB:(c + 1) * NB])
        m["mem"] = np.ascontiguousarray(mem[c * NB:(c + 1) * NB])
        in_maps.append(m)
    res = run_bass_kernel_spmd(nc, in_maps, core_ids=list(range(n)))
    out = np.concatenate([np.asarray(r["out"], np.float32) for r in res.results], axis=0)
    return out
```

```python
import contextlib
import numpy as np
import concourse.bass as bass
import concourse.mybir as mybir
from concourse.bass_utils import run_bass_kernel_spmd

F32 = mybir.dt.float32
BF16 = mybir.dt.bfloat16
ALU = mybir.AluOpType
AF = mybir.ActivationFunctionType
AX = mybir.AxisListType

PE, ACT, DVE, POOL, SP = "tensor", "scalar", "vector", "gpsimd", "sync"
ENGS = (PE, ACT, DVE, POOL, SP)
EPOCH = 24000

D = 1024
MEML = 256
DFF = 2816
NFC = DFF // 128
FOX_COLS = 1544
RW_COLS = 1792
RW0 = FOX_COLS
MQ0 = RW0 + RW_COLS
GT0 = MQ0 + 512
IN_COLS = GT0 + 3072
C0 = float(np.exp(-0.5))
NORM_EPS = 1e-6
GN_EPS = 64e-5


class Unit:
    __slots__ = ("name", "last_w", "readers", "psum", "sem", "cnt", "nobar")

    def __init__(self, name, psum=False, nobar=False):
        self.name = name
        self.last_w = None
        self.readers = []
        self.psum = psum
        self.sem = None
        self.cnt = 0
        self.nobar = nobar


class Op:
    __slots__ = ("eng", "fn", "deps", "mark", "midx", "dma", "sem", "val", "waits")

    def __init__(self, eng, fn, dma):
        self.eng = eng
        self.fn = fn
        self.deps = []
        self.mark = False
        self.midx = -1
        self.dma = dma
        self.sem = None
        self.val = 0
        self.waits = []


class _Rec:
    __slots__ = ("call",)

    def __init__(self):
        self.call = None

    def __getattr__(self, name):
        def f(*a, **k):
            assert self.call is None
            self.call = (name, a, k)
            return self
        return f


class Prog:
    def __init__(self, nc, stack):
        self.nc = nc
        self.stack = stack
        self.ops = {e: [] for e in ENGS}
        self.units = []
        self.dma_units = []
        self.out_dma_ops = []
        self.nops = 0
        self.cur_bar = None
        self.defer = None
        self.atomic_depth = 0

    def unit(self, name, psum=False, nobar=False):
        u = Unit(name, psum, nobar)
        u.last_w = self.cur_bar
        self.units.append(u)
        return u

    def sb(self, name, shape, dtype):
        return self.stack.enter_context(self.nc.sbuf_tensor(name, list(shape), dtype))

    def ps(self, name, shape, dtype):
        return self.stack.enter_context(self.nc.psum_tensor(name, list(shape), dtype))

    def add(self, eng, fn, reads=(), writes=(), dma=None, is_out=False):
        rec = _Rec()
        fn(rec)
        assert rec.call is not None
        if self.defer is not None:
            entry = (eng, rec.call, tuple(reads), tuple(writes), dma, is_out)
            if self.atomic_depth and self.defer and self.defer[-1][0]:
                self.defer[-1][1].append(entry)
            else:
                self.defer.append([bool(self.atomic_depth), [entry]])
            return None
        return self._reg(eng, rec.call, reads, writes, dma, is_out)

    def atomic_begin(self):
        self.atomic_depth += 1
        if self.defer is not None:
            self.defer.append([True, []])

    def atomic_end(self):
        self.atomic_depth -= 1
        if self.defer is not None and self.defer and self.defer[-1][0]:
            self.defer[-1][0] = False

    def run_deferred(self, queue, n=1):
        for _ in range(n):
            if not queue:
                return
            _, entries = queue.pop(0)
            for (eng, call, reads, writes, dma, is_out) in entries:
                self._reg(eng, call, reads, writes, dma, is_out)

    def _reg(self, eng, call, reads=(), writes=(), dma=None, is_out=False):
        op = Op(eng, call, dma)
        self.nops += 1
        deps = op.deps
        for u in reads:
            if u.psum:
                if u.last_w is not None:
                    deps.append((u.last_w, "RAW"))
                u.last_w = op
                continue
            if u.last_w is not None:
                deps.append((u.last_w, "RAW"))
            u.readers.append(op)
        for u in writes:
            if u.last_w is not None and u.last_w is not op:
                deps.append((u.last_w, "RAW" if u.psum else "WAW"))
            for r in u.readers:
                if r is not op:
                    deps.append((r, "WAR"))
            u.last_w = op
            u.readers = []
        if dma is not None:
            if dma.sem is None:
                dma.sem = True
                self.dma_units.append(dma)
            dma.cnt += 16
            op.sem = dma
            op.val = dma.cnt
            if is_out:
                self.out_dma_ops.append(op)
        self.ops[eng].append(op)
        return op

    def barrier(self, scratch_ap):
        us = [u for u in self.units if not u.nobar]
        self.cur_bar = self.add(DVE, lambda e: e.memset(scratch_ap, 0.0), writes=us)

    def emit(self):
        nc = self.nc
        for e in ENGS:
            for op in self.ops[e]:
                need = []
                seen = set()
                for (p, kind) in op.deps:
                    if id(p) in seen:
                        continue
                    if p.dma is not None or op.dma is not None:
                        pass
                    elif p.eng == e:
                        if e == PE or kind != "RAW":
                            continue
                    seen.add(id(p))
                    need.append(p)
                    if p.dma is None:
                        p.mark = True
                op.waits = need
                op.deps = None
        nep = {}
        for e in ENGS:
            k = 0
            for op in self.ops[e]:
                if op.mark:
                    op.midx = k
                    k += 1
            nep[e] = (k + EPOCH - 1) // EPOCH
        esem = {e: [self.stack.enter_context(nc.semaphore(f"es_{e}_{i}")) for i in range(nep[e])] for e in ENGS}
        for u in self.dma_units:
            u.sem = self.stack.enter_context(nc.semaphore(f"ds_{u.name}"))
        block = self.stack.enter_context(nc.Block())
        prog = self

        def body(e):
            def run(eng):
                waited = {}
                for op in prog.ops[e]:
                    for p in op.waits:
                        if p.dma is not None:
                            key = ("d", id(p.sem))
                            sem = p.sem.sem
                            val = p.val
                        else:
                            ep = p.midx // EPOCH
                            key = (p.eng, ep)
                            sem = esem[p.eng][ep]
                            val = p.midx % EPOCH + 1
                        if waited.get(key, 0) >= val:
                            continue
                        waited[key] = val
                        eng.wait_ge(sem, val)
                    nm, a_, k_ = op.fn
                    ins = getattr(eng, nm)(*a_, **k_)
                    if op.dma is not None:
                        ins.then_inc(op.sem.sem, 16)
                    elif op.mark:
                        ins.then_inc(esem[e][op.midx // EPOCH], 1)
                if e == SP:
                    done = set()
                    for op in prog.out_dma_ops:
                        if id(op.sem) in done:
                            continue
                        done.add(id(op.sem))
                        eng.wait_ge(op.sem.sem, op.sem.cnt)
            return run

        block.tensor(body(PE))
        block.scalar(body(ACT))
        block.vector(body(DVE))
        block.gpsimd(body(POOL))
        block.sync(body(SP))


CST_IDENT, CST_SU, CST_SL, CST_UI, CST_CAUS, CST_ONES, CST_SEL, CST_BONES, CST_STK = 0, 128, 256, 384, 512, 640, 768, 896, 1024
NCST = 1088
PV_PRE1, PV_PRE2, PV_MEMG, PV_MU, PV_W0, PV_A0, PV_KK, PV_KA, PV_RK, PV_FB = 0, 8, 16, 24, 38, 42, 46, 50, 54, 58
NPV = 64


def make_consts():
    c = np.zeros((128, NCST), np.float32)
    i = np.arange(128)
    blk = (i[:, None] // 64) == (i[None, :] // 64)
    s = i[:, None] % 64
    t = i[None, :] % 64
    c[:, CST_IDENT:CST_IDENT + 128] = np.eye(128)
    c[:, CST_SU:CST_SU + 128] = blk & (s < t)
    c[:, CST_SL:CST_SL + 128] = blk & (s > t)
    c[:, CST_UI:CST_UI + 128] = blk & (s <= t)
    c[:, CST_CAUS:CST_CAUS + 128] = i[:, None] <= i[None, :]
    c[:, CST_ONES:CST_ONES + 128] = 1.0
    c[127, CST_SEL:CST_SEL + 128] = 1.0
    c[:, CST_BONES:CST_BONES + 128] = blk
    c[:, CST_STK:CST_STK + 64] = (i[:, None] % 64) == np.arange(64)[None, :]
    return c


def build(NB=2, S=2048, dbg=False):
    nc = bass.Bass("TRN2", target_bir_lowering=False)
    NT = S // 128
    NQ = S // 512
    TQ = 512

    def din(name, shape):
        return nc.dram_tensor(name, list(shape), F32, kind="ExternalInput").ap()

    x_d = din("x", [NB, S, D])
    mem_d = din("mem", [NB, MEML, D])
    w_in_d = din("w_in", [D, IN_COLS])
    w_kv_d = din("w_mem_kv", [D, 1024])
    w_fo_d = din("w_fox_out", [512, D])
    w_ro_d = din("w_rwkv_out", [512, D])
    w_mo_d = din("w_mem_out", [512, D])
    w_o_d = din("w_o", [D, D])
    w_fg_d = din("w_ffn_gate", [D, DFF])
    w_fu_d = din("w_ffn_up", [D, DFF])
    w_fd_d = din("w_ffn_down", [DFF, D])
    wup_d = din("rwkv_w_up", [64, 512])
    aup_d = din("rwkv_a_up", [64, 512])
    gup_d = din("rwkv_g_up", [128, 512])
    pv_d = din("pv", [128, NPV])
    cst_d = din("cst", [128, NCST])
    gnp_d = din("gnp", [128, 4 * 2 * 64])
    rowp_d = din("rowp", [1, 2 * D])
    out_d = nc.dram_tensor("out", [NB, S, D], F32, kind="ExternalOutput").ap()
    dbg_d = {}
    if dbg:
        for nm, shp in (("d_uT", [128, 8 * S]), ("d_brT", [128, 12 * S]), ("d_h", [S, D])):
            dbg_d[nm] = nc.dram_tensor(nm, shp, F32, kind="ExternalOutput").ap()

    with contextlib.ExitStack() as st:
        P = Prog(nc, st)
        add = P.add

        cst = P.sb("cst_sb", [128, NCST], F32)
        cstb = P.sb("cstb", [128, NCST], BF16)
        pv = P.sb("pv_sb", [128, NPV], F32)
        pvx = P.sb("pvx", [128, 16], F32)
        u_cst = P.unit("cst")
        u_pv = P.unit("pv")
        UT_B, BR_B, W_B, SCR_B = 32768, 49152, 49152, 57344
        AR_B = UT_B + BR_B + W_B + SCR_B
        ar = P.sb("arena", [128, AR_B // 4], F32)
        OFF_UT, OFF_BR, OFF_W, OFF_SCR = 0, UT_B, UT_B + BR_B, UT_B + BR_B + W_B

        def view(off, nbytes, dtype, pat=None, **kw):
            assert off % 4 == 0 and nbytes % 4 == 0
            a = ar[:, off // 4:(off + nbytes) // 4]
            if dtype == BF16:
                a = a.bitcast(BF16)
            if pat:
                a = a.rearrange(pat, **kw)
            return a

        class Carver:
            def __init__(self, off, size):
                self.off = off
                self.end = off + size

            def take(self, nbytes, dtype, pat=None, **kw):
                nbytes = (nbytes + 3) // 4 * 4
                assert self.off + nbytes <= self.end, ("carver overflow", self.off + nbytes - self.end)
                v = view(self.off, nbytes, dtype, pat, **kw)
                self.off += nbytes
                return v

        SMAX = 2048
        uT = view(OFF_UT, 8 * SMAX * 2, BF16, "p (k t) -> p k t", k=8)[:, :, 0:S]
        brT = view(OFF_BR, 12 * SMAX * 2, BF16, "p (k t) -> p k t", k=12)[:, :, 0:S]
        BR_FOX, BR_MEM, BR_RWK = 0, 4, 8
        u_uT = P.unit("uT")
        u_br = [P.unit(f"brT{i}") for i in range(3)]

        banks = [P.ps(f"pb{i}", [128, 512], F32) for i in range(8)]
        ub = [P.unit(f"pb{i}", psum=True) for i in range(8)]
        bar_scr = P.sb("barscr", [128, 2], F32)

        class Rot:
            def __init__(self, ids):
                self.ids = list(ids)
                self.i = 0

            def nxt(self):
                b = self.ids[self.i % len(self.ids)]
                self.i += 1
                return b

        identb = cstb[:, CST_IDENT:CST_IDENT + 128]
        identf = cst[:, CST_IDENT:CST_IDENT + 128]

        add(SP, lambda e: e.dma_start(out=cst[:], in_=cst_d[:, :]), writes=[u_cst], dma=u_cst)
        add(SP, lambda e: e.dma_start(out=pv[:], in_=pv_d[:, :]), writes=[u_pv], dma=u_pv)
        add(DVE, lambda e: e.tensor_copy(out=cstb[:], in_=cst[:]), reads=[u_cst], writes=[u_cst])
        add(DVE, lambda e: e.tensor_scalar(out=pvx[:, 0:4], in0=pv[:, PV_KA:PV_KA + 4], scalar1=-1.0, scalar2=1.0,
                                           op0=ALU.mult, op1=ALU.add), reads=[u_pv], writes=[u_pv])
        add(DVE, lambda e: e.tensor_scalar(out=pvx[:, 4:5], in0=pv[:, PV_FB:PV_FB + 1], scalar1=-1.0, scalar2=None,
                                           op0=ALU.mult), reads=[u_pv], writes=[u_pv])
        add(DVE, lambda e: e.tensor_scalar(out=pvx[:, 8:12], in0=pv[:, PV_W0:PV_W0 + 4], scalar1=0.5, scalar2=None,
                                           op0=ALU.mult), reads=[u_pv], writes=[u_pv])
        add(DVE, lambda e: e.tensor_scalar(out=pvx[:, 12:16], in0=pv[:, PV_A0:PV_A0 + 4], scalar1=0.5, scalar2=None,
                                           op0=ALU.mult), reads=[u_pv], writes=[u_pv])

        def wload(dst, src, unit):
            add(POOL, lambda e: e.dma_start(out=dst, in_=src), writes=[unit], dma=unit)

        def mk_scr(name, ncols):
            return nc.dram_tensor(name, [128, ncols], BF16).ap(), P.unit("S" + name, nobar=True)

        S_rw, uS_rw = mk_scr("s_rw", 8 * RW_COLS)
        S_fx, uS_fx = mk_scr("s_fx", 8 * FOX_COLS)
        S_mq, uS_mq = mk_scr("s_mq", 8 * 512)
        S_kv, uS_kv = mk_scr("s_kv", 8 * 1024)
        S_bo, uS_bo = mk_scr("s_bo", 12 * 1024)
        S_gt, uS_gt = mk_scr("s_gt", 8 * 8 * 3 * 128)
        S_o, uS_o = mk_scr("s_o", 8 * 1024)
        S_dn, uS_dn = mk_scr("s_dn", NFC * 1024)
        S_ff, uS_ff = mk_scr("s_ff", NFC * 8 * 2 * 128)
        stg = [P.sb(f"stg{i}", [128, 4096], BF16) for i in range(2)]
        u_stg = [P.unit(f"stg{i}", nobar=True) for i in range(2)]
        stg_n = [0]

        def stage(scr, u_scr, col0, ncols, parts):
            sidx = stg_n[0] % 2
            stg_n[0] += 1
            for (dv, src) in parts:
                add(POOL, lambda e: e.dma_start(out=dv(stg[sidx]), in_=src), writes=[u_stg[sidx]], dma=u_stg[sidx])
            add(SP, lambda e: e.dma_start(out=scr[:, col0:col0 + ncols], in_=stg[sidx][:, 0:ncols]), reads=[u_stg[sidx]], writes=[u_scr], dma=u_scr)

        def kp(ap_):
            return ap_.rearrange("(k p) n -> p k n", p=128)

        def stage_first():
            for k0 in range(0, 8, 2):
                stage(S_rw, uS_rw, k0 * RW_COLS, 2 * RW_COLS,
                      [(lambda t: t[:, 0:2 * RW_COLS].rearrange("p (k n) -> p k n", k=2), kp(w_in_d[k0 * 128:(k0 + 2) * 128, RW0:RW0 + RW_COLS]))])

        def stage_rest():
            for k0 in range(0, 8, 2):
                stage(S_fx, uS_fx, k0 * FOX_COLS, 2 * FOX_COLS,
                      [(lambda t: t[:, 0:2 * FOX_COLS].rearrange("p (k n) -> p k n", k=2), kp(w_in_d[k0 * 128:(k0 + 2) * 128, 0:FOX_COLS]))])
            stage(S_mq, uS_mq, 0, 4096, [(lambda t: t[:, 0:4096].rearrange("p (k n) -> p k n", k=8), kp(w_in_d[:, MQ0:MQ0 + 512]))])
            for k0 in range(0, 8, 4):
                stage(S_kv, uS_kv, k0 * 1024, 4096, [(lambda t: t[:, 0:4096].rearrange("p (k n) -> p k n", k=4), kp(w_kv_d[k0 * 128:(k0 + 4) * 128, :]))])
            for r, wd in enumerate((w_fo_d, w_ro_d, w_mo_d)):
                stage(S_bo, uS_bo, r * 4096, 4096, [(lambda t: t[:, 0:4096].rearrange("p (k n) -> p k n", k=4), kp(wd[:, :]))])
            for oc in range(8):
                parts = []
                for r in range(3):
                    gc0 = GT0 + r * 1024 + oc * 128
                    parts.append((lambda t, r=r: t[:, 0:3072].rearrange("p (k r n) -> p k r n", k=8, r=3)[:, :, r, :], kp(w_in_d[:, gc0:gc0 + 128])))
                stage(S_gt, uS_gt, oc * 3072, 3072, parts)
            for k0 in range(0, 8, 4):
                stage(S_o, uS_o, k0 * 1024, 4096, [(lambda t: t[:, 0:4096].rearrange("p (k n) -> p k n", k=4), kp(w_o_d[k0 * 128:(k0 + 4) * 128, :]))])
            for f0 in range(0, NFC, 4):
                nf = min(4, NFC - f0)
                stage(S_dn, uS_dn, f0 * 1024, nf * 1024,
                      [(lambda t, nf=nf: t[:, 0:nf * 1024].rearrange("p (k n) -> p k n", k=nf), kp(w_fd_d[f0 * 128:(f0 + nf) * 128, :]))])
            for f0 in range(0, NFC, 2):
                parts = []
                for ff in range(2):
                    for r, wd in enumerate((w_fg_d, w_fu_d)):
                        parts.append((lambda t, ff=ff, r=r: t[:, 0:4096].rearrange("p (f k r n) -> p f k r n", f=2, k=8, r=2)[:, ff, :, r, :],
                                      kp(wd[:, (f0 + ff) * 128:(f0 + ff + 1) * 128])))
                stage(S_ff, uS_ff, f0 * 2048, 4096, parts)

        def sload(dst_flat, scr, col0, ncols, u_dst, u_scr):
            add(SP, lambda e: e.dma_start(out=dst_flat, in_=scr[:, col0:col0 + ncols]), reads=[u_scr], writes=[u_dst], dma=u_dst)

        stage_first()

        def rms_T(cv, src_rows, ntiles, gcol, dstT, u_dst, bank_ids, tag):
            xin = [cv.take(4096, F32) for _ in range(2)]
            xn = [cv.take(2048, BF16) for _ in range(2)]
            junk = cv.take(2048, BF16)
            stt = [cv.take(16, F32) for _ in range(2)]
            u_x = [P.unit(f"{tag}x{i}") for i in range(2)]
            u_xn = [P.unit(f"{tag}xn{i}") for i in range(2)]
            u_s = [P.unit(f"{tag}s{i}") for i in range(2)]
            u_j = P.unit(f"{tag}j")
            for tt in range(ntiles):
                s = tt % 2
                bk = bank_ids[tt % len(bank_ids)]
                add(SP, lambda e, s=s, tt=tt: e.dma_start(out=xin[s], in_=src_rows(tt)), writes=[u_x[s]], dma=u_x[s])
                add(ACT, lambda e, s=s: e.activation(out=junk, in_=xin[s], func=AF.Square, accum_out=stt[s][:, 0:1]),
                    reads=[u_x[s]], writes=[u_j, u_s[s]])
                add(ACT, lambda e, s=s: e.activation(out=stt[s][:, 1:2], in_=stt[s][:, 0:1], func=AF.Sqrt,
                                                     scale=1.0 / D, bias=NORM_EPS), reads=[u_s[s]], writes=[u_s[s]])
                add(DVE, lambda e, s=s: e.reciprocal(out=stt[s][:, 2:3], in_=stt[s][:, 1:2]), reads=[u_s[s]], writes=[u_s[s]])
                add(DVE, lambda e, s=s: e.tensor_scalar(out=xn[s], in0=xin[s], scalar1=stt[s][:, 2:3], scalar2=None,
                                                        op0=ALU.mult), reads=[u_x[s], u_s[s]], writes=[u_xn[s]])
                pbf = banks[bk].bitcast(BF16)
                for c in range(8):
                    add(PE, lambda e, c=c, s=s, pbf=pbf: e.transpose(pbf[:, c * 128:(c + 1) * 128], xn[s][:, c * 128:(c + 1) * 128], identb),
                        reads=[u_xn[s], u_cst], writes=[ub[bk]])
                add(DVE, lambda e, tt=tt, pbf=pbf: e.tensor_tensor(
                    out=dstT[:, :, tt * 128:(tt + 1) * 128],
                    in0=pbf[:, 0:1024].rearrange("p (k t) -> p k t", k=8),
                    in1=pv[:, gcol:gcol + 8].unsqueeze(2).broadcast_to([128, 8, 128]), op=ALU.mult),
                    reads=[ub[bk], u_pv], writes=[u_dst])

        def proj_fm(w_tile, u_w, col0, rhs_fn, u_rhs, ntok, bk, nk=8, m=128):
            for k in range(nk):
                add(PE, lambda e, k=k: e.matmul(banks[bk][0:m, 0:ntok], lhsT=w_tile[:, k, col0:col0 + m], rhs=rhs_fn(k),
                                                start=(k == 0), stop=(k == nk - 1)),
                    reads=[u_w] + list(u_rhs), writes=[ub[bk]])

        for b in range(NB):
            P.barrier(bar_scr[:, 0:1])
            cv = Carver(OFF_SCR, SCR_B)
            rms_T(cv, lambda tt: x_d[b, tt * 128:(tt + 1) * 128, :], NT, PV_PRE1, uT, u_uT, [0, 1], f"A{b}")
            if dbg and b == 0:
                P.barrier(bar_scr[:, 0:1])
                cvd = Carver(OFF_SCR, SCR_B)
                dtmp = cvd.take(S * 4, F32)
                u_d = P.unit("dbgA")
                for kk_ in range(8):
                    add(DVE, lambda e: e.tensor_copy(out=dtmp, in_=uT[:, kk_, :]), reads=[u_uT], writes=[u_d])
                    add(SP, lambda e: e.dma_start(out=dbg_d["d_uT"][:, kk_ * S:(kk_ + 1) * S], in_=dtmp), reads=[u_d], dma=u_d, is_out=True)

            P.barrier(bar_scr[:, 0:1])
            cw = Carver(OFF_W, W_B)
            w_rw = cw.take(8 * RW_COLS * 2, BF16, "p (k n) -> p k n", k=8)
            lu = cw.take(3 * 512 * 2, BF16, "p (k n) -> p k n", k=3)
            u_wrw = P.unit(f"wrw{b}")
            u_lu = P.unit(f"lu{b}")
            rkv2 = [cw.take(TQ * 4, F32) for _ in range(3)]
            u_rkv2 = [P.unit(f"rkv2{b}{i}") for i in range(3)]
            bonp = [None, cw.take(TQ * 4, F32)]
            gbp = [None, cw.take(TQ * 2, BF16)]
            gcxp = [None, None]
            u_bonp = [None, P.unit(f"bon1{b}")]
            u_gbp = [None, P.unit(f"gb1{b}")]
            u_gcp = [None, P.unit(f"gc1{b}")]
            H2d = cw.take(8 * 128 * 2, BF16, "p (c n) -> p c n", c=8)
            YMd = cw.take(8 * 128 * 2, BF16, "p (c n) -> p c n", c=8)
            uH2d = [P.unit(f"h2d{b}{g}") for g in range(2)]
            uYMd = [P.unit(f"ymd{b}{g}") for g in range(2)]
            YAd = cw.take(TQ * 4, F32)
            u_YAd = P.unit(f"yad{b}")
            stage2_q = []
            bgq = []
            pend_gn = [None]

            def tick(n=1):
                P.run_deferred(bgq, n)

            def pump(n=1):
                for _ in range(n):
                    if stage2_q:
                        stage2_q.pop(0)()
            sload(w_rw.rearrange("p k n -> p (k n)"), S_rw, 0, 8 * RW_COLS, u_wrw, uS_rw)
            wload(lu[0:64, 0, :], wup_d[:, :], u_lu)
            wload(lu[64:128, 1, :], aup_d[:, :], u_lu)
            wload(lu[:, 2, :], gup_d[:, :], u_lu)

            cv = Carver(OFF_SCR, SCR_B)
            cv2 = Carver(OFF_BR, 8 * SMAX * 2)
            l12 = cv2.take(S * 2, BF16)
            l13 = cv2.take(S * 2, BF16)
            u_l = P.unit(f"l{b}")
            gnt = cv.take(4 * 2 * 64 * 4, F32, "p (a c i) -> p a c i", a=4, c=2)
            u_gn = P.unit(f"gn{b}")
            add(SP, lambda e: e.dma_start(out=gnt.rearrange("p a c i -> p (a c i)"), in_=gnp_d[:, :]), writes=[u_gn], dma=u_gn)
            if b == 0:
                stage_rest()
            rmask = cv.take(TQ * 4, F32)
            u_rm = P.unit(f"rm{b}")
            add(DVE, lambda e: e.memset(rmask, 1.0), writes=[u_rm])
            add(DVE, lambda e: e.memset(rmask.rearrange("p (c t) -> p c t", t=64)[:, :, 0:1], 0.0), writes=[u_rm])
            praw = [cv.take((TQ + 2) * 4, F32) for _ in range(3)]
            u_praw = [P.unit(f"praw{b}{i}") for i in range(3)]
            NSC = 12
            sc = [cv.take(TQ * 4, F32) for _ in range(NSC)]
            u_sc = [P.unit(f"sc{b}{i}") for i in range(NSC)]
            scb = [cv.take(TQ * 2, BF16) for _ in range(3)]
            u_scb = [P.unit(f"scb{b}{i}") for i in range(3)]
            NBD = 7
            bd = [cv2.take(8 * 128 * 2, BF16, "p (c n) -> p c n", c=8) for _ in range(NBD)]
            u_bd = [P.unit(f"bd{b}{i}") for i in range(NBD)]
            for i in range(NBD):
                add(DVE, lambda e, i=i: e.memset(bd[i].rearrange("p c n -> p (c n)"), 0.0), writes=[u_bd[i]])
            ybd = cv2.take(8 * 128 * 2, BF16, "p (c n) -> p c n", c=8)
            u_ybd = P.unit(f"ybd{b}")
            add(DVE, lambda e: e.memset(ybd.rearrange("p c n -> p (c n)"), 0.0), writes=[u_ybd])
            gcx = cv.take(8 * 4, F32)
            u_gc = P.unit(f"gc{b}")
            TMd = cv.take(TQ * 4, F32)
            u_TMd = P.unit(f"tmd{b}")
            gcxp[1] = cv.take(8 * 4, F32)
            bonp[0], gbp[0], gcxp[0] = sc[11], scb[0], gcx
            u_bonp[0], u_gbp[0], u_gcp[0] = u_sc[11], u_scb[0], u_gc
            def carr(cvx):
                return cvx.take(8 * 128 * 2, BF16, "p (c n) -> p c n", c=8)
            Wt, Lt, Xt, LAKt, PRBt, PRKt, ATMt, BHTt = [carr(cv) for _ in range(8)]
            KHTt, YVt = carr(cv2), carr(cv2)
            VSt = cv2.take(8 * 64 * 2, BF16, "p (c i) -> p c i", c=8)
            uW, uL, uX, uLAK, uPRB, uPRK, uATM, uBHT, uKHT, uYV = [[P.unit(f"ca{b}{n_}{g}") for g in range(2)] for n_ in range(10)]
            u_VS = P.unit(f"VS{b}")
            Mf = cv.take(64 * 4, F32)
            Mb = [cv.take(64 * 2, BF16) for _ in range(2)]
            u_M = P.unit(f"M{b}")
            u_Mb = [P.unit(f"Mb{b}{i}") for i in range(2)]
            gst = cv.take(8 * 8 * 4, F32, "p (a c) -> p a c", a=8)
            u_gst = P.unit(f"gst{b}")

            rp = Rot([0, 1])
            rs = Rot([3, 4, 5, 6, 7])
            rsY = Rot([2])
            rsM = Rot([4, 5, 6, 7])

            def proj_lerp(cc, tq, pslot, dst, u_dst):
                bk = rp.nxt()
                proj_fm(w_rw, u_wrw, cc * 128, lambda k: uT[:, k, tq * TQ:(tq + 1) * TQ], [u_uT], TQ, bk)
                pr = praw[pslot]
                up = u_praw[pslot]
                if tq == 0:
                    add(DVE, lambda e: e.memset(pr[:, 0:1], 0.0), writes=[up])
                else:
                    add(DVE, lambda e: e.tensor_copy(out=pr[:, 0:1], in_=pr[:, TQ:TQ + 1]), reads=[up], writes=[up])
                add(ACT, lambda e: e.copy(out=pr[:, 1:TQ + 1], in_=banks[bk][:, 0:TQ]), reads=[ub[bk]], writes=[up])
                add(DVE, lambda e: e.tensor_tensor(out=dst, in0=pr[:, 0:TQ], in1=pr[:, 1:TQ + 1], op=ALU.subtract),
                    reads=[up], writes=[u_dst])
                add(DVE, lambda e: e.scalar_tensor_tensor(out=dst, in0=dst, scalar=pv[:, PV_MU + cc:PV_MU + cc + 1],
                                                          in1=pr[:, 1:TQ + 1], op0=ALU.mult, op1=ALU.add),
                    reads=[up, u_dst, u_pv], writes=[u_dst])

            for tq in range(NQ):
                tsl = slice(tq * TQ, (tq + 1) * TQ)
                proj_lerp(12, tq, 0, sc[0], u_sc[0])
                add(ACT, lambda e, tsl=tsl: e.activation(out=l12[0:64, tsl], in_=sc[0][0:64, :], func=AF.Tanh),
                    reads=[u_sc[0]], writes=[u_l])
                add(ACT, lambda e, tsl=tsl: e.copy(out=l12[64:128, tsl], in_=sc[0][64:128, :]), reads=[u_sc[0]], writes=[u_l])
                proj_lerp(13, tq, 1, sc[1], u_sc[1])
                add(ACT, lambda e, tsl=tsl: e.activation(out=l13[:, tsl], in_=sc[1], func=AF.Sigmoid),
                    reads=[u_sc[1]], writes=[u_l])

            msu = cst[:, CST_SU:CST_SU + 128]
            msl = cst[:, CST_SL:CST_SL + 128]
            mui = cst[:, CST_UI:CST_UI + 128]
            bones = cstb[:, CST_BONES:CST_BONES + 128]
            stk = cstb[:, CST_STK:CST_STK + 64]

            for hp in range(4):
                for tq in range(NQ):
                    tsl = slice(tq * TQ, (tq + 1) * TQ)
                    tidx = hp * NQ + tq
                    _, _, _, SG, A_, KK, T1, BV, CS, CP, DE, BON = sc
                    _, _, _, uSG, uA, uKK, uT1, uBV, uCS, uCP, uDE, uBON = u_sc

                    def rkv_bufs(ti):
                        if ti % 2 == 0:
                            return (sc[0], sc[1], sc[2]), (u_sc[0], u_sc[1], u_sc[2])
                        return tuple(rkv2), tuple(u_rkv2)

                    (R_, K_, V_), (uR, uK, uV) = rkv_bufs(tidx)

                    def emit_proj(ti, which):
                        hp_, tq_ = ti // NQ, ti % NQ
                        bufs, us = rkv_bufs(ti)
                        proj_lerp(which * 4 + hp_, tq_, which, bufs[which], us[which])

                    if tidx == 0:
                        for w_ in range(3):
                            emit_proj(0, w_)
                    hs = slice(hp * 128, (hp + 1) * 128)
                    _, SQb, RKb = scb
                    _, uSQ, uRK = u_scb
                    par = tidx % 2
                    BON, uBON, Gb, uG, gcx, u_gc = bonp[par], u_bonp[par], gbp[par], u_gbp[par], gcxp[par], u_gcp[par]
                    pump()
                    def prep(ti):
                        hp_, tq_ = ti // NQ, ti % NQ
                        tsl_ = slice(tq_ * TQ, (tq_ + 1) * TQ)
                        hs_ = slice(hp_ * 128, (hp_ + 1) * 128)
                        pr_ = ti % 2
                        (R_, K_, V_), (uR, uK, uV) = rkv_bufs(ti)
                        BON, uBON, Gb, uG, gcx, u_gc = bonp[pr_], u_bonp[pr_], gbp[pr_], u_gbp[pr_], gcxp[pr_], u_gcp[pr_]
                        add(DVE, lambda e: e.tensor_scalar(out=KK, in0=K_, scalar1=pv[:, PV_KK + hp_:PV_KK + hp_ + 1], scalar2=None, op0=ALU.mult),
                            reads=[uK, u_pv], writes=[uKK])
                        add(DVE, lambda e: e.tensor_tensor(out=SQb, in0=KK, in1=KK, op=ALU.mult), reads=[uKK], writes=[uSQ])
                        P.atomic_begin()
                        bs = rp.nxt()
                        add(PE, lambda e, bs=bs: e.matmul(banks[bs][:, 0:TQ], lhsT=bones, rhs=SQb, start=True, stop=True),
                            reads=[u_cst, uSQ], writes=[ub[bs]])
                        add(ACT, lambda e, bs=bs: e.activation(out=T1, in_=banks[bs][:, 0:TQ], func=AF.Sqrt), reads=[ub[bs]], writes=[uT1])
                        P.atomic_end()
                        for (li_, rows_, rhs_, is_g) in ((0, slice(0, 64), l12, False), (1, slice(64, 128), l12, False), (2, slice(0, 128), l13, True)):
                            P.atomic_begin()
                            bq = rp.nxt()
                            add(PE, lambda e: e.matmul(banks[bq][:, 0:TQ], lhsT=lu[rows_, li_, hs_], rhs=rhs_[rows_, tsl_], start=True, stop=True),
                                reads=[u_lu, u_l], writes=[ub[bq]])
                            if li_ == 0:
                                add(ACT, lambda e: e.activation(out=SG, in_=banks[bq][:, 0:TQ], func=AF.Tanh, scale=0.5, bias=pvx[:, 8 + hp_:9 + hp_]),
                                    reads=[ub[bq], u_pv], writes=[uSG])
                            elif li_ == 1:
                                add(ACT, lambda e: e.activation(out=A_, in_=banks[bq][:, 0:TQ], func=AF.Tanh, scale=0.5, bias=pvx[:, 12 + hp_:13 + hp_]),
                                    reads=[ub[bq], u_pv], writes=[uA])
                            else:
                                add(ACT, lambda e: e.copy(out=Gb, in_=banks[bq][:, 0:TQ]), reads=[ub[bq]], writes=[uG])
                            P.atomic_end()
                        add(DVE, lambda e: e.tensor_scalar(out=SG, in0=SG, scalar1=0.5, scalar2=0.5, op0=ALU.mult, op1=ALU.add), reads=[uSG], writes=[uSG])
                        add(DVE, lambda e: e.tensor_scalar(out=A_, in0=A_, scalar1=0.5, scalar2=0.5, op0=ALU.mult, op1=ALU.add), reads=[uA], writes=[uA])
                        add(DVE, lambda e: e.tensor_scalar(out=T1, in0=T1, scalar1=1e-12, scalar2=None, op0=ALU.max), reads=[uT1], writes=[uT1])
                        add(DVE, lambda e: e.reciprocal(out=T1, in_=T1), reads=[uT1], writes=[uT1])
                        add(DVE, lambda e: e.tensor_tensor(out=KK, in0=KK, in1=T1, op=ALU.mult), reads=[uKK, uT1], writes=[uKK])
                        add(DVE, lambda e: e.tensor_scalar(out=T1, in0=A_, scalar1=pv[:, PV_KA + hp_:PV_KA + hp_ + 1], scalar2=pvx[:, hp_:hp_ + 1],
                                                           op0=ALU.mult, op1=ALU.add), reads=[uA, u_pv, uT1], writes=[uT1])
                        add(DVE, lambda e: e.tensor_tensor(out=K_, in0=K_, in1=T1, op=ALU.mult), reads=[uK, uT1], writes=[uK])
                        add(DVE, lambda e: e.tensor_tensor(out=BV, in0=KK, in1=A_, op=ALU.mult), reads=[uKK, uA], writes=[uBV])
                        add(DVE, lambda e: e.scalar_tensor_tensor(out=RKb, in0=R_, scalar=pv[:, PV_RK + hp_:PV_RK + hp_ + 1], in1=K_,
                                                                  op0=ALU.mult, op1=ALU.mult), reads=[uR, uK, u_pv], writes=[uRK])
                        P.atomic_begin()
                        bb = rp.nxt()
                        add(PE, lambda e, bb=bb: e.matmul(banks[bb][:, 0:TQ], lhsT=bones, rhs=RKb, start=True, stop=True),
                            reads=[u_cst, uRK], writes=[ub[bb]])
                        add(DVE, lambda e, bb=bb: e.tensor_tensor(out=BON, in0=banks[bb][:, 0:TQ], in1=V_, op=ALU.mult),
                            reads=[ub[bb], uV], writes=[uBON])
                        P.atomic_end()
                        add(DVE, lambda e: e.tensor_tensor_scan(out=CS, data0=rmask, data1=SG, initial=0.0, op0=ALU.mult, op1=ALU.add),
                            reads=[u_rm, uSG], writes=[uCS])
                        add(DVE, lambda e: e.tensor_tensor(out=CP, in0=CS, in1=SG, op=ALU.subtract), reads=[uCS, uSG], writes=[uCP])
                        CS3 = CS.rearrange("p (c t) -> p c t", t=64)
                        add(DVE, lambda e: e.tensor_tensor(out=DE.rearrange("p (c t) -> p c t", t=64),
                                                           in0=CS3[:, :, 63:64].broadcast_to([128, 8, 64]), in1=CS3, op=ALU.subtract),
                            reads=[uCS], writes=[uDE])
                        add(ACT, lambda e: e.activation(out=gcx.unsqueeze(2), in_=CS3[:, :, 63:64], func=AF.Exp, scale=-C0),
                            reads=[uCS], writes=[u_gc])
                        add(ACT, lambda e: e.activation(out=SG, in_=CS, func=AF.Exp, scale=-C0), reads=[uCS, uCP], writes=[uSG])
                        add(ACT, lambda e: e.activation(out=T1, in_=CS, func=AF.Exp, scale=C0), reads=[uCS, uK], writes=[uT1])
                        add(ACT, lambda e: e.activation(out=CP, in_=CP, func=AF.Exp, scale=-C0), reads=[uCP], writes=[uCP])
                        add(ACT, lambda e: e.activation(out=DE, in_=DE, func=AF.Exp, scale=-C0), reads=[uDE], writes=[uDE])

                    if tidx == 0:
                        prep(0)
                    EP, EN, EPP, EE = SG, T1, CP, DE
                    uEP, uEN, uEPP, uEE = uSG, uT1, uCP, uDE
                    bdR, bdK, bdB, bdA, bdBH, bdKH, bdV = bd
                    uBR, uBK, uBB, uBA, uBBH, uBKH, uBVv = u_bd
                    for hh in range(2):
                        pump(2)
                        ps_ = slice(hh * 64, hh * 64 + 64)

                        def bdo(t):
                            return t[ps_, :, hh * 64:hh * 64 + 64]

                        def src(t):
                            return t[ps_, :].rearrange("p (c t) -> p c t", t=64)

                        add(DVE, lambda e, bdo=bdo, src=src: e.tensor_tensor(out=bdo(bdR), in0=src(R_), in1=src(EP), op=ALU.mult),
                            reads=[uR, uEP], writes=[uBR])
                        add(DVE, lambda e, bdo=bdo, src=src: e.tensor_tensor(out=bdo(bdK), in0=src(K_), in1=src(EN), op=ALU.mult),
                            reads=[uK, uEN], writes=[uBK])
                        add(DVE, lambda e, bdo=bdo, src=src: e.tensor_tensor(out=bdo(bdB), in0=src(BV), in1=src(EN), op=ALU.mult),
                            reads=[uBV, uEN], writes=[uBB])
                        add(DVE, lambda e, bdo=bdo, src=src: e.scalar_tensor_tensor(out=bdo(bdA), in0=src(KK), scalar=-1.0, in1=src(EPP),
                                                                                    op0=ALU.mult, op1=ALU.mult),
                            reads=[uKK, uEPP], writes=[uBA])
                        add(DVE, lambda e, bdo=bdo, src=src: e.tensor_tensor(out=bdo(bdBH), in0=src(BV), in1=src(EE), op=ALU.mult),
                            reads=[uBV, uEE], writes=[uBBH])
                        add(DVE, lambda e, bdo=bdo, src=src: e.tensor_tensor(out=bdo(bdKH), in0=src(K_), in1=src(EE), op=ALU.mult),
                            reads=[uK, uEE], writes=[uBKH])
                        add(ACT, lambda e, bdo=bdo, src=src: e.copy(out=bdo(bdV), in_=src(V_)), reads=[uV], writes=[uBVv])

                    def b4(bk):
                        return banks[bk][:, 0:512].rearrange("p (c n) -> p c n", c=4)

                    def g4(t, g):
                        return t[:, g * 4:(g + 1) * 4, :]

                    def mm4(bk, g, lt, ult, rt, urt, plus=None):
                        for cc in range(4):
                            c = g * 4 + cc
                            if plus is not None:
                                add(PE, lambda e: e.matmul(banks[bk][:, cc * 128:(cc + 1) * 128], lhsT=identb, rhs=plus[0][:, c, :], start=True, stop=False),
                                    reads=[u_cst, plus[1]], writes=[ub[bk]])
                            add(PE, lambda e: e.matmul(banks[bk][:, cc * 128:(cc + 1) * 128], lhsT=lt[:, c, :], rhs=rt[:, c, :],
                                                       start=(plus is None), stop=True),
                                reads=[ult, urt], writes=[ub[bk]])

                    evn = [0]

                    def evcopy(dst, bk, udst):
                        evn[0] += 1
                        if evn[0] % 2 == 0:
                            add(ACT, lambda e: e.copy(out=dst, in_=b4(bk)), reads=[ub[bk]], writes=[udst])
                        else:
                            add(DVE, lambda e: e.tensor_copy(out=dst, in_=b4(bk)), reads=[ub[bk]], writes=[udst])
                        tick()

                    def score(dst, udst, lt, ult, rt, urt, mask):
                        for g in range(2):
                            bk = rs.nxt()
                            mm4(bk, g, lt, ult, rt, urt)
                            add(DVE, lambda e: e.tensor_tensor(out=g4(dst, g), in0=b4(bk), in1=mask.unsqueeze(1).broadcast_to([128, 4, 128]), op=ALU.mult),
                                reads=[ub[bk], u_cst], writes=[udst[g]])

                    score(Wt, uW, bdB, uBB, bdA, uBA, msu)
                    pump()
                    score(Lt, uL, bdA, uBA, bdB, uBB, msl)
                    pump()
                    for g in range(2):
                        add(DVE, lambda e: e.tensor_tensor(out=g4(Xt, g), in0=g4(Wt, g), in1=identb.unsqueeze(1).broadcast_to([128, 4, 128]), op=ALU.add),
                            reads=[uW[g], u_cst], writes=[uX[g]])
                    score(LAKt, uLAK, bdA, uBA, bdK, uBK, msl)
                    pump()
                    score(PRBt, uPRB, bdB, uBB, bdR, uBR, mui)
                    pump()
                    score(PRKt, uPRK, bdK, uBK, bdR, uBR, mui)
                    pump(99)
                    if pend_gn[0] is not None:
                        P.defer = bgq
                        pend_gn[0]()
                        P.defer = None
                        pend_gn[0] = None
                    if tidx + 1 < 4 * NQ:
                        for w_ in range(3):
                            emit_proj(tidx + 1, w_)
                        P.defer = bgq
                        prep(tidx + 1)
                        P.defer = None
                    for lev in range(5):
                        bl = [rs.nxt(), rs.nxt()]
                        bw2 = [rs.nxt(), rs.nxt()] if lev < 4 else None
                        for g in range(2):
                            mm4(bl[g], g, Wt, uW[g], Lt, uL[g])
                            if lev < 4:
                                mm4(bw2[g], g, Lt, uL[g], Wt, uW[g])
                        for g in range(2):
                            add(ACT, lambda e: e.copy(out=g4(Lt, g), in_=b4(bl[g])), reads=[ub[bl[g]]], writes=[uL[g]])
                            tick()
                            if lev < 4:
                                if (lev + g) % 2 == 0:
                                    add(ACT, lambda e: e.copy(out=g4(Wt, g), in_=b4(bw2[g])), reads=[ub[bw2[g]]], writes=[uW[g]])
                                else:
                                    add(DVE, lambda e: e.tensor_copy(out=g4(Wt, g), in_=b4(bw2[g])), reads=[ub[bw2[g]]], writes=[uW[g]])
                        bx = [rs.nxt(), rs.nxt()]
                        for g in range(2):
                            mm4(bx[g], g, Lt, uL[g], Xt, uX[g], plus=(Xt, uX[g]))
                        for g in range(2):
                            evcopy(g4(Xt, g), bx[g], uX[g])
                    for (srcbd, usrc, dstt, udst) in ((bdA, uBA, ATMt, uATM), (bdBH, uBBH, BHTt, uBHT), (bdKH, uBKH, KHTt, uKHT)):
                        bk = rs.nxt()
                        pbf = banks[bk].bitcast(BF16)
                        for c in range(8):
                            add(PE, lambda e: e.transpose(pbf[:, c * 128:(c + 1) * 128], srcbd[:, c, :], identb), reads=[usrc, u_cst], writes=[ub[bk]])
                        add(ACT, lambda e: e.copy(out=dstt.rearrange("p c n -> p (c n)"), in_=pbf[:, 0:1024]), reads=[ub[bk]], writes=udst)
                        tick()
                    bk = rs.nxt()
                    for c in range(8):
                        add(PE, lambda e: e.matmul(banks[bk][:, c * 64:(c + 1) * 64], lhsT=bdV[:, c, :], rhs=stk, start=True, stop=True),
                            reads=[uBVv, u_cst], writes=[ub[bk]])
                    add(ACT, lambda e: e.copy(out=VSt.rearrange("p c i -> p (c i)"), in_=banks[bk][:, 0:512]), reads=[ub[bk]], writes=[u_VS])
                    bta = [rs.nxt(), rs.nxt()]
                    btk = [rs.nxt(), rs.nxt()]
                    for g in range(2):
                        mm4(bta[g], g, Xt, uX[g], ATMt, uATM[g])
                        mm4(btk[g], g, Xt, uX[g], LAKt, uLAK[g])
                    for g in range(2):
                        add(ACT, lambda e: e.copy(out=g4(Wt, g), in_=b4(bta[g])), reads=[ub[bta[g]]], writes=[uW[g]])
                        tick()
                        add(ACT, lambda e: e.copy(out=g4(Lt, g), in_=b4(btk[g])), reads=[ub[btk[g]]], writes=[uL[g]])
                        tick()
                    TAt, uTA, TKt, uTK = Wt, uW, Lt, uL
                    for g in range(2):
                        b1, b2, b3, b4_ = rs.nxt(), rs.nxt(), rs.nxt(), rs.nxt()
                        mm4(b1, g, TAt, uTA[g], BHTt, uBHT[g])
                        mm4(b2, g, TKt, uTK[g], BHTt, uBHT[g], plus=(KHTt, uKHT[g]))
                        mm4(b3, g, TAt, uTA[g], PRBt, uPRB[g], plus=(bdR, uBR))
                        mm4(b4_, g, TKt, uTK[g], PRBt, uPRB[g], plus=(PRKt, uPRK[g]))
                        evcopy(g4(ATMt, g), b1, uATM[g])
                        evcopy(g4(H2d, g), b2, uH2d[g])
                        evcopy(g4(YMd, g), b3, uYMd[g])
                        evcopy(g4(YVt, g), b4_, uYV[g])
                    G1t, uG1, H2t, uH2, YMt, uYM = ATMt, uATM, H2d, uH2d, YMd, uYMd
                    tick(10 ** 6)
                    def make_stage2(hp, tq, tsl, YA, uYA, TM, uTM, BON, uBON, Gb, uG, gcx, u_gc, G1t, uG1, H2t, uH2, YMt, uYM):
                        st = {}
                        steps = []

                        def chain_step(c):
                            def f():
                                if c == 0:
                                    st["bkY"] = rsY.nxt()
                                    if tq == 0:
                                        add(DVE, lambda e: e.memset(Mf, 0.0), writes=[u_M])
                                        add(DVE, lambda e: e.memset(Mb[0], 0.0), writes=[u_Mb[0]])
                                bkY = st["bkY"]
                                g = c // 4
                                mo = c % 2
                                bkM = rsM.nxt()
                                add(PE, lambda e: e.matmul(banks[bkM][:, 0:64], lhsT=G1t[:, c, :], rhs=Mb[mo], start=True, stop=False),
                                    reads=[uG1[g], u_Mb[mo]], writes=[ub[bkM]])
                                add(PE, lambda e: e.matmul(banks[bkM][:, 0:64], lhsT=H2t[:, c, :], rhs=VSt[:, c, :], start=False, stop=True),
                                    reads=[uH2[g], u_VS], writes=[ub[bkM]])
                                add(PE, lambda e: e.matmul(banks[bkY][:, c * 64:(c + 1) * 64], lhsT=YMt[:, c, :], rhs=Mb[mo], start=True, stop=False),
                                    reads=[uYM[g], u_Mb[mo]], writes=[ub[bkY]])
                                add(PE, lambda e: e.matmul(banks[bkY][:, c * 64:(c + 1) * 64], lhsT=YVt[:, c, :], rhs=VSt[:, c, :], start=False, stop=True),
                                    reads=[uYV[g], u_VS], writes=[ub[bkY]])
                                add(DVE, lambda e: e.scalar_tensor_tensor(out=Mb[1 - mo], in0=Mf, scalar=gcx[:, c:c + 1], in1=banks[bkM][:, 0:64],
                                                                          op0=ALU.mult, op1=ALU.add),
                                    reads=[u_M, u_gc, ub[bkM]], writes=[u_Mb[1 - mo]])
                                add(DVE, lambda e: e.scalar_tensor_tensor(out=Mf, in0=Mf, scalar=gcx[:, c:c + 1], in1=banks[bkM][:, 0:64],
                                                                          op0=ALU.mult, op1=ALU.add),
                                    reads=[u_M, u_gc, ub[bkM]], writes=[u_M])
                                if c == 7:
                                    add(ACT, lambda e: e.copy(out=YA, in_=banks[bkY][:, 0:512]), reads=[ub[bkY]], writes=[uYA])
                            return f

                        for c_ in range(8):
                            steps.append(chain_step(c_))

                        def gn_final():
                            YA3 = YA.rearrange("p (c i) -> p c i", i=64)
                            add(DVE, lambda e: e.tensor_reduce(out=gst[:, 0, :], in_=YA3, axis=AX.X, op=ALU.add), reads=[uYA], writes=[u_gst])
                            add(ACT, lambda e: e.activation(out=TM, in_=YA, func=AF.Square), reads=[uYA], writes=[uTM])
                            add(DVE, lambda e: e.tensor_reduce(out=gst[:, 1, :], in_=TM.rearrange("p (c i) -> p c i", i=64), axis=AX.X, op=ALU.add),
                                reads=[uTM, u_gst], writes=[u_gst])
                            add(DVE, lambda e: e.tensor_scalar(out=gst[:, 2, :], in0=gst[:, 0, :], scalar1=1.0 / 64, scalar2=None, op0=ALU.mult),
                                reads=[u_gst], writes=[u_gst])
                            add(DVE, lambda e: e.tensor_tensor(out=gst[:, 3, :], in0=gst[:, 2, :], in1=gst[:, 2, :], op=ALU.mult),
                                reads=[u_gst], writes=[u_gst])
                            add(DVE, lambda e: e.scalar_tensor_tensor(out=gst[:, 4, :], in0=gst[:, 1, :], scalar=1.0 / 64, in1=gst[:, 3, :],
                                                                      op0=ALU.mult, op1=ALU.subtract), reads=[u_gst], writes=[u_gst])
                            add(ACT, lambda e: e.activation(out=gst[:, 5, :], in_=gst[:, 4, :], func=AF.Sqrt, bias=GN_EPS), reads=[u_gst], writes=[u_gst])
                            add(DVE, lambda e: e.reciprocal(out=gst[:, 6, :], in_=gst[:, 5, :]), reads=[u_gst], writes=[u_gst])
                            add(DVE, lambda e: e.tensor_tensor(out=YA3, in0=YA3, in1=gst[:, 2, :].unsqueeze(2).broadcast_to([128, 8, 64]), op=ALU.subtract),
                                reads=[uYA, u_gst], writes=[uYA])
                            add(DVE, lambda e: e.tensor_tensor(out=YA3, in0=YA3, in1=gst[:, 6, :].unsqueeze(2).broadcast_to([128, 8, 64]), op=ALU.mult),
                                reads=[uYA, u_gst], writes=[uYA])
                            add(DVE, lambda e: e.tensor_tensor(out=YA3, in0=YA3, in1=gnt[:, hp, 0:1, :].broadcast_to([128, 8, 64]), op=ALU.mult),
                                reads=[uYA, u_gn], writes=[uYA])
                            for hh in range(2):
                                ps_ = slice(hh * 64, hh * 64 + 64)
                                add(DVE, lambda e, ps_=ps_, hh=hh: e.tensor_tensor(out=ybd[ps_, :, hh * 64:hh * 64 + 64], in0=YA3[ps_],
                                                                                   in1=gnt[ps_, hp, 1:2, :].broadcast_to([64, 8, 64]), op=ALU.add),
                                    reads=[uYA, u_gn], writes=[u_ybd])
                            P.atomic_begin()
                            bk = rp.nxt()
                            for c in range(8):
                                add(PE, lambda e, c=c, bk=bk: e.matmul(banks[bk][:, c * 64:(c + 1) * 64], lhsT=ybd[:, c, :], rhs=stk, start=True, stop=True),
                                    reads=[u_ybd, u_cst], writes=[ub[bk]])
                            add(DVE, lambda e, bk=bk: e.tensor_tensor(out=TM, in0=banks[bk][:, 0:TQ], in1=BON, op=ALU.add),
                                reads=[ub[bk], uBON], writes=[uTM])
                            P.atomic_end()
                            add(DVE, lambda e: e.tensor_tensor(out=brT[:, BR_RWK + hp, tsl], in0=TM, in1=Gb, op=ALU.mult),
                                reads=[uTM, uG], writes=[u_br[1]])

                        return steps, gn_final

                    steps_, gnf_ = make_stage2(hp, tq, tsl, YAd, u_YAd, TMd, u_TMd, BON, uBON, Gb, uG, gcx, u_gc, G1t, uG1, H2t, uH2, YMt, uYM)
                    stage2_q.extend(steps_)
                    assert pend_gn[0] is None
                    pend_gn[0] = gnf_

            pump(99)
            if pend_gn[0] is not None:
                pend_gn[0]()
                pend_gn[0] = None
            P.barrier(bar_scr[:, 0:1])
            cw = Carver(OFF_W, W_B)
            w_fx = cw.take(8 * FOX_COLS * 2, BF16, "p (k n) -> p k n", k=8)
            u_wfx = P.unit(f"wfx{b}")
            sload(w_fx.rearrange("p k n -> p (k n)"), S_fx, 0, 8 * FOX_COLS, u_wfx, uS_fx)
            cv = Carver(OFF_SCR, SCR_B)
            qa = [cv.take(S * 2, BF16) for _ in range(2)]
            ka = [cv.take(S * 2, BF16) for _ in range(2)]
            vaug = cv.take(NT * 2 * 65 * 2, BF16, "p (t h d) -> p t h d", t=NT, h=2)
            ftm = cv.take(NT * 128 * 2, BF16, "p (t n) -> p t n", t=NT)
            NPT = 4
            pT = [cv.take(4 * 128 * 2, BF16) for _ in range(NPT - 1)]
            u_qa = [P.unit(f"qa{b}{i}") for i in range(2)]
            u_ka = [P.unit(f"ka{b}{i}") for i in range(2)]
            u_v, u_ftm = P.unit(f"v{b}"), P.unit(f"ftm{b}")
            u_pT = [P.unit(f"pT{b}{i}") for i in range(NPT)]
            EL = cv.take(S * 4, F32)
            CPs = cv.take(S * 4, F32)
            QP = cv.take(3 * S * 2, BF16, "p (r s) -> p r s", r=3)
            cmem = Carver(OFF_BR + 4 * SMAX * 2, 4 * SMAX * 2)
            KP = cmem.take(3 * S * 2, BF16, "p (r s) -> p r s", r=3)
            pT.append(cmem.take(4 * 128 * 2, BF16))
            rden = cv.take(16, F32)
            u_EL, u_CP, u_KP, u_QP, u_rden = (P.unit(f"EL{b}"), P.unit(f"CP{b}"), P.unit(f"KP{b}"), P.unit(f"QP{b}"), P.unit(f"rden{b}"))
            rp = Rot([0, 1])
            for tq in range(NQ):
                tsl = slice(tq * TQ, (tq + 1) * TQ)
                bk = rp.nxt()
                proj_fm(w_fx, u_wfx, 1536, lambda k: uT[:, k, tsl], [u_uT], TQ, bk, m=8)
                add(ACT, lambda e, bk=bk, tsl=tsl: e.activation(out=EL[0:8, tsl], in_=banks[bk][0:8, 0:TQ], func=AF.Exp, scale=-1.0,
                                                                bias=pvx[0:8, 4:5]), reads=[ub[bk], u_pv], writes=[u_EL])
            add(ACT, lambda e: e.activation(out=EL[0:8, :], in_=EL[0:8, :], func=AF.Ln, bias=1.0), reads=[u_EL], writes=[u_EL])
            add(DVE, lambda e: e.tensor_tensor_scan(out=CPs[0:8, :], data0=EL[0:8, :], data1=EL[0:8, :], initial=0.0, op0=ALU.add, op1=ALU.max),
                reads=[u_EL], writes=[u_CP])
            def pieces(dst, u_dst):
                for r in range(3):
                    add(DVE, lambda e: e.tensor_copy(out=dst[0:8, r, :], in_=EL[0:8, :]), reads=[u_EL], writes=[u_dst])
                    if r < 2:
                        add(DVE, lambda e: e.tensor_tensor(out=EL[0:8, :], in0=EL[0:8, :], in1=dst[0:8, r, :], op=ALU.subtract),
                            reads=[u_EL, u_dst], writes=[u_EL])

            add(DVE, lambda e: e.tensor_scalar(out=EL[0:8, :], in0=CPs[0:8, :], scalar1=8.0, scalar2=None, op0=ALU.mult), reads=[u_CP], writes=[u_EL])
            pieces(KP, u_KP)
            CP3 = CPs[0:8, :].rearrange("p (t n) -> p t n", n=128)
            add(DVE, lambda e: e.tensor_scalar(out=EL[0:8, :].rearrange("p (t n) -> p t n", n=128), in0=CP3[:, :, 127:128].broadcast_to([8, NT, 128]),
                                               scalar1=-8.0, scalar2=None, op0=ALU.mult), reads=[u_CP, u_EL], writes=[u_EL])
            pieces(QP, u_QP)
            for h2 in range(2):
                base = 64 if h2 == 0 else 0
                add(DVE, lambda e: e.memset(ka[h2][base:base + 64, :], 0.0), writes=[u_ka[h2]])
                add(DVE, lambda e: e.memset(ka[h2][base:base + 32, :], 1.0), writes=[u_ka[h2]])
                add(DVE, lambda e: e.memset(qa[h2][base:base + 64, :], 0.0), writes=[u_qa[h2]])
                add(DVE, lambda e: e.memset(qa[h2][base:base + 3, :], 1.0), writes=[u_qa[h2]])
            add(DVE, lambda e: e.memset(vaug[:, :, :, 64:65], 1.0), writes=[u_v])
            caus = cstb[:, CST_CAUS:CST_CAUS + 128]
            rsS = Rot([2, 3, 4, 5])
            rsO = Rot([6, 7])
            for hp in range(4):
                for tq in range(NQ):
                    tsl = slice(tq * TQ, (tq + 1) * TQ)
                    bk = rp.nxt()
                    proj_fm(w_fx, u_wfx, hp * 128, lambda k: uT[:, k, tsl], [u_uT], TQ, bk)
                    add(ACT, lambda e: e.copy(out=qa[0][0:64, tsl], in_=banks[bk][0:64, 0:TQ]), reads=[ub[bk]], writes=[u_qa[0]])
                    add(ACT, lambda e: e.copy(out=qa[1][64:128, tsl], in_=banks[bk][64:128, 0:TQ]), reads=[ub[bk]], writes=[u_qa[1]])
                    bk = rp.nxt()
                    proj_fm(w_fx, u_wfx, 512 + hp * 128, lambda k: uT[:, k, tsl], [u_uT], TQ, bk)
                    add(DVE, lambda e: e.tensor_copy(out=ka[0][0:64, tsl], in_=banks[bk][0:64, 0:TQ]), reads=[ub[bk]], writes=[u_ka[0]])
                    add(DVE, lambda e: e.tensor_copy(out=ka[1][64:128, tsl], in_=banks[bk][64:128, 0:TQ]), reads=[ub[bk]], writes=[u_ka[1]])
                for h2 in range(2):
                    h = 2 * hp + h2
                    base = 64 if h2 == 0 else 0
                    for r in range(3):
                        add(SP, lambda e: e.dma_start(out=ka[h2][base + r:base + r + 1, :], in_=KP[h:h + 1, r, :]),
                            reads=[u_KP], writes=[u_ka[h2]], dma=u_ka[h2])
                        add(SP, lambda e: e.dma_start(out=qa[h2][base + 3 + r:base + 4 + r, :], in_=QP[h:h + 1, r, :]),
                            reads=[u_QP], writes=[u_qa[h2]], dma=u_qa[h2])
                for tt in range(NT):
                    bk = rp.nxt()
                    for k in range(8):
                        add(PE, lambda e, k=k, bk=bk, tt=tt: e.matmul(banks[bk][:, 0:128], lhsT=uT[:, k, tt * 128:(tt + 1) * 128],
                                                                      rhs=w_fx[:, k, 1024 + hp * 128:1024 + (hp + 1) * 128],
                                                                      start=(k == 0), stop=(k == 7)),
                            reads=[u_uT, u_wfx], writes=[ub[bk]])
                    add(ACT, lambda e, bk=bk, tt=tt: e.copy(out=vaug[:, tt, :, 0:64], in_=banks[bk][:, 0:128].rearrange("p (h d) -> p h d", h=2)),
                        reads=[ub[bk]], writes=[u_v])
                npt = 0
                pend = []

                def emit_pv(G):
                    (qb, h2, grp, sl, bo, last_of_qb) = G
                    for jj, kb in enumerate(grp):
                        add(PE, lambda e: e.matmul(banks[bo][:, h2 * 65:(h2 + 1) * 65], lhsT=pT[sl][:, jj * 128:(jj + 1) * 128],
                                                   rhs=vaug[:, kb, h2, :], start=(kb == 0), stop=(kb == qb)),
                            reads=[u_pT[sl], u_v], writes=[ub[bo]])
                    if last_of_qb:
                        qs = slice(qb * 128, (qb + 1) * 128)
                        o3 = banks[bo][:, 0:130].rearrange("p (h d) -> p h d", h=2)
                        add(DVE, lambda e: e.reciprocal(out=rden[:, 0:2].unsqueeze(2), in_=o3[:, :, 64:65]), reads=[ub[bo]], writes=[u_rden])
                        add(DVE, lambda e: e.tensor_tensor(out=ftm[:, qb, :].rearrange("p (h d) -> p h d", h=2), in0=o3[:, :, 0:64],
                                                           in1=rden[:, 0:2].unsqueeze(2).broadcast_to([128, 2, 64]), op=ALU.mult),
                            reads=[ub[bo], u_rden], writes=[u_ftm])
                        bk = rp.nxt()
                        pbf = banks[bk].bitcast(BF16)
                        add(PE, lambda e: e.transpose(pbf[:, 0:128], ftm[:, qb, :], identb), reads=[u_ftm, u_cst], writes=[ub[bk]])
                        add(DVE, lambda e: e.tensor_copy(out=brT[:, BR_FOX + hp, qs], in_=pbf[:, 0:128]), reads=[ub[bk]], writes=[u_br[0]])

                for qb in range(NT):
                    bo = rsO.nxt()
                    qs = slice(qb * 128, (qb + 1) * 128)
                    for h2 in range(2):
                        h = 2 * hp + h2
                        rows = slice(h2 * 64, h2 * 64 + 64)
                        starts = list(range(0, qb + 1, 4))
                        for kb0 in starts:
                            grp = list(range(kb0, min(kb0 + 4, qb + 1)))
                            bsk = rsS.nxt()
                            sl = npt % NPT
                            npt += 1
                            for jj, kb in enumerate(grp):
                                add(PE, lambda e: e.matmul(banks[bsk][:, jj * 128:(jj + 1) * 128], lhsT=ka[h2][:, kb * 128:(kb + 1) * 128],
                                                           rhs=qa[h2][:, qs], start=True, stop=True),
                                    reads=[u_ka[h2], u_qa[h2]], writes=[ub[bsk]])
                            ng = len(grp)
                            add(ACT, lambda e: e.activation(out=pT[sl][:, 0:ng * 128], in_=banks[bsk][:, 0:ng * 128], func=AF.Exp, scale=0.125),
                                reads=[ub[bsk]], writes=[u_pT[sl]])
                            for jj, kb in enumerate(grp):
                                if kb == qb:
                                    add(DVE, lambda e: e.tensor_tensor(out=pT[sl][:, jj * 128:(jj + 1) * 128], in0=pT[sl][:, jj * 128:(jj + 1) * 128],
                                                                       in1=caus, op=ALU.mult),
                                        reads=[u_pT[sl], u_cst], writes=[u_pT[sl]])
                            pend.append((qb, h2, grp, sl, bo, (h2 == 1 and kb0 == starts[-1])))
                            if len(pend) > 3:
                                emit_pv(pend.pop(0))
                while pend:
                    emit_pv(pend.pop(0))


            P.barrier(bar_scr[:, 0:1])
            cw = Carver(OFF_W, W_B)
            w_mq = cw.take(8 * 512 * 2, BF16, "p (k n) -> p k n", k=8)
            w_kv = cw.take(8 * 1024 * 2, BF16, "p (k n) -> p k n", k=8)
            u_wmq, u_wkv = P.unit(f"wmq{b}"), P.unit(f"wkv{b}")
            sload(w_mq.rearrange("p k n -> p (k n)"), S_mq, 0, 4096, u_wmq, uS_mq)
            sload(w_kv.rearrange("p k n -> p (k n)"), S_kv, 0, 8192, u_wkv, uS_kv)
            cv = Carver(OFF_SCR, SCR_B)
            mnT = cv.take(8 * MEML * 2, BF16, "p (k t) -> p k t", k=8)
            u_mn = P.unit(f"mn{b}")
            rms_T(cv, lambda tt: mem_d[b, tt * 128:(tt + 1) * 128, :], 2, PV_MEMG, mnT, u_mn, [0, 1], f"D{b}")
            kmT = cv.take(4 * MEML * 2, BF16, "p (h t) -> p h t", h=4)
            vma = cv.take(2 * 4 * 129 * 2, BF16, "p (t h d) -> p t h d", t=2, h=4)
            qmT = cv.take(S * 2, BF16)
            mtm = cv.take(NT * 128 * 2, BF16, "p (t n) -> p t n", t=NT)
            pM = [cv.take(TQ * 2, BF16) for _ in range(4)]
            qm2 = [qmT, cv.take(S * 2, BF16)]
            rdm = cv.take(16, F32)
            u_km, u_vm, u_qm, u_mtm, u_rdm = P.unit(f"km{b}"), P.unit(f"vm{b}"), P.unit(f"qm{b}"), P.unit(f"mtm{b}"), P.unit(f"rdm{b}")
            u_pM = [P.unit(f"pM{b}{i}") for i in range(4)]
            u_qm2 = [u_qm, P.unit(f"qmb{b}")]
            rp = Rot([0, 1])
            for h in range(4):
                bk = rp.nxt()
                proj_fm(w_kv, u_wkv, h * 128, lambda k: mnT[:, k, :], [u_mn], MEML, bk)
                add(ACT, lambda e, bk=bk, h=h: e.copy(out=kmT[:, h, :], in_=banks[bk][:, 0:MEML]), reads=[ub[bk]], writes=[u_km])
            add(DVE, lambda e: e.memset(vma[:, :, :, 128:129], 1.0), writes=[u_vm])
            for kt in range(2):
                bk = rp.nxt()
                for k in range(8):
                    add(PE, lambda e, k=k, bk=bk, kt=kt: e.matmul(banks[bk][:, 0:512], lhsT=mnT[:, k, kt * 128:(kt + 1) * 128],
                                                                  rhs=w_kv[:, k, 512:1024], start=(k == 0), stop=(k == 7)),
                        reads=[u_mn, u_wkv], writes=[ub[bk]])
                add(ACT, lambda e, bk=bk, kt=kt: e.copy(out=vma[:, kt, :, 0:128], in_=banks[bk][:, 0:512].rearrange("p (h d) -> p h d", h=4)),
                    reads=[ub[bk]], writes=[u_vm])
            rsS = Rot([2, 3, 4, 5])
            rsO = Rot([6, 7])
            items = [(h, tq) for h in range(4) for tq in range(NQ)]

            def emit_qproj(h):
                for tq in range(NQ):
                    tsl = slice(tq * TQ, (tq + 1) * TQ)
                    bk = rp.nxt()
                    proj_fm(w_mq, u_wmq, h * 128, lambda k: uT[:, k, tsl], [u_uT], TQ, bk)
                    add(ACT, lambda e: e.copy(out=qm2[h % 2][:, tsl], in_=banks[bk][:, 0:TQ]), reads=[ub[bk]], writes=[u_qm2[h % 2]])

            def emit_S(i):
                h, tq = items[i]
                tsl = slice(tq * TQ, (tq + 1) * TQ)
                for kt in range(2):
                    sl = (i % 2) * 2 + kt
                    bsk = rsS.nxt()
                    add(PE, lambda e: e.matmul(banks[bsk][:, 0:TQ], lhsT=kmT[:, h, kt * 128:(kt + 1) * 128], rhs=qm2[h % 2][:, tsl],
                                               start=True, stop=True), reads=[u_km, u_qm2[h % 2]], writes=[ub[bsk]])
                    add(ACT, lambda e: e.activation(out=pM[sl], in_=banks[bsk][:, 0:TQ], func=AF.Exp, scale=128 ** -0.5),
                        reads=[ub[bsk]], writes=[u_pM[sl]])

            def emit_PV(i):
                h, tq = items[i]
                for half in range(2):
                    bo = rsO.nxt()
                    for s2 in range(2):
                        sub = half * 2 + s2
                        for kt in range(2):
                            sl = (i % 2) * 2 + kt
                            add(PE, lambda e: e.matmul(banks[bo][:, s2 * 129:(s2 + 1) * 129], lhsT=pM[sl][:, sub * 128:(sub + 1) * 128],
                                                       rhs=vma[:, kt, h, :], start=(kt == 0), stop=(kt == 1)),
                                reads=[u_pM[sl], u_vm], writes=[ub[bo]])
                    o3 = banks[bo][:, 0:258].rearrange("p (s d) -> p s d", s=2)
                    t0 = tq * 4 + half * 2
                    add(DVE, lambda e: e.reciprocal(out=rdm[:, 0:2].unsqueeze(2), in_=o3[:, :, 128:129]), reads=[ub[bo]], writes=[u_rdm])
                    add(DVE, lambda e: e.tensor_tensor(out=mtm[:, t0:t0 + 2, :], in0=o3[:, :, 0:128],
                                                       in1=rdm[:, 0:2].unsqueeze(2).broadcast_to([128, 2, 128]), op=ALU.mult),
                        reads=[ub[bo], u_rdm], writes=[u_mtm])
                    for s2 in range(2):
                        tt = t0 + s2
                        bk = rp.nxt()
                        pbf = banks[bk].bitcast(BF16)
                        add(PE, lambda e: e.transpose(pbf[:, 0:128], mtm[:, tt, :], identb), reads=[u_mtm, u_cst], writes=[ub[bk]])
                        add(ACT, lambda e: e.copy(out=brT[:, BR_MEM + h, tt * 128:(tt + 1) * 128], in_=pbf[:, 0:128]),
                            reads=[ub[bk]], writes=[u_br[2]])

            emit_qproj(0)
            emit_S(0)
            for i in range(len(items)):
                h, tq = items[i]
                if tq == 0 and h + 1 < 4:
                    emit_qproj(h + 1)
                if i + 1 < len(items):
                    emit_S(i + 1)
                emit_PV(i)

            if dbg and b == 0:
                P.barrier(bar_scr[:, 0:1])
                cvd = Carver(OFF_SCR, SCR_B)
                dt2 = cvd.take(S * 4, F32)
                u_d2 = P.unit("dbgB")
                for kk_ in range(12):
                    add(DVE, lambda e, kk_=kk_: e.tensor_copy(out=dt2, in_=brT[:, kk_, :]), reads=u_br, writes=[u_d2])
                    add(SP, lambda e, kk_=kk_: e.dma_start(out=dbg_d["d_brT"][:, kk_ * S:(kk_ + 1) * S], in_=dt2), reads=[u_d2], dma=u_d2, is_out=True)

            P.barrier(bar_scr[:, 0:1])
            cw = Carver(OFF_W, W_B)
            w_bo = cw.take(3 * 4 * 1024 * 2, BF16, "p (r c n) -> p r c n", r=3, c=4)
            u_wbo = P.unit(f"wbo{b}")
            sload(w_bo.rearrange("p r c n -> p (r c n)"), S_bo, 0, 12 * 1024, u_wbo, uS_bo)
            wg = [cw.take(8 * 3 * 128 * 2, BF16, "p (k r n) -> p k r n", k=8, r=3) for _ in range(2)]
            u_wg = [P.unit(f"wg{b}{i}", nobar=False) for i in range(2)]
            cv = Carver(OFF_SCR, SCR_B)
            mgT = cv.take(8 * SMAX * 2, BF16, "p (k t) -> p k t", k=8)[:, :, 0:S]
            u_mg = P.unit(f"mg{b}")
            sgt = [cv.take(TQ * 4, F32) for _ in range(3)]
            u_sg = [P.unit(f"sg{b}{i}") for i in range(3)]
            ra = Rot(range(8))
            for oc in range(8):
                s = oc % 2
                sload(wg[s].rearrange("p k r n -> p (k r n)"), S_gt, oc * 3072, 3072, u_wg[s], uS_gt)
                for tq in range(NQ):
                    tsl = slice(tq * TQ, (tq + 1) * TQ)
                    bg_, bb_ = [], []
                    for r in range(3):
                        bk = ra.nxt()
                        bg_.append(bk)
                        for k in range(8):
                            add(PE, lambda e, k=k, bk=bk, r=r: e.matmul(banks[bk][:, 0:TQ], lhsT=wg[s][:, k, r, :], rhs=uT[:, k, tsl],
                                                                        start=(k == 0), stop=(k == 7)), reads=[u_wg[s], u_uT], writes=[ub[bk]])
                        add(ACT, lambda e, bk=bk, r=r: e.activation(out=sgt[r], in_=banks[bk][:, 0:TQ], func=AF.Sigmoid), reads=[ub[bk]], writes=[u_sg[r]])
                    for r in range(3):
                        bk = ra.nxt()
                        bb_.append(bk)
                        for c in range(4):
                            add(PE, lambda e, c=c, bk=bk, r=r: e.matmul(banks[bk][:, 0:TQ], lhsT=w_bo[:, r, c, oc * 128:(oc + 1) * 128],
                                                                        rhs=brT[:, (BR_FOX, BR_RWK, BR_MEM)[r] + c, tsl], start=(c == 0), stop=(c == 3)),
                                reads=[u_wbo, u_br[r]], writes=[ub[bk]])
                        add(DVE, lambda e, bk=bk, r=r: e.tensor_tensor(out=sgt[r], in0=sgt[r], in1=banks[bk][:, 0:TQ], op=ALU.mult),
                            reads=[u_sg[r], ub[bk]], writes=[u_sg[r]])
                    add(DVE, lambda e: e.tensor_tensor(out=sgt[0], in0=sgt[0], in1=sgt[1], op=ALU.add), reads=[u_sg[0], u_sg[1]], writes=[u_sg[0]])
                    add(DVE, lambda e, tsl=tsl: e.tensor_tensor(out=mgT[:, oc, tsl], in0=sgt[0], in1=sgt[2], op=ALU.add),
                        reads=[u_sg[0], u_sg[2]], writes=[u_mg])

            P.barrier(bar_scr[:, 0:1])
            cw = Carver(OFF_W, W_B)
            w_dn = cw.take(NFC * 1024 * 2, BF16, "p (f n) -> p f n", f=NFC)
            u_wdn = P.unit(f"wdn{b}")
            cb = Carver(OFF_BR, BR_B)
            w_o = cb.take(8 * 1024 * 2, BF16, "p (k n) -> p k n", k=8)
            u_wo = P.unit(f"wo{b}")
            sload(w_o.rearrange("p k n -> p (k n)"), S_o, 0, 8192, u_wo, uS_o)
            sload(w_dn.rearrange("p f n -> p (f n)"), S_dn, 0, NFC * 1024, u_wdn, uS_dn)
            actT = cb.take(NFC * TQ * 2, BF16, "p (f t) -> p f t", f=NFC)
            u_act = P.unit(f"act{b}")
            cu = Carver(OFF_UT, UT_B)
            hres = cu.take(4 * 1024 * 4, F32, "p (s n) -> p s n", s=4)
            u2T = cu.take(8 * TQ * 2, BF16, "p (k t) -> p k t", k=8)
            xt = [cu.take(4096, F32) for _ in range(2)]
            u_hres = [P.unit(f"hres{b}{i}") for i in range(4)]
            u_u2 = P.unit(f"u2{b}")
            u_xt = [P.unit(f"xt{b}{i}") for i in range(2)]
            cv = Carver(OFF_SCR + 8 * SMAX * 2, SCR_B - 8 * SMAX * 2)
            wgu = [cv.take(8 * 2 * 128 * 2, BF16, "p (k r n) -> p k r n", k=8, r=2) for _ in range(3)]
            wgu.append(cw.take(8 * 2 * 128 * 2, BF16, "p (k r n) -> p k r n", k=8, r=2))
            NWG = len(wgu)
            u_wgu = [P.unit(f"wgu{b}{i}") for i in range(NWG)]
            pg = cv.take(2 * 1024 * 4, F32, "p (r n) -> p r n", r=2)
            u_pg = P.unit(f"pg{b}")
            add(SP, lambda e: e.dma_start(out=pg.rearrange("p r n -> p (r n)"), in_=rowp_d.partition_broadcast(128)), writes=[u_pg], dma=u_pg)
            yt = cb.take(4096, F32)
            u_yt = P.unit(f"yt{b}")
            hnb = [cb.take(2048, BF16) for _ in range(2)]
            u_hnb = [P.unit(f"hnb{b}{i}") for i in range(2)]
            u_ss4 = [P.unit(f"ss4{b}{i}") for i in range(4)]
            sst = cb.take(64 * 4, F32)
            u_ss = P.unit(f"ss{b}")
            slt = [cv.take(TQ * 4, F32) for _ in range(2)]
            u_sl = [P.unit(f"slt{b}{i}") for i in range(2)]
            ra = Rot(range(8))
            nxl = [0]
            nwl = 0
            for tq in range(NQ):
                bys, toks = [], []
                for sub in range(4):
                    tok0 = tq * TQ + sub * 128
                    by = [ra.nxt(), ra.nxt()]
                    bys.append(by)
                    toks.append(tok0)
                    for nh in range(2):
                        for k in range(8):
                            add(PE, lambda e: e.matmul(banks[by[nh]][:, 0:512], lhsT=mgT[:, k, tok0:tok0 + 128],
                                                       rhs=w_o[:, k, nh * 512:(nh + 1) * 512], start=(k == 0), stop=(k == 7)),
                                reads=[u_mg, u_wo], writes=[ub[by[nh]]])
                        add(ACT, lambda e: e.activation(out=hnb[sub % 2][:, 0:512], in_=banks[by[nh]][:, 0:512], func=AF.Square,
                                                        accum_out=sst[:, sub * 16 + nh:sub * 16 + nh + 1]),
                            reads=[ub[by[nh]]], writes=[u_hnb[sub % 2], u_ss4[sub]])
                for sub in range(4):
                    o_ = sub * 16
                    add(DVE, lambda e: e.tensor_tensor(out=sst[:, o_ + 2:o_ + 3], in0=sst[:, o_:o_ + 1], in1=sst[:, o_ + 1:o_ + 2], op=ALU.add),
                        reads=[u_ss4[sub]], writes=[u_ss4[sub]])
                for sub in range(4):
                    o_ = sub * 16
                    add(ACT, lambda e: e.activation(out=sst[:, o_ + 3:o_ + 4], in_=sst[:, o_ + 2:o_ + 3], func=AF.Sqrt, scale=1.0 / D, bias=NORM_EPS),
                        reads=[u_ss4[sub]], writes=[u_ss4[sub]])
                for sub in range(4):
                    o_ = sub * 16
                    add(DVE, lambda e: e.reciprocal(out=sst[:, o_ + 4:o_ + 5], in_=sst[:, o_ + 3:o_ + 4]), reads=[u_ss4[sub]], writes=[u_ss4[sub]])
                for sub in range(4):
                    o_ = sub * 16
                    by, tok0 = bys[sub], toks[sub]
                    xs = nxl[0] % 2
                    nxl[0] += 1
                    add(SP, lambda e: e.dma_start(out=xt[xs], in_=x_d[b, tok0:tok0 + 128, :]), writes=[u_xt[xs]], dma=u_xt[xs])
                    for nh in range(2):
                        add(DVE, lambda e: e.scalar_tensor_tensor(out=hres[:, sub, nh * 512:(nh + 1) * 512], in0=banks[by[nh]][:, 0:512],
                                                                  scalar=sst[:, o_ + 4:o_ + 5], in1=pg[:, 0, nh * 512:(nh + 1) * 512],
                                                                  op0=ALU.mult, op1=ALU.mult),
                            reads=[ub[by[nh]], u_ss4[sub], u_pg], writes=[u_hres[sub]])
                    add(DVE, lambda e: e.tensor_tensor(out=hres[:, sub, :], in0=hres[:, sub, :], in1=xt[xs], op=ALU.add),
                        reads=[u_hres[sub], u_xt[xs]], writes=[u_hres[sub]])
                    add(ACT, lambda e: e.activation(out=hnb[sub % 2], in_=hres[:, sub, :], func=AF.Square, accum_out=sst[:, o_ + 5:o_ + 6]),
                        reads=[u_hres[sub]], writes=[u_hnb[sub % 2], u_ss4[sub]])
                    if dbg and b == 0:
                        add(SP, lambda e: e.dma_start(out=dbg_d["d_h"][tok0:tok0 + 128, :], in_=hres[:, sub, :]),
                            reads=[u_hres[sub]], dma=u_hres[sub], is_out=True)
                for half in range(2):
                    subs = (2 * half, 2 * half + 1)
                    for sub in subs:
                        o_ = sub * 16
                        add(ACT, lambda e: e.activation(out=sst[:, o_ + 6:o_ + 7], in_=sst[:, o_ + 5:o_ + 6], func=AF.Sqrt, scale=1.0 / D, bias=NORM_EPS),
                            reads=[u_ss4[sub]], writes=[u_ss4[sub]])
                    for sub in subs:
                        o_ = sub * 16
                        add(DVE, lambda e: e.reciprocal(out=sst[:, o_ + 7:o_ + 8], in_=sst[:, o_ + 6:o_ + 7]), reads=[u_ss4[sub]], writes=[u_ss4[sub]])
                    for sub in subs:
                        o_ = sub * 16
                        add(DVE, lambda e: e.tensor_scalar(out=hnb[sub % 2], in0=hres[:, sub, :], scalar1=sst[:, o_ + 7:o_ + 8], scalar2=None, op0=ALU.mult),
                            reads=[u_hres[sub], u_ss4[sub]], writes=[u_hnb[sub % 2]])
                    for sub in subs:
                        bk = ra.nxt()
                        pbf = banks[bk].bitcast(BF16)
                        for c in range(8):
                            add(PE, lambda e: e.transpose(pbf[:, c * 128:(c + 1) * 128], hnb[sub % 2][:, c * 128:(c + 1) * 128], identb),
                                reads=[u_hnb[sub % 2], u_cst], writes=[ub[bk]])
                        add(DVE, lambda e: e.tensor_tensor(out=u2T[:, :, sub * 128:(sub + 1) * 128],
                                                           in0=pbf[:, 0:1024].rearrange("p (k t) -> p k t", k=8),
                                                           in1=pv[:, PV_PRE2:PV_PRE2 + 8].unsqueeze(2).broadcast_to([128, 8, 128]), op=ALU.mult),
                            reads=[ub[bk], u_pv], writes=[u_u2])
                for fc in range(NFC):
                    s = nwl % NWG
                    nwl += 1
                    sload(wgu[s].rearrange("p k r n -> p (k r n)"), S_ff, fc * 2048, 2048, u_wgu[s], uS_ff)
                    bgk, buk = ra.nxt(), ra.nxt()
                    for r, bk in ((0, bgk), (1, buk)):
                        for k in range(8):
                            add(PE, lambda e, k=k, r=r, bk=bk, s=s: e.matmul(banks[bk][:, 0:TQ], lhsT=wgu[s][:, k, r, :], rhs=u2T[:, k, :],
                                                                            start=(k == 0), stop=(k == 7)), reads=[u_wgu[s], u_u2], writes=[ub[bk]])
                    add(ACT, lambda e, bgk=bgk, s=s: e.activation(out=slt[s % 2], in_=banks[bgk][:, 0:TQ], func=AF.Silu), reads=[ub[bgk]], writes=[u_sl[s % 2]])
                    add(DVE, lambda e, buk=buk, s=s, fc=fc: e.tensor_tensor(out=actT[:, fc, :], in0=slt[s % 2], in1=banks[buk][:, 0:TQ], op=ALU.mult),
                        reads=[u_sl[s % 2], ub[buk]], writes=[u_act])
                for sub in range(4):
                    tok0 = tq * TQ + sub * 128
                    by = [ra.nxt(), ra.nxt()]
                    for nh in range(2):
                        for f in range(NFC):
                            add(PE, lambda e, f=f, nh=nh, by=by, sub=sub: e.matmul(banks[by[nh]][:, 0:512], lhsT=actT[:, f, sub * 128:(sub + 1) * 128],
                                                                                 rhs=w_dn[:, f, nh * 512:(nh + 1) * 512], start=(f == 0), stop=(f == NFC - 1)),
                                reads=[u_act, u_wdn], writes=[ub[by[nh]]])
                        add(ACT, lambda e, nh=nh, by=by: e.activation(out=hnb[0][:, 0:512], in_=banks[by[nh]][:, 0:512], func=AF.Square,
                                                                      accum_out=sst[:, 8 + nh:9 + nh]), reads=[ub[by[nh]]], writes=[u_hnb[0], u_ss])
                    add(DVE, lambda e: e.tensor_tensor(out=sst[:, 10:11], in0=sst[:, 8:9], in1=sst[:, 9:10], op=ALU.add), reads=[u_ss], writes=[u_ss])
                    add(ACT, lambda e: e.activation(out=sst[:, 11:12], in_=sst[:, 10:11], func=AF.Sqrt, scale=1.0 / D, bias=NORM_EPS), reads=[u_ss], writes=[u_ss])
                    add(DVE, lambda e: e.reciprocal(out=sst[:, 12:13], in_=sst[:, 11:12]), reads=[u_ss], writes=[u_ss])
                    for nh in range(2):
                        add(DVE, lambda e, nh=nh, by=by: e.scalar_tensor_tensor(out=yt[:, nh * 512:(nh + 1) * 512], in0=banks[by[nh]][:, 0:512],
                                                                               scalar=sst[:, 12:13], in1=pg[:, 1, nh * 512:(nh + 1) * 512],
                                                                               op0=ALU.mult, op1=ALU.mult),
                            reads=[ub[by[nh]], u_ss, u_pg], writes=[u_yt])
                    add(DVE, lambda e, sub=sub: e.tensor_tensor(out=hres[:, sub, :], in0=yt, in1=hres[:, sub, :], op=ALU.add),
                        reads=[u_yt, u_hres[sub]], writes=[u_hres[sub]])
                    add(SP, lambda e, sub=sub, tok0=tok0: e.dma_start(out=out_d[b, tok0:tok0 + 128, :], in_=hres[:, sub, :]),
                        reads=[u_hres[sub]], dma=u_hres[sub], is_out=True)
        P.emit()
        nops = P.nops
    return nc, nops


def _cols(v, n):
    return np.ascontiguousarray(np.asarray(v, np.float32).reshape(n, 128).T)


def host_params(inp):
    pvv = np.zeros((128, NPV), np.float32)
    pvv[:, PV_PRE1:PV_PRE1 + 8] = _cols(inp["pre1_g"][0], 8)
    pvv[:, PV_PRE2:PV_PRE2 + 8] = _cols(inp["pre2_g"][0], 8)
    pvv[:, PV_MEMG:PV_MEMG + 8] = _cols(inp["mem_norm_g"][0], 8)
    pvv[:, PV_MU:PV_MU + 14] = _cols(inp["rwkv_mu"][0], 14)
    pvv[:, PV_W0:PV_W0 + 4] = _cols(inp["rwkv_w0"][0], 4)
    pvv[:, PV_A0:PV_A0 + 4] = _cols(inp["rwkv_a0"][0], 4)
    pvv[:, PV_KK:PV_KK + 4] = _cols(inp["rwkv_k_k"][0], 4)
    pvv[:, PV_KA:PV_KA + 4] = _cols(inp["rwkv_k_a"][0], 4)
    pvv[:, PV_RK:PV_RK + 4] = _cols(np.asarray(inp["rwkv_r_k"][0]).reshape(-1), 4)
    pvv[0:8, PV_FB] = np.asarray(inp["fox_f_bias"][0], np.float32)
    gg = np.asarray(inp["rwkv_gn_g"][0], np.float32).reshape(4, 2, 64)
    gb = np.asarray(inp["rwkv_gn_b"][0], np.float32).reshape(4, 2, 64)
    gnp = np.zeros((128, 4, 2, 64), np.float32)
    for hp in range(4):
        for hh in range(2):
            gnp[hh * 64:(hh + 1) * 64, hp, 0, :] = gg[hp, hh][None, :]
            gnp[hh * 64:(hh + 1) * 64, hp, 1, :] = gb[hp, hh][None, :]
    rowp = np.concatenate([np.asarray(inp["post1_g"][0], np.float32), np.asarray(inp["post2_g"][0], np.float32)])[None, :]
    f = lambda k: np.ascontiguousarray(np.asarray(inp[k][0], np.float32))
    shared = {
        "w_in": f("w_in"), "w_mem_kv": f("w_mem_kv"), "w_fox_out": f("w_fox_out"), "w_rwkv_out": f("w_rwkv_out"),
        "w_mem_out": f("w_mem_out"), "w_o": f("w_o"), "w_ffn_gate": f("w_ffn_gate"), "w_ffn_up": f("w_ffn_up"),
        "w_ffn_down": f("w_ffn_down"), "rwkv_w_up": f("rwkv_w_up"), "rwkv_a_up": f("rwkv_a_up"), "rwkv_g_up": f("rwkv_g_up"),
        "pv": pvv, "cst": make_consts(), "gnp": gnp.reshape(128, -1), "rowp": np.ascontiguousarray(rowp),
    }
    return shared


_CACHE = {}


def kernel(**inputs):
    x = np.asarray(inputs["x"], np.float32)
    mem = np.asarray(inputs["mem"], np.float32)
    B, S, _ = x.shape
    n = 8
    NB = B // n
    shared = host_params(inputs)
    key = (NB, S)
    if key not in _CACHE:
        _CACHE[key] = build(NB=NB, S=S)[0]
    nc = _CACHE[key]
    in_maps = []
    for c in range(n):
        m = dict(shared)
        m["x"] = np.ascontiguousarray(x[c * NB:(c + 1) * NB])
        m["mem"] = np.ascontiguousarray(mem[c * NB:(c + 1) * NB])
        in_maps.append(m)
    res = run_bass_kernel_spmd(nc, in_maps, core_ids=list(range(n)))
    out = np.concatenate([np.asarray(r["out"], np.float32) for r in res.results], axis=0)
    return out
```
